# Optimizing a Trainium2 kernel written in Bass

```python
import jax, jax.numpy as jnp
from jax import lax
import numpy as np

D_MODEL = 1024
BATCH = 8
SEQ = 4096
DEPTH = 2

PLE_DIM = 256
HEAD_DIM = 64
RWKV_WIDTH = D_MODEL // 4
RWKV_HEADS = RWKV_WIDTH // HEAD_DIM
RWKV_W_RANK = 64
RWKV_A_RANK = 64
RWKV_V_RANK = 32
RWKV_G_RANK = 128
RWKV_GN_EPS = 64e-5
MOBA_WIDTH = D_MODEL // 2
MOBA_HEADS = MOBA_WIDTH // HEAD_DIM
MOBA_BLOCK = 256
MOBA_TOPK = 3
MOBA_QBLOCK = 64
MLSTM_WIDTH = D_MODEL // 4
MLSTM_HEADS = MLSTM_WIDTH // HEAD_DIM
MLSTM_CHUNK = 64
MLSTM_CONV = 4
D_FF = ((8 * D_MODEL // 3 + 127) // 128) * 128
FFN_CONV = 3
NORM_EPS = 1e-6
MASK_VALUE = -1e30
RWKV_SLAB = 3 * RWKV_WIDTH + RWKV_W_RANK + RWKV_A_RANK + RWKV_G_RANK
MOBA_SLAB = 3 * MOBA_WIDTH
MLSTM_SLAB = 4 * MLSTM_WIDTH + 2 * MLSTM_HEADS
GATE_SLAB = 3 * D_MODEL
IN_WIDTH = RWKV_SLAB + MOBA_SLAB + MLSTM_SLAB + GATE_SLAB

kernel_name = 'hybrid_rwkv7_moba_mlstm_convffn_block'

F32 = jnp.float32


def rms_norm(x, g):
    xf = x.astype(F32)
    y = xf * lax.rsqrt(jnp.mean(xf * xf, axis=-1, keepdims=True) + NORM_EPS)
    return (y * g).astype(x.dtype)


def token_shift(x):
    return jnp.pad(x, ((0, 0), (1, 0), (0, 0)))[:, :-1]


def causal_dwconv(x, w, b):
    k_w, s = w.shape[0], x.shape[1]
    xp = jnp.pad(x, ((0, 0), (k_w - 1, 0), (0, 0)))
    out = b + w[k_w - 1] * x
    for j in range(k_w - 1):
        out = out + w[j] * xp[:, j:j + s]
    return out


def split_cols(t, sizes):
    return jnp.split(t, np.cumsum(sizes)[:-1].tolist(), axis=-1)


def rwkv7_time_mix(slab, mu, w0, w2, a0, a2, g2, k_k, k_a, r_k, gn_g, gn_b,
                   v_first=None, v0=None, v1=None, v2=None):
    B, S, _ = slab.shape
    H, N = RWKV_HEADS, HEAD_DIM
    dt = slab.dtype
    slab = slab + mu * (token_shift(slab) - slab)
    r, k, v, xw, xa, xg = split_cols(
        slab, [RWKV_WIDTH] * 3 + [RWKV_W_RANK, RWKV_A_RANK, RWKV_G_RANK])
    w = -jax.nn.softplus(-(w0 + jnp.tanh(xw) @ w2)) - 0.5
    a = jax.nn.sigmoid(a0 + xa @ a2)
    g = jax.nn.sigmoid(xg) @ g2
    if v_first is not None:
        v = v + (v_first - v) * jax.nn.sigmoid(v0 + (v @ v1) @ v2)

    def heads(t):
        return t.astype(F32).reshape(B, S, H, N)

    kk = heads(k * k_k)
    kk = kk / jnp.maximum(jnp.sqrt(jnp.sum(kk * kk, axis=-1, keepdims=True)), 1e-12)
    k = k * (1.0 + (a - 1.0) * k_a)
    rh, kh, vh, ah = heads(r), heads(k), heads(v), heads(a)
    decay = jnp.exp(-jnp.exp(heads(w)))

    def step(state, inp):
        r_t, w_t, k_t, v_t, kk_t, a_t = inp
        sa = jnp.einsum('bhvk,bhk->bhv', state, -kk_t)
        state = (state * w_t[:, :, None, :] + sa[..., None] * (kk_t * a_t)[:, :, None, :]
                 + v_t[..., None] * k_t[:, :, None, :])
        return state, jnp.einsum('bhvk,bhk->bhv', state, r_t)

    def tm(t):
        return jnp.moveaxis(t, 1, 0)

    _, y = lax.scan(step, jnp.zeros((B, H, N, N), F32),
                    (tm(rh), tm(decay), tm(kh), tm(vh), tm(kk), tm(ah)))
    y = jnp.moveaxis(y, 0, 1)
    m = jnp.mean(y, axis=-1, keepdims=True)
    var = jnp.mean(jnp.square(y - m), axis=-1, keepdims=True)
    y = ((y - m) * lax.rsqrt(var + RWKV_GN_EPS)).reshape(B, S, RWKV_WIDTH) * gn_g + gn_b
    bonus = jnp.sum(rh * kh * r_k.astype(F32).reshape(H, N), axis=-1, keepdims=True) * vh
    y = (y + bonus.reshape(B, S, RWKV_WIDTH)) * g
    return y.astype(dt), v


def moba_attention(mq, mk, mv):
    B, S, _ = mq.shape
    H, dh, BL, QB = MOBA_HEADS, HEAD_DIM, MOBA_BLOCK, MOBA_QBLOCK
    dt = mq.dtype
    scale = HEAD_DIM ** -0.5

    def heads(t):
        return t.reshape(B, S, H, dh).transpose(0, 2, 1, 3)

    q, k, v = heads(mq), heads(mk), heads(mv)
    nb = -(-S // BL)
    pad = ((0, 0), (0, 0), (0, nb * BL - S), (0, 0))
    kb = jnp.pad(k, pad).reshape(B, H, nb, BL, dh)
    vb = jnp.pad(v, pad).reshape(B, H, nb, BL, dh)
    n_sel = min(MOBA_TOPK, nb - 1)
    nq = S // QB

    def to_qb(t):
        return jnp.moveaxis(t.reshape(B, H, nq, QB, *t.shape[3:]), 2, 0)

    xs = (to_qb(q), jnp.arange(nq, dtype=jnp.int32) * QB)
    if n_sel > 0:
        kmean = jnp.mean(kb.astype(F32), axis=3)
        bscore = jnp.einsum('bhsd,bhnd->bhsn', q.astype(F32), kmean)
        past = jnp.arange(nb)[None, :] < (jnp.arange(S) // BL)[:, None]
        bscore = jnp.where(past, bscore, MASK_VALUE)
        _, sel = lax.top_k(bscore, n_sel)
        xs = xs + (to_qb(sel),)

    gather = jax.vmap(jax.vmap(lambda blocks, idx: blocks[idx]))

    def attend(args):
        qc, start = args[0], args[1]
        ob = start // BL
        k_own = lax.dynamic_index_in_dim(kb, ob, axis=2, keepdims=False)
        v_own = lax.dynamic_index_in_dim(vb, ob, axis=2, keepdims=False)
        qpos = start + jnp.arange(QB)
        kpos = ob * BL + jnp.arange(BL)
        s_own = jnp.einsum('bhqd,bhkd->bhqk', qc, k_own).astype(F32) * scale
        s_own = jnp.where(kpos[None, :] <= qpos[:, None], s_own, MASK_VALUE)
        if n_sel > 0:
            ic = args[2]
            k_sel = gather(kb, ic)
            v_sel = gather(vb, ic)
            s_sel = jnp.einsum('bhqd,bhqjkd->bhqjk', qc, k_sel).astype(F32) * scale
            s_sel = jnp.where((ic < ob)[..., None], s_sel, MASK_VALUE)
            probs = jax.nn.softmax(
                jnp.concatenate([s_sel.reshape(B, H, QB, n_sel * BL), s_own], axis=-1), axis=-1)
            p_sel, p_own = jnp.split(probs, [n_sel * BL], axis=-1)
            out = (jnp.einsum('bhqjk,bhqjkd->bhqd',
                              p_sel.reshape(B, H, QB, n_sel, BL).astype(dt), v_sel)
                   + jnp.einsum('bhqk,bhkd->bhqd', p_own.astype(dt), v_own))
        else:
            probs = jax.nn.softmax(s_own, axis=-1)
            out = jnp.einsum('bhqk,bhkd->bhqd', probs.astype(dt), v_own)
        return out

    o = lax.map(attend, xs)
    o = jnp.moveaxis(o, 0, 2).reshape(B, H, S, dh)
    return o.transpose(0, 2, 1, 3).reshape(B, S, MOBA_WIDTH)


def mlstm_chunkwise(lq, lk, lv, lo, li, lf, conv_w, conv_b, i_b, f_b, hn_g):
    B, S, _ = lq.shape
    H, dh, L = MLSTM_HEADS, HEAD_DIM, MLSTM_CHUNK
    nc = S // L
    dt = lq.dtype
    qk = jax.nn.silu(causal_dwconv(jnp.concatenate([lq, lk], axis=-1), conv_w, conv_b))
    q, k = jnp.split(qk, 2, axis=-1)

    def heads(t):
        return t.astype(F32).reshape(B, S, H, dh).transpose(0, 2, 1, 3)

    q, k, v = heads(q), heads(k) * (dh ** -0.5), heads(lv)
    log_i = (li + i_b).astype(F32).transpose(0, 2, 1)
    log_f = jax.nn.log_sigmoid((lf + f_b).astype(F32)).transpose(0, 2, 1)

    def chunks(t):
        return jnp.moveaxis(t.reshape(B, H, nc, L, *t.shape[3:]), 2, 0)

    causal = jnp.tril(jnp.ones((L, L), dtype=bool))

    def step(carry, inp):
        C, n, m = carry
        qc, kc, vc, ic, fc = inp
        b = jnp.cumsum(fc, axis=-1)
        dmat = jnp.where(causal, b[..., :, None] - b[..., None, :] + ic[..., None, :], -jnp.inf)
        inter = b + m[..., None]
        m_t = jnp.maximum(inter, jnp.max(dmat, axis=-1))
        wts = jnp.exp(dmat - m_t[..., None])
        s_inter = jnp.exp(inter - m_t)
        qk_w = jnp.einsum('bhld,bhsd->bhls', qc, kc) * wts
        num = (s_inter[..., None] * jnp.einsum('bhvk,bhlk->bhlv', C, qc)
               + jnp.einsum('bhls,bhsv->bhlv', qk_w, vc))
        den = s_inter * jnp.einsum('bhk,bhlk->bhl', n, qc) + jnp.sum(qk_w, axis=-1)
        h = num / jnp.maximum(jnp.abs(den), jnp.exp(-m_t))[..., None]
        b_end = b[..., -1]
        g_s = b_end[..., None] - b + ic
        m_new = jnp.maximum(b_end + m, jnp.max(g_s, axis=-1))
        w_s = jnp.exp(g_s - m_new[..., None])
        carry_scale = jnp.exp(b_end + m - m_new)
        C = carry_scale[..., None, None] * C + jnp.einsum('bhs,bhsv,bhsk->bhvk', w_s, vc, kc)
        n = carry_scale[..., None] * n + jnp.einsum('bhs,bhsk->bhk', w_s, kc)
        return (C, n, m_new), h

    init = (jnp.zeros((B, H, dh, dh), F32), jnp.zeros((B, H, dh), F32), jnp.zeros((B, H), F32))
    _, h = lax.scan(step, init, (chunks(q), chunks(k), chunks(v), chunks(log_i), chunks(log_f)))
    h = jnp.moveaxis(h, 0, 2).reshape(B, H, S, dh)
    mu = jnp.mean(h, axis=-1, keepdims=True)
    var = jnp.mean(jnp.square(h - mu), axis=-1, keepdims=True)
    h = ((h - mu) * lax.rsqrt(var + NORM_EPS)).transpose(0, 2, 1, 3).reshape(B, S, MLSTM_WIDTH)
    return (h * hn_g * jax.nn.sigmoid(lo.astype(F32))).astype(dt)


def setup_inputs(seed: int = 0) -> dict:
    key = jax.random.key(seed)
    ks = iter(jax.random.split(key, 48))
    L, D = DEPTH, D_MODEL
    RW, MW, LW, LH = RWKV_WIDTH, MOBA_WIDTH, MLSTM_WIDTH, MLSTM_HEADS

    def nrm(shape, scale):
        return scale * jax.random.normal(next(ks), shape, F32)

    def gain(shape):
        return 1.0 + nrm(shape, 0.05)

    def unif(shape, lo, hi):
        return jax.random.uniform(next(ks), shape, F32, lo, hi)

    return {
        'x': nrm((BATCH, SEQ, D), 1.0),
        'p': nrm((L, BATCH, SEQ, PLE_DIM), 1.0),
        'ln_mix_pre': gain((L, D)),
        'ln_mix_post': gain((L, D)),
        'ln_ffn_pre': gain((L, D)),
        'ln_ffn_post': gain((L, D)),
        'ln_ple': gain((L, D)),
        'w_in': nrm((L, D, IN_WIDTH), D ** -0.5),
        'rwkv_mu': unif((L, RWKV_SLAB), 0.0, 1.0),
        'rwkv_w0': unif((L, RW), -6.0, -1.0),
        'rwkv_w2': nrm((L, RWKV_W_RANK, RW), 0.1 * RWKV_W_RANK ** -0.5),
        'rwkv_a0': nrm((L, RW), 0.1),
        'rwkv_a2': nrm((L, RWKV_A_RANK, RW), 0.1 * RWKV_A_RANK ** -0.5),
        'rwkv_g2': nrm((L, RWKV_G_RANK, RW), RWKV_G_RANK ** -0.5),
        'rwkv_k_k': 0.85 + nrm((L, RW), 0.05),
        'rwkv_k_a': gain((L, RW)),
        'rwkv_r_k': nrm((L, RW), 0.1),
        'rwkv_gn_g': gain((L, RW)),
        'rwkv_gn_b': nrm((L, RW), 0.01),
        'rwkv_v0': nrm((L - 1, RW), 0.1),
        'rwkv_v1': nrm((L - 1, RW, RWKV_V_RANK), RW ** -0.5),
        'rwkv_v2': nrm((L - 1, RWKV_V_RANK, RW), 0.1 * RWKV_V_RANK ** -0.5),
        'mlstm_conv_w': nrm((L, MLSTM_CONV, 2 * LW), 0.5),
        'mlstm_conv_b': nrm((L, 2 * LW), 0.02),
        'mlstm_i_b': nrm((L, LH), 0.1),
        'mlstm_f_b': jnp.linspace(3.0, 6.0, LH, dtype=F32)[None, :] + nrm((L, LH), 0.1),
        'mlstm_hn_g': gain((L, LW)),
        'w_br_rwkv': nrm((L, RW, D), RW ** -0.5),
        'w_br_moba': nrm((L, MW, D), MW ** -0.5),
        'w_br_mlstm': nrm((L, LW, D), LW ** -0.5),
        'w_out': nrm((L, D, D), D ** -0.5),
        'ffn_up': nrm((L, D, 2 * D_FF), D ** -0.5),
        'ffn_conv_w': nrm((L, FFN_CONV, 2 * D_FF), FFN_CONV ** -0.5),
        'ffn_conv_b': nrm((L, 2 * D_FF), 0.02),
        'ffn_down': nrm((L, D_FF, D), D_FF ** -0.5),
        'ple_proj': nrm((L, PLE_DIM, D), PLE_DIM ** -0.5),
        'ple_gate': nrm((L, D, D), D ** -0.5),
    }


def reference(x, p, ln_mix_pre, ln_mix_post, ln_ffn_pre, ln_ffn_post, ln_ple, w_in,
              rwkv_mu, rwkv_w0, rwkv_w2, rwkv_a0, rwkv_a2, rwkv_g2, rwkv_k_k, rwkv_k_a,
              rwkv_r_k, rwkv_gn_g, rwkv_gn_b, rwkv_v0, rwkv_v1, rwkv_v2,
              mlstm_conv_w, mlstm_conv_b, mlstm_i_b, mlstm_f_b, mlstm_hn_g,
              w_br_rwkv, w_br_moba, w_br_mlstm, w_out,
              ffn_up, ffn_conv_w, ffn_conv_b, ffn_down, ple_proj, ple_gate):
    v_first = None
    for i in range(DEPTH):
        h = rms_norm(x, ln_mix_pre[i])
        proj = h @ w_in[i]
        s_rwkv, s_moba, s_mlstm, s_gate = split_cols(
            proj, [RWKV_SLAB, MOBA_SLAB, MLSTM_SLAB, GATE_SLAB])
        if i == 0:
            y_a, v_first = rwkv7_time_mix(
                s_rwkv, rwkv_mu[i], rwkv_w0[i], rwkv_w2[i], rwkv_a0[i], rwkv_a2[i], rwkv_g2[i],
                rwkv_k_k[i], rwkv_k_a[i], rwkv_r_k[i], rwkv_gn_g[i], rwkv_gn_b[i])
        else:
            y_a, _ = rwkv7_time_mix(
                s_rwkv, rwkv_mu[i], rwkv_w0[i], rwkv_w2[i], rwkv_a0[i], rwkv_a2[i], rwkv_g2[i],
                rwkv_k_k[i], rwkv_k_a[i], rwkv_r_k[i], rwkv_gn_g[i], rwkv_gn_b[i],
                v_first, rwkv_v0[i - 1], rwkv_v1[i - 1], rwkv_v2[i - 1])
        mq, mk, mv = split_cols(s_moba, [MOBA_WIDTH] * 3)
        y_b = moba_attention(mq, mk, mv)
        lq, lk, lv, lo, li, lf = split_cols(
            s_mlstm, [MLSTM_WIDTH] * 4 + [MLSTM_HEADS] * 2)
        y_c = mlstm_chunkwise(lq, lk, lv, lo, li, lf, mlstm_conv_w[i], mlstm_conv_b[i],
                              mlstm_i_b[i], mlstm_f_b[i], mlstm_hn_g[i])
        g_a, g_b, g_c = jnp.split(jax.nn.sigmoid(s_gate), 3, axis=-1)
        merged = (g_a * (y_a @ w_br_rwkv[i]) + g_b * (y_b @ w_br_moba[i])
                  + g_c * (y_c @ w_br_mlstm[i]))
        x = x + rms_norm(merged @ w_out[i], ln_mix_post[i])
        h = rms_norm(x, ln_ffn_pre[i])
        u = causal_dwconv(h @ ffn_up[i], ffn_conv_w[i], ffn_conv_b[i])
        u_gate, u_val = jnp.split(u, 2, axis=-1)
        f = (jax.nn.gelu(u_gate, approximate=True) * u_val) @ ffn_down[i]
        x = x + rms_norm(f, ln_ffn_post[i])
        gate = jax.nn.sigmoid(rms_norm(x, ln_ple[i]) @ ple_gate[i])
        x = x + gate * (p[i] @ ple_proj[i])
    return x
```

```python
import contextlib
import numpy as np
import concourse.bass as bass
import concourse.mybir as mybir

F32 = mybir.dt.float32
BF16 = mybir.dt.bfloat16
AF = mybir.ActivationFunctionType
ALU = mybir.AluOpType
AX = mybir.AxisListType

SAME_ENGINE_RAW = True


class Em:
    def __init__(self, nc, ndma=8):
        self.nc = nc
        self.engs = ['pe', 'act', 'dve', 'pool', 'sp']
        self.prog = {e: [] for e in self.engs}
        self.cnt = {e: 0 for e in self.engs}
        self.seen = {e: {} for e in self.engs}
        self.res = {}
        self.ndma = ndma
        self.slotval = {}
        self.dnext = {e: 0 for e in self.engs}
        self.stack = contextlib.ExitStack()
        self.nalloc = 0
        self.psum_banks = []
        self.psum_next = 0
        self.epoch = 0

    def sb(self, shape, dtype=F32, name=None):
        self.nalloc += 1
        name = f"{name or 'sb'}_{self.nalloc}"
        return self.stack.enter_context(self.nc.sbuf_tensor(name, list(shape), dtype))

    def init_psum(self):
        for i in range(8):
            t = self.stack.enter_context(self.nc.psum_tensor(f"psb{i}", [128, 512], F32))
            self.psum_banks.append(t)

    def psum(self):
        i = self.psum_next
        self.psum_next = (i + 1) % 8
        return self.psum_banks[i], ('ps', i)

    def _wait(self, eng, tok, kind):
        if tok is None:
            return
        sk, val = tok
        if sk[0] == 'e' and sk[1] == eng:
            if eng in ('pe', 'sp'):
                return
            if not SAME_ENGINE_RAW:
                return
        if self.seen[eng].get(sk, 0) >= val:
            return
        self.seen[eng][sk] = val
        self.prog[eng].append(('wait', sk, val))

    def _deps(self, eng, reads, writes):
        for r in reads:
            e = self.res.get(r)
            if e is not None:
                self._wait(eng, e[0], 'raw')
        for w in writes:
            e = self.res.get(w)
            if e is not None:
                self._wait(eng, e[0], 'waw')
                for sk, val in e[1].items():
                    self._wait(eng, (sk, val), 'war')

    def _mark(self, tok, reads, writes):
        sk, val = tok
        for r in reads:
            e = self.res.setdefault(r, [None, {}])
            e[1][sk] = max(e[1].get(sk, 0), val)
        for w in writes:
            self.res[w] = [tok, {}]

    def op(self, eng, fn, reads=(), writes=(), inc=True):
        self._deps(eng, reads, writes)
        tok = (('e', eng, self.epoch), self.cnt[eng] + 1)
        if inc:
            self.cnt[eng] += 1
        self.prog[eng].append(('op', fn, inc, ('e', eng, self.epoch)))
        self._mark(tok, reads, writes)

    def dma(self, issuer, out, in_, reads=(), writes=(), **kw):
        self._deps(issuer, reads, writes)
        slot = self.dnext[issuer]
        self.dnext[issuer] = (slot + 1) % self.ndma
        sk = ('dma', issuer, slot)
        cur = self.slotval.get(sk, 0)
        if cur:
            self._wait(issuer, (sk, cur), 'raw')
        self.slotval[sk] = cur + 16
        tok = (sk, cur + 16)
        self.prog[issuer].append(('dma', out, in_, sk, kw))
        self._mark(tok, reads, writes)

    def barrier(self):
        toks = [(sk, v) for sk, v in self.slotval.items()]
        toks += [(('e', e, self.epoch), self.cnt[e]) for e in self.engs if self.cnt[e]]
        for e in self.engs:
            for tok in toks:
                if not (tok[0][0] == 'e' and tok[0][1] == e):
                    self._wait(e, tok, 'raw')
        self.res = {}
        self.epoch += 1
        self.cnt = {e: 0 for e in self.engs}

    def build(self):
        nc = self.nc
        for sk, v in self.slotval.items():
            self._wait('sp', (sk, v), 'raw')
        for e in self.engs:
            if e != 'sp' and self.cnt[e]:
                self._wait('sp', (('e', e, self.epoch), self.cnt[e]), 'raw')
        semkeys = set()
        for e in self.engs:
            for it in self.prog[e]:
                if it[0] == 'wait':
                    semkeys.add(it[1])
                elif it[0] == 'dma':
                    semkeys.add(it[3])
                elif it[0] == 'op' and it[2]:
                    semkeys.add(it[3])
        sems = {}
        for i, sk in enumerate(sorted(semkeys, key=str)):
            nm = "s_" + ("_".join(str(x) for x in sk) if isinstance(sk, tuple) else sk)
            sems[sk] = self.stack.enter_context(nc.semaphore(nm))
        prog = self.prog

        def run(eng_name):
            def body(e):
                for it in prog[eng_name]:
                    if it[0] == 'wait':
                        e.wait_ge(sems[it[1]], it[2])
                    elif it[0] == 'op':
                        ins = it[1](e)
                        if it[2]:
                            ins.then_inc(sems[it[3]], 1)
                    else:
                        e.dma_start(out=it[1], in_=it[2], **it[4]).then_inc(sems[it[3]], 16)
            return body

        with nc.Block() as block:
            if prog['sp']:
                block.sync(run('sp'))
            if prog['pe']:
                block.tensor(run('pe'))
            if prog['act']:
                block.scalar(run('act'))
            if prog['dve']:
                block.vector(run('dve'))
            if prog['pool']:
                block.gpsimd(run('pool'))
        self.stack.close()
        n = {e: len(prog[e]) for e in self.engs}
        return n


S = 4096
D = 1024
NTILE = 32
NG = 8
TG = 512
PAD = 4
DFF = 2816
RW_OFF, MQ_OFF, MK_OFF, MV_OFF = 0, 1024, 1536, 2048
LQ_OFF, LK_OFF, LV_OFF, LO_OFF, LI_OFF, LF_OFF, GT_OFF = 2560, 2816, 3072, 3328, 3584, 3588, 3592
INW = 6664
FM_RW, FM_MQK, FM_LQK, FM_LIF = 0, 1024, 2048, 2560
NFM = 2568

IN_SPECS = [
    ("x", [S, D]), ("p", [2, S, 256]),
    ("ln_mix_pre", [2, D]), ("ln_mix_post", [2, D]), ("ln_ffn_pre", [2, D]), ("ln_ffn_post", [2, D]), ("ln_ple", [2, D]),
    ("w_in", [2, D, INW]), ("rwkv_mu", [2, 1024]), ("rwkv_w0", [2, 256]), ("rwkv_w2", [2, 64, 256]), ("rwkv_a0", [2, 256]),
    ("rwkv_a2", [2, 64, 256]), ("rwkv_g2", [2, 128, 256]), ("rwkv_k_k", [2, 256]), ("rwkv_k_a", [2, 256]), ("rwkv_r_k", [2, 256]),
    ("rwkv_gn_g", [2, 256]), ("rwkv_gn_b", [2, 256]), ("rwkv_v0", [1, 256]), ("rwkv_v1", [1, 256, 32]), ("rwkv_v2", [1, 32, 256]),
    ("mlstm_conv_w", [2, 4, 512]), ("mlstm_conv_b", [2, 512]), ("mlstm_i_b", [2, 4]), ("mlstm_f_b", [2, 4]), ("mlstm_hn_g", [2, 256]),
    ("w_br_rwkv", [2, 256, D]), ("w_br_moba", [2, 512, D]), ("w_br_mlstm", [2, 256, D]), ("w_out", [2, D, D]),
    ("ffn_up", [2, D, 2 * DFF]), ("ffn_conv_w", [2, 3, 2 * DFF]), ("ffn_conv_b", [2, 2 * DFF]), ("ffn_down", [2, DFF, D]),
    ("ple_proj", [2, 256, D]), ("ple_gate", [2, D, D]),
]


def bcast(ap, dim, n):
    l = [list(a) for a in ap.ap]
    l[dim] = [0, n]
    return bass.AP(ap.tensor, ap.offset, l)


def kname(ap):
    return ap.tensor.name


class KB:
    def __init__(self, dbg=None, ext_in=(), ext_out=()):
        self.nc = nc = bass.Bass("TRN2", target_bir_lowering=False)
        self.em = Em(nc)
        self.em.init_psum()
        self.I = {}
        for name, shape in IN_SPECS:
            self.I[name] = nc.dram_tensor(name, shape, F32, kind="ExternalInput").ap()
        self.out = nc.dram_tensor("out", [S, D], F32, kind="ExternalOutput").ap()
        self.dbg = dbg or {}
        self.dbg_out = {}
        mk = lambda n, sh, dt=F32: nc.dram_tensor(n, sh, dt, kind=("ExternalInput" if n in ext_in else "ExternalOutput" if n in ext_out else "Internal")).ap()
        self.xres = mk("xres", [S, D])
        self.s_fm = mk("s_fm", [NFM, PAD + S])
        self.s_tm = mk("s_tm", [S, 1024])
        self.s_y = mk("s_y", [S, 1024])
        self.s_hT = mk("s_hT", [128, 8, S], BF16)
        self.vfirst = mk("vfirst", [256, S])
        self.rot_cache = {}
        self.pp = {}
        self.consts()

    def keys(self, aps, override):
        if override is not None:
            return list(override)
        ks = []
        for a in aps:
            if a is None or isinstance(a, (int, float)):
                continue
            k = kname(a)
            if k not in ks:
                ks.append(k)
        return ks

    def rot(self, name, shape, dtype, n):
        key = (name, self.scope_id)
        if key not in self.rot_cache:
            self.rot_cache[key] = [[self.em.sb(shape, dtype, name=f"{name}_{self.scope_id}_{i}") for i in range(n)], 0]
        ent = self.rot_cache[key]
        t = ent[0][ent[1]]
        ent[1] = (ent[1] + 1) % n
        return t

    scope_id = 0
    scope_ctr = 0

    @contextlib.contextmanager
    def scope(self):
        em = self.em
        old = em.stack
        em.stack = contextlib.ExitStack()
        old_id = self.scope_id
        KB.scope_ctr += 1
        self.scope_id = KB.scope_ctr
        try:
            yield
        finally:
            em.barrier()
            em.stack.close()
            em.stack = old
            self.scope_id = old_id

    def ps(self, pool='g'):
        banks = self.pp.setdefault(pool, {'g': [0, 1, 2, 3, 4, 5, 6, 7]}.get(pool))
        st = self.pp.setdefault(pool + '_i', [0])
        b = banks[st[0] % len(banks)]
        st[0] += 1
        return self.em.psum_banks[b]

    def set_pools(self, **pools):
        for k, v in pools.items():
            self.pp[k] = v
            self.pp[k + '_i'] = [0]

    def tt(self, eng, out, in0, in1, op, rk=None, wk=None):
        self.em.op(eng, lambda e: e.tensor_tensor(out=out, in0=in0, in1=in1, op=op), reads=self.keys([in0, in1], rk), writes=self.keys([out], wk))

    def ts(self, eng, out, in0, s1, s2, op0, op1=None, rk=None, wk=None, accum=None):
        def f(e):
            kw = {}
            if accum is not None:
                kw['accum_out'] = accum
            if op1 is None:
                return e.tensor_scalar(out=out, in0=in0, scalar1=s1, scalar2=s2, op0=op0, **kw)
            return e.tensor_scalar(out=out, in0=in0, scalar1=s1, scalar2=s2, op0=op0, op1=op1, **kw)
        self.em.op(eng, f, reads=self.keys([in0, s1, s2], rk), writes=self.keys([out, accum], wk))

    def stt(self, eng, out, in0, sc, in1, op0, op1, rk=None, wk=None):
        self.em.op(eng, lambda e: e.scalar_tensor_tensor(out=out, in0=in0, scalar=sc, in1=in1, op0=op0, op1=op1),
                   reads=self.keys([in0, sc, in1], rk), writes=self.keys([out], wk))

    def act(self, out, in_, func, bias=None, scale=None, accum=None, rk=None, wk=None, eng='act'):
        def f(e):
            kw = {}
            if bias is not None:
                kw['bias'] = bias
            if scale is not None:
                kw['scale'] = scale
            if accum is not None:
                kw['accum_out'] = accum
            return e.activation(out=out, in_=in_, func=func, **kw)
        self.em.op(eng, f, reads=self.keys([in_, bias, scale], rk), writes=self.keys([out, accum], wk))

    def cp(self, eng, out, in_, rk=None, wk=None):
        if eng == 'act':
            self.em.op(eng, lambda e: e.copy(out=out, in_=in_), reads=self.keys([in_], rk), writes=self.keys([out], wk))
        else:
            self.em.op(eng, lambda e: e.tensor_copy(out=out, in_=in_), reads=self.keys([in_], rk), writes=self.keys([out], wk))

    fp32r = False
    rw_bfrac = 1.0
    rm_bfrac = 1.0

    def mm(self, out, lhsT, rhs, start=True, stop=True, rk=None, wk=None):
        reads = self.keys([lhsT, rhs], rk)
        if self.fp32r and lhsT.dtype == F32 and rhs.dtype == F32:
            lhsT = lhsT.bitcast(mybir.dt.float32r)
            rhs = rhs.bitcast(mybir.dt.float32r)
        self.em.op('pe', lambda e: e.matmul(out, lhsT=lhsT, rhs=rhs, start=start, stop=stop),
                   reads=reads, writes=self.keys([out], wk), inc=stop)

    def tr(self, out, in_, ident, rk=None, wk=None):
        self.em.op('pe', lambda e: e.transpose(out=out, in_=in_, identity=ident), reads=self.keys([in_, ident], rk), writes=self.keys([out], wk))

    def memset(self, eng, ap, val, wk=None):
        self.em.op(eng, lambda e: e.memset(ap, val), writes=self.keys([ap], wk))

    def asel(self, out, pattern, cmp, fill, base, cm):
        self.em.op('pool', lambda e: e.affine_select(out=out, in_=out, pattern=pattern, compare_op=cmp, fill=fill, base=base, channel_multiplier=cm),
                   reads=self.keys([out], None), writes=self.keys([out], None))

    def ld(self, out, in_, rk=None, wk=None, q='sp', **kw):
        self.em.dma(q, out, in_, reads=self.keys([in_], rk), writes=self.keys([out], wk), **kw)

    def st(self, out, in_, rk=None, wk=None, q='pool', **kw):
        self.em.dma(q, out, in_, reads=self.keys([in_], rk), writes=self.keys([out], wk), **kw)

    def capture(self, fn, *args):
        real = self.em

        class _Rec:
            def __init__(s):
                s.items = []

            def op(s, *a, **kw):
                s.items.append(('op', a, kw))

            def dma(s, *a, **kw):
                s.items.append(('dma', a, kw))

            def __getattr__(s, name):
                return getattr(real, name)

        rec = _Rec()
        self.em = rec
        try:
            fn(*args)
        finally:
            self.em = real
        return rec.items

    @staticmethod
    def merge_streams(A, B, bfrac=1.0):
        na, nb = len(A), len(B)
        ia = ib = 0
        out = []
        while ia < na or ib < nb:
            if ib >= nb or (ia < na and ia * nb <= ib * na * bfrac):
                out.append(A[ia])
                ia += 1
            else:
                out.append(B[ib])
                ib += 1
        return out

    def emit_interleaved(self, A, B, bfrac=1.0):
        for it in self.merge_streams(A, B, bfrac):
            getattr(self.em, it[0])(*it[1], **it[2])

    def consts(self):
        em = self.em
        self.identf = em.sb([128, 128], F32, "identf")
        self.identb = em.sb([128, 128], BF16, "identb")
        self.memset('pool', self.identf[:], 0.0)
        self.asel(self.identf[:], [[-1, 128]], ALU.not_equal, 1.0, 0, 1)
        self.cp('pool', self.identb[:], self.identf[:])
        self.zeros = em.sb([128, 512], F32, "zeros")
        self.memset('pool', self.zeros[:], 0.0)
        for r0 in range(0, NFM, 128):
            n = min(128, NFM - r0)
            self.st(self.s_fm[r0:r0 + n, 0:PAD], self.zeros[0:n, 0:PAD], wk=[('s_fm_pad', r0)])

    def colvec(self, src_vec, C, name):
        t = self.em.sb([128, C], F32, name)
        self.ld(t[:], src_vec.rearrange("(c p) -> p c", p=128), allow_slow_non_contiguous=True)
        return t

    def rowbc(self, src_vec, n, P, name):
        t = self.em.sb([P, n], F32, name)
        self.ld(t[:], src_vec.partition_broadcast(P))
        return t

    def norm_group(self, src, srckey, g, hT_dst, hT_key, eps=1e-6):
        for tt in range(4):
            t = g * 4 + tt
            xt = self.rot('ng_x', [128, D], F32, 2)
            self.ld(xt[:], src[t * 128:(t + 1) * 128, :], rk=[(srckey, t)])
            sq = self.rot('ng_sq', [128, D], BF16, 2)
            ss = self.rot('ng_ss', [128, 1], F32, 4)
            self.act(sq[:], xt[:], AF.Square, accum=ss[:])
            rs = self.rot('ng_rs', [128, 1], F32, 4)
            self.act(rs[:], ss[:], AF.Sqrt, scale=1.0 / D, bias=eps)
            self.em.op('dve', lambda e, rs=rs: e.reciprocal(out=rs[:], in_=rs[:]), reads=[kname(rs[:])], writes=[kname(rs[:])])
            hb = self.rot('ng_hb', [128, D], BF16, 2)
            self.ts('dve', hb[:], xt[:], rs[:, 0:1], None, ALU.mult)
            pt = self.ps('tr')
            ptb = pt.bitcast(BF16)
            for c in range(8):
                self.tr(ptb[:, c * 128:(c + 1) * 128], hb[:, c * 128:(c + 1) * 128], self.identb[:])
            self.cp('act' if tt % 2 == 0 else 'dve', hT_dst[:, :, tt * 128:(tt + 1) * 128], ptb[:, 0:1024].rearrange("p (c t) -> p c t", c=8),
                    wk=[hT_key])

    def load_w(self, src, K, ncols, gcol=None, dst=None, dst_key=None, eng='pool', kchunk=None, nbuf=2, engs=('act', 'dve', 'act', 'dve', 'pool'), tag=''):
        if dst is None:
            dst = self.rot(f'wb_{K}_{ncols}{tag}', [128, K, ncols], BF16, nbuf)
            dst = dst[:]
        kc = kchunk or max(1, min(K, 4096 // ncols))
        k0 = 0
        while k0 < K:
            kn = min(kc, K - k0)
            wf = self.rot(f'wf_{kc * ncols}{tag}', [128, kc * ncols], F32, nbuf)
            wfv = wf[:, 0:kn * ncols].rearrange("p (c n) -> p c n", n=ncols)
            self.ld(wfv, src[k0 * 128:(k0 + kn) * 128, :].rearrange("(c p) n -> p c n", p=128))
            for c in range(kn):
                wk = None
                self.cast_rr = getattr(self, 'cast_rr', 0) + 1
                e_ = engs[self.cast_rr % len(engs)]
                if gcol is not None and e_ == 'pool':
                    e_ = 'act' if self.cast_rr % 2 == 0 else 'dve'
                if gcol is not None:
                    if e_ == 'act':
                        self.act(dst[:, k0 + c, :], wfv[:, c, :], AF.Copy, scale=gcol[:, k0 + c:k0 + c + 1], wk=wk)
                    else:
                        self.ts(e_, dst[:, k0 + c, :], wfv[:, c, :], gcol[:, k0 + c:k0 + c + 1], None, ALU.mult, wk=wk)
                else:
                    self.cp(e_, dst[:, k0 + c, :], wfv[:, c, :], wk=wk)
            k0 += kn
        return dst

    def phase_in(self, l):
        I = self.I
        src, srckey = (I["x"], "x") if l == 0 else (self.xres, "xres")
        with self.scope():
            self.set_pools(tr=[0, 1], mm=[2, 3, 4, 5, 6, 7])
            g_pre = self.colvec(I["ln_mix_pre"][l], 8, "g_pre")
            hT = self.em.sb([128, 8, S], BF16, "hT_all")
            def norms(gs):
                for g in gs:
                    self.norm_group(src, srckey, g, hT[:, :, g * TG:(g + 1) * TG], ('hT', g))
                    self.st(self.s_hT[:, :, g * TG:(g + 1) * TG], hT[:, :, g * TG:(g + 1) * TG], rk=[('hT', g)], wk=[('s_hT', g)])

            w_in = I["w_in"][l]

            def fm_iter(wb, r0, cc, g):
                pt = self.ps('mm')
                for k in range(8):
                    self.mm(pt[:, 0:TG], wb[:, k, cc * 128:(cc + 1) * 128], hT[:, k, g * TG:(g + 1) * TG], start=(k == 0), stop=(k == 7),
                            rk=[kname(wb), ('hT', g)])
                ev = self.rot('tm_ev', [128, 512], F32, 4)
                self.cp('act' if g % 2 == 0 else 'dve', ev[:], pt[:, 0:TG])
                self.st(self.s_fm[r0 + cc * 128:r0 + (cc + 1) * 128, PAD + g * TG:PAD + (g + 1) * TG], ev[:], wk=[('s_fm', r0 + cc * 128, g)])

            def first_block_g(wb, g):
                for cc in range(4):
                    fm_iter(wb, 0, cc, g)

            norms([0])
            wb0 = self.load_w(w_in[:, 0:512], 8, 512, gcol=g_pre)
            for g in range(NG):
                nxt = self.capture(norms, [g + 1]) if g + 1 < NG else []
                self.emit_interleaved(nxt, self.capture(first_block_g, wb0, g))
            fm_blocks = [(512, 512, 512), (1024, 512, 1024), (1536, 512, 1536), (LQ_OFF, 512, FM_LQK)]
            for (c0, ncols, r0) in fm_blocks:
                wb = self.load_w(w_in[:, c0:c0 + ncols], 8, ncols, gcol=g_pre)
                for cc in range(ncols // 128):
                    for g in range(NG):
                        pt = self.ps('mm')
                        for k in range(8):
                            self.mm(pt[:, 0:TG], wb[:, k, cc * 128:(cc + 1) * 128], hT[:, k, g * TG:(g + 1) * TG], start=(k == 0), stop=(k == 7),
                                    rk=[kname(wb), ('hT', g)])
                        ev = self.rot('tm_ev', [128, 512], F32, 4)
                        self.cp('act' if g % 2 == 0 else 'dve', ev[:], pt[:, 0:TG])
                        self.st(self.s_fm[r0 + cc * 128:r0 + (cc + 1) * 128, PAD + g * TG:PAD + (g + 1) * TG], ev[:], wk=[('s_fm', r0 + cc * 128, g)])
            wb = self.load_w(w_in[:, LI_OFF:LI_OFF + 8], 8, 8, gcol=g_pre)
            for g in range(NG):
                pt = self.ps('mm')
                for k in range(8):
                    self.mm(pt[0:8, 0:TG], wb[:, k, 0:8], hT[:, k, g * TG:(g + 1) * TG], start=(k == 0), stop=(k == 7), rk=[kname(wb), ('hT', g)])
                ev = self.rot('tm_ev', [128, 512], F32, 4)
                self.cp('act' if g % 2 == 0 else 'dve', ev[0:8, :], pt[0:8, 0:TG])
                self.st(self.s_fm[FM_LIF:FM_LIF + 8, PAD + g * TG:PAD + (g + 1) * TG], ev[0:8, :], wk=[('s_fm', FM_LIF, g)])
            for (c0, tc0) in [(MV_OFF, 0), (LV_OFF, 512)]:
                wb = self.load_w(w_in[:, c0:c0 + 512], 8, 512, gcol=g_pre)
                for t in range(NTILE):
                    pt = self.ps('mm')
                    for k in range(8):
                        self.mm(pt[:, 0:512], hT[:, k, t * 128:(t + 1) * 128], wb[:, k, :], start=(k == 0), stop=(k == 7), rk=[kname(wb), ('hT', t // 4)])
                    ev = self.rot('tm_ev', [128, 512], F32, 4)
                    self.cp('act' if t % 2 == 0 else 'dve', ev[:], pt[:, 0:512])
                    self.st(self.s_tm[t * 128:(t + 1) * 128, tc0:tc0 + 512], ev[:], wk=[('s_tm', tc0, t)])

    def resid_epilogue(self, pts, t, gbc, xsrc, xsrckey, dst, dstkey, eps=1e-6):
        ssa = self.rot('ep_ss', [128, 2], F32, 4)
        junk = self.rot('ep_junk', [128, 512], BF16, 2)
        for hh in range(2):
            self.act(junk[:], pts[hh][:, 0:512], AF.Square, accum=ssa[:, hh:hh + 1])
        rs = self.rot('ep_rs', [128, 1], F32, 4)
        self.tt('dve', rs[:], ssa[:, 0:1], ssa[:, 1:2], ALU.add)
        self.act(rs[:], rs[:], AF.Sqrt, scale=1.0 / D, bias=eps)
        self.em.op('dve', lambda e, rs=rs: e.reciprocal(out=rs[:], in_=rs[:]), reads=[kname(rs[:])], writes=[kname(rs[:])])
        xt = self.rot('ep_x', [128, D], F32, 2)
        self.ld(xt[:], xsrc[t * 128:(t + 1) * 128, :], rk=[(xsrckey, t)])
        ot = self.rot('ep_o', [128, D], F32, 2)
        for hh in range(2):
            self.stt('dve', ot[:, hh * 512:(hh + 1) * 512], pts[hh][:, 0:512], rs[:, 0:1], gbc[:, hh * 512:(hh + 1) * 512], ALU.mult, ALU.mult)
        self.tt('pool', ot[:], ot[:], xt[:], ALU.add)
        self.st(dst[t * 128:(t + 1) * 128, :], ot[:], wk=[(dstkey, t)])

    def phase_merge(self, l, wgt_pf=None, wbr_pf=None, wout_pf=None):
        I = self.I
        xsrc, xkey = (I["x"], "x") if l == 0 else (self.xres, "xres")
        with self.scope():
            self.set_pools(tr=[0], mm=[1, 2, 3, 4, 5], o=[6, 7])
            g_pre = self.colvec(I["ln_mix_pre"][l], 8, "g_pre_m")
            gpost = self.rowbc(I["ln_mix_post"][l], D, 128, "gpost_bc")
            wbr = wbr_pf if wbr_pf is not None else self.em.sb([128, 8, D], BF16, "wbr")
            wgt = wgt_pf if wgt_pf is not None else self.em.sb([128, 8, 3072], BF16, "wgt")
            wout = wout_pf if wout_pf is not None else self.em.sb([128, 8, D], BF16, "wout")
            with self.scope():
                if wbr_pf is None:
                    self.load_w(I["w_br_rwkv"][l], 2, D, dst=wbr[:, 0:2, :], dst_key='wbr', nbuf=4)
                    self.load_w(I["w_br_moba"][l], 4, D, dst=wbr[:, 2:6, :], dst_key='wbr', nbuf=4)
                    self.load_w(I["w_br_mlstm"][l], 2, D, dst=wbr[:, 6:8, :], dst_key='wbr', nbuf=4)
                for cb in (range(6) if wgt_pf is None else ()):
                    self.load_w(I["w_in"][l][:, GT_OFF + cb * 512:GT_OFF + (cb + 1) * 512], 8, 512, gcol=g_pre, dst=wgt[:, :, cb * 512:(cb + 1) * 512], dst_key='wgt', nbuf=4)
                for cb in (range(2) if wout_pf is None else ()):
                    self.load_w(I["w_out"][l][:, cb * 512:(cb + 1) * 512], 8, 512, dst=wout[:, :, cb * 512:(cb + 1) * 512], dst_key='wout', nbuf=4)
            kgrp = [(0, 2), (2, 6), (6, 8)]
            def stA(g):
                yT = self.rot('mg_yT', [128, 8, TG], BF16, 1)
                for tt in range(4):
                    t = g * 4 + tt
                    yt = self.rot('mg_y', [128, D], F32, 2)
                    self.ld(yt[:], self.s_y[t * 128:(t + 1) * 128, :], rk=['s_y'])
                    yb = self.rot('mg_yb', [128, D], BF16, 2)
                    self.cp('pool', yb[:], yt[:])
                    pt = self.ps('tr')
                    ptb = pt.bitcast(BF16)
                    for c in range(8):
                        self.tr(ptb[:, c * 128:(c + 1) * 128], yb[:, c * 128:(c + 1) * 128], self.identb[:])
                    self.cp('act', yT[:, :, tt * 128:(tt + 1) * 128], ptb[:, 0:1024].rearrange("p (c t) -> p c t", c=8))
                hTg = self.rot('mg_hT', [128, 8, TG], BF16, 1)
                self.ld(hTg[:], self.s_hT[:, :, g * TG:(g + 1) * TG], rk=[('s_hT', g)])
                mT = self.rot('mg_mT', [128, 8, TG], BF16, 2)
                for fc in range(8):
                    acc = self.rot('mg_acc', [128, TG], F32, 2)
                    for b in range(3):
                        pg = self.ps('mm')
                        for k in range(8):
                            self.mm(pg[:, 0:TG], wgt[:, k, b * 1024 + fc * 128:b * 1024 + (fc + 1) * 128], hTg[:, k, :], start=(k == 0), stop=(k == 7))
                        sg = self.rot('mg_sg', [128, TG], F32, 3)
                        self.act(sg[:], pg[:, 0:TG], AF.Sigmoid)
                        pb = self.ps('mm')
                        k0, k1 = kgrp[b]
                        for k in range(k0, k1):
                            self.mm(pb[:, 0:TG], wbr[:, k, fc * 128:(fc + 1) * 128], yT[:, k, :], start=(k == k0), stop=(k == k1 - 1))
                        if b == 0:
                            self.tt('dve', acc[:], sg[:], pb[:, 0:TG], ALU.mult)
                        else:
                            tmp = self.rot('mg_tmp', [128, TG], F32, 2)
                            self.tt('dve', tmp[:], sg[:], pb[:, 0:TG], ALU.mult)
                            if b == 1:
                                self.tt('pool', acc[:], acc[:], tmp[:], ALU.add)
                            else:
                                self.tt('pool', mT[:, fc, :], acc[:], tmp[:], ALU.add)
                MT[g] = mT

            def stB(g):
                mT = MT.pop(g)
                for tt in range(4):
                    t = g * 4 + tt
                    pts = [self.ps('o'), self.ps('o')]
                    for hh in range(2):
                        for k in range(8):
                            self.mm(pts[hh][:, 0:512], mT[:, k, tt * 128:(tt + 1) * 128], wout[:, k, hh * 512:(hh + 1) * 512], start=(k == 0), stop=(k == 7))
                    self.resid_epilogue(pts, t, gpost, xsrc, xkey, self.xres, "xres")


            MT = {}
            self.emit_interleaved(self.capture(stA, 0), [])
            for g in range(NG):
                nxt = self.capture(stA, g + 1) if g + 1 < NG else []
                self.emit_interleaved(nxt, self.capture(stB, g))
    def phase_ffn(self, l):
        I = self.I
        if not hasattr(self, 's_aT'):
            self.s_aT = self.nc.dram_tensor("s_aT", [22, 128, S], BF16, kind="Internal").ap()
        with self.scope():
            self.set_pools(tr=[0, 1], mm0=[2, 3, 4], mm1=[5, 6, 7])
            g_pre = self.colvec(I["ln_ffn_pre"][l], 8, "g_ffn")
            cw = self.em.sb([128, 3, 44], F32, "ffn_cw")
            for j3 in range(3):
                self.ld(cw[:, j3, :], I["ffn_conv_w"][l][j3].rearrange("(c p) -> p c", p=128), allow_slow_non_contiguous=True)
            cb = self.colvec(I["ffn_conv_b"][l], 44, "ffn_cb")
            hT = self.em.sb([128, 8, S], BF16, "ffn_hT_all")
            def fnorms(gs):
                for g in gs:
                    self.norm_group(self.xres, "xres", g, hT[:, :, g * TG:(g + 1) * TG], ('fhT', g))

            fnorms([0])
            uprev = {0: {}, 1: {}}
            pend = {0: [], 1: []}

            def back(s):
                aT_, g_, ucs_, j_ = pend[s].pop(0)
                ge = self.rot('ff_ge%d' % s, [128, TG], F32, 2)
                self.act(ge[:], ucs_[0][:], AF.Gelu_apprx_tanh)
                self.tt('pool', aT_[:, g_ * TG:(g_ + 1) * TG], ge[:], ucs_[1][:], ALU.mult)
                if g_ == NG - 1:
                    self.st(self.s_aT[j_], aT_[:], wk=[('s_aT', j_)], q='sp')

            def up_w(s, j):
                wbs = []
                for half in range(2):
                    col0 = half * DFF + j * 128
                    wbs.append(self.load_w(I["ffn_up"][l][:, col0:col0 + 128], 8, 128, gcol=g_pre, nbuf=3, engs=('act', 'act', 'pool'), tag='s%d' % s))
                aT = self.rot('ff_aT%d' % s, [128, S], BF16, 1)
                return wbs, aT

            def up_jg(s, j, g, wbs, aT):
                ucs = []
                for half in range(2):
                    jj = half * 22 + j
                    wb = wbs[half]
                    pt = self.ps('mm%d' % s)
                    for k in range(8):
                        self.mm(pt[:, 0:TG], wb[:, k, :], hT[:, k, g * TG:(g + 1) * TG], start=(k == 0), stop=(k == 7), rk=[kname(wb), ('fhT', g)])
                    u = self.rot('ff_u%d_%d' % (half, s), [128, TG + 2], F32, 3)
                    self.cp('act', u[:, 2:TG + 2], pt[:, 0:TG])
                    if g == 0:
                        self.memset('pool', u[:, 0:2], 0.0)
                    else:
                        self.cp('pool', u[:, 0:2], uprev[s][half][:, TG:TG + 2])
                    uprev[s][half] = u
                    uc = self.rot('ff_uc%d_%d' % (half, s), [128, TG], F32, 3)
                    self.act(uc[:], pt[:, 0:TG], AF.Identity, scale=cw[:, 2, jj:jj + 1], bias=cb[:, jj:jj + 1])
                    self.stt('dve', uc[:], u[:, 1:TG + 1], cw[:, 1, jj:jj + 1], uc[:], ALU.mult, ALU.add)
                    self.stt('dve', uc[:], u[:, 0:TG], cw[:, 0, jj:jj + 1], uc[:], ALU.mult, ALU.add)
                    ucs.append(uc)
                pend[s].append((aT, g, ucs, j))
                if len(pend[s]) > 1:
                    back(s)

            def run_stream(s, js):
                for j in js:
                    wbs_, aT_j = up_w(s, j)
                    for g in range(NG):
                        up_jg(s, j, g, wbs_, aT_j)
                while pend[s]:
                    back(s)

            wbs_, aT_j = up_w(0, 0)
            for g in range(NG):
                nxt = self.capture(fnorms, [g + 1]) if g + 1 < NG else []
                self.emit_interleaved(nxt, self.capture(up_jg, 0, 0, g, wbs_, aT_j))
            self.emit_interleaved(self.capture(run_stream, 0, range(1, 12)), self.capture(run_stream, 1, range(12, 22)))
        with self.scope():
            self.set_pools(o=[0, 1, 2, 3, 4, 5, 6, 7])
            gpost = self.rowbc(I["ln_ffn_post"][l], D, 128, "gffn_post_bc")
            wdn = self.em.sb([128, 22, D], BF16, "wdn")
            for cbk in range(4):
                self.load_w(I["ffn_down"][l][:, cbk * 256:(cbk + 1) * 256], 22, 256, dst=wdn[:, :, cbk * 256:(cbk + 1) * 256], kchunk=8)
            for g in range(NG):
                aTg = self.rot('ff_aTg', [128, 22, TG], BF16, 2)
                self.ld(aTg[:], self.s_aT[:, :, g * TG:(g + 1) * TG].rearrange("j p t -> p j t"), rk=['s_aT'])
                for tt in range(4):
                    t = g * 4 + tt
                    pts = [self.ps('o'), self.ps('o')]
                    for hh in range(2):
                        for j in range(22):
                            self.mm(pts[hh][:, 0:512], aTg[:, j, tt * 128:(tt + 1) * 128], wdn[:, j, hh * 512:(hh + 1) * 512], start=(j == 0), stop=(j == 21))
                    self.resid_epilogue(pts, t, gpost, self.xres, "xres", self.xres, "xres")

    def phase_ple(self, l, dst, dstkey):
        I = self.I
        with self.scope():
            self.set_pools(tr=[0, 1], mm=[2, 3, 4, 5, 6, 7])
            g_pre = self.colvec(I["ln_ple"][l], 8, "g_ple")
            wg = self.em.sb([128, 8, D], BF16, "wpg")
            for cbk in range(2):
                self.load_w(I["ple_gate"][l][:, cbk * 512:(cbk + 1) * 512], 8, 512, gcol=g_pre, dst=wg[:, :, cbk * 512:(cbk + 1) * 512], dst_key='wpg')
            wp = self.em.sb([128, 2, D], BF16, "wpp")
            self.load_w(I["ple_proj"][l], 2, D, dst=wp[:], dst_key='wpp')
            HT = {}

            def stA(g):
                hTg = self.rot('pl_hT', [128, 8, TG], BF16, 2)
                self.norm_group(self.xres, "xres", g, hTg[:], kname(hTg[:]))
                HT[g] = hTg

            def stB(g):
                hTg = HT.pop(g)
                for tt in range(4):
                    t = g * 4 + tt
                    pt_ = self.rot('pl_p', [128, 256], F32, 2)
                    self.ld(pt_[:], I["p"][l][t * 128:(t + 1) * 128, :])
                    pb = self.rot('pl_pb', [128, 256], BF16, 2)
                    self.cp('pool', pb[:], pt_[:])
                    ptr = self.ps('tr')
                    ptrb = ptr.bitcast(BF16)
                    for c in range(2):
                        self.tr(ptrb[:, c * 128:(c + 1) * 128], pb[:, c * 128:(c + 1) * 128], self.identb[:])
                    pT = self.rot('pl_pT', [128, 2, 128], BF16, 2)
                    self.cp('act', pT[:], ptrb[:, 0:256].rearrange("p (c t) -> p c t", c=2))
                    xt = self.rot('pl_x', [128, D], F32, 2)
                    self.ld(xt[:], self.xres[t * 128:(t + 1) * 128, :], rk=[("xres", t)])
                    ot = self.rot('pl_o', [128, D], F32, 2)
                    for hh in range(2):
                        pg = self.ps('mm')
                        for k in range(8):
                            self.mm(pg[:, 0:512], hTg[:, k, tt * 128:(tt + 1) * 128], wg[:, k, hh * 512:(hh + 1) * 512], start=(k == 0), stop=(k == 7))
                        sg = self.rot('pl_sg', [128, 512], F32, 2)
                        self.act(sg[:], pg[:, 0:512], AF.Sigmoid)
                        pp = self.ps('mm')
                        for k in range(2):
                            self.mm(pp[:, 0:512], pT[:, k, :], wp[:, k, hh * 512:(hh + 1) * 512], start=(k == 0), stop=(k == 1))
                        self.tt('dve', sg[:], sg[:], pp[:, 0:512], ALU.mult)
                        self.tt('pool', ot[:, hh * 512:(hh + 1) * 512], sg[:], xt[:, hh * 512:(hh + 1) * 512], ALU.add)
                    self.st(dst[t * 128:(t + 1) * 128, :], ot[:], wk=[(dstkey, t)])

            self.emit_interleaved(self.capture(stA, 0), [])
            for g in range(NG):
                nxt = self.capture(stA, g + 1) if g + 1 < NG else []
                self.emit_interleaved(nxt, self.capture(stB, g))

    def phase_moba(self, l, extra=None):
        with self.scope():
            self.set_pools(sc=[0, 1, 2, 3], acc=[4, 5], ot=[6], bs=[7], tr=[7])
            em = self.em
            KQ = []
            for i_ in range(2):
                Ka = em.sb([80, S], BF16, "mb_KaugT%d" % i_)
                Qa = em.sb([80, S], BF16, "mb_QaugT%d" % i_)
                Va = em.sb([128, 32, 65], BF16, "mb_Vaug%d" % i_)
                self.memset('pool', Va[:, :, 64:65], 1.0)
                KQ.append((Ka, Qa, Va))
            with self.scope():
                ohb = em.sb([16, S], BF16, "mb_ohb")
                oh = em.sb([16, S], F32, "mb_oh")
                self.memset('pool', oh[:], 1.0)
                self.asel(oh[:], [[1, S]], ALU.is_ge, 0.0, 0, -256)
                self.asel(oh[:], [[-1, S]], ALU.is_ge, 0.0, 255, 256)
                self.cp('pool', ohb[:], oh[:])
                for i_ in range(2):
                    self.st(KQ[i_][0][64:80, :], ohb[:], q='sp')
            EX = self.capture(extra) if extra is not None else []
            tri = em.sb([128, 128], BF16, "mb_tri")
            trif = em.sb([128, 128], F32, "mb_trif")
            self.memset('pool', trif[:], 1.0)
            self.asel(trif[:], [[1, 128]], ALU.is_ge, 0.0, 0, -1)
            self.cp('pool', tri[:], trif[:])
            pastm = em.sb([128, 32, 16], F32, "mb_pastm")
            ownm = em.sb([128, 32, 16], F32, "mb_ownm")
            negm = em.sb([128, 32, 16], F32, "mb_negm")
            self.memset('pool', pastm[:], 0.0)
            self.memset('pool', ownm[:], 0.0)
            for tt in range(32):
                ob = tt // 2
                if ob > 0:
                    self.memset('pool', pastm[:, tt, 0:ob], 1.0)
                self.memset('pool', ownm[:, tt, ob:ob + 1], 1.0)
            self.ts('pool', negm[:], pastm[:], -1.0, 1e30, ALU.add, ALU.mult)
            ident65 = self.identf[0:65, 0:65]
            HS = {}

            def setupA(h):
                KaugT, QaugT, Vaug = KQ[h % 2]
                qf = self.rot('mb_qf', [64, S], F32, 1)
                kf = self.rot('mb_kf', [64, S], F32, 1)
                r0 = FM_MQK + h * 64
                self.ld(qf[:], self.s_fm[r0:r0 + 64, PAD:PAD + S])
                self.ld(kf[:], self.s_fm[r0 + 512:r0 + 576, PAD:PAD + S])
                vf = self.rot('mb_vf', [128, 32, 64], F32, 1)
                self.ld(vf[:], self.s_tm[:, h * 64:(h + 1) * 64].rearrange("(n p) d -> p n d", p=128))
                self.cp('dve', Vaug[:, :, 0:64], vf[:])
                self.cp('dve', KaugT[0:64, :], kf[:])
                self.cp('dve', QaugT[0:64, :], qf[:])
                km = self.rot('mb_km', [64, 16], F32, 2)
                self.em.op('dve', lambda e, km=km, kf=kf: e.tensor_reduce(out=km[:], in_=kf[:].rearrange("p (j s) -> p j s", s=256), axis=AX.X, op=ALU.add),
                           reads=[kname(kf[:])], writes=[kname(km[:])])
                bs = self.ps('bs')
                for tt in range(32):
                    self.mm(bs[:, tt * 16:(tt + 1) * 16], qf[:, tt * 128:(tt + 1) * 128], km[:, :])
                bsm = self.rot('mb_bsm', [128, 32, 16], F32, 1)
                self.tt('dve', bsm[:], bs[:, 0:512].rearrange("p (t j) -> p t j", j=16), pastm[:], ALU.mult)
                self.tt('dve', bsm[:], bsm[:], negm[:], ALU.add)
                m8 = self.rot('mb_m8', [128, 32, 8], F32, 1)
                for tt in range(32):
                    self.em.op('dve', lambda e, tt=tt, m8=m8, bsm=bsm: e.max(out=m8[:, tt, :], in_=bsm[:, tt, :]),
                               reads=[kname(bsm[:])], writes=[kname(m8[:])])
                sel = self.rot('mb_sel', [128, 32, 16], F32, 2)
                self.tt('dve', sel[:], bsm[:], bcast(m8[:, :, 2:3], 2, 16), ALU.is_ge)
                self.tt('dve', sel[:], sel[:], pastm[:], ALU.mult)
                self.tt('dve', sel[:], sel[:], ownm[:], ALU.add)
                self.ts('dve', sel[:], sel[:], -1.0, 30000.0, ALU.add, ALU.mult)
                HS[h] = sel

            def setupB(h):
                KaugT, QaugT, Vaug = KQ[h % 2]
                sel = HS[h]
                mbT = self.rot('mb_mbT', [16, S], BF16, 1)
                for t4 in range(8):
                    pt = self.ps('tr')
                    for q in range(4):
                        tt = t4 * 4 + q
                        self.tr(pt[0:16, q * 128:(q + 1) * 128], sel[:, tt, :], self.identf[:])
                    self.cp('dve', mbT[:, t4 * 512:(t4 + 1) * 512], pt[0:16, 0:512])
                self.st(QaugT[64:80, :], mbT[:], q='sp')

            def main(h):
                KaugT, QaugT, Vaug = KQ[h % 2]
                iters = [(tg, st_) for tg in range(8) for st_ in range(4 * (tg + 1))]
                LA = 3
                pTs = {}
                ots = {}

                def front(i):
                    tg, st_ = iters[i]
                    sl_ = st_ - 4 * tg
                    c0 = 256 if sl_ >= 2 else 0
                    sc = self.ps('sc')
                    self.mm(sc[:, c0:512], KaugT[0:80, st_ * 128:(st_ + 1) * 128], QaugT[0:80, tg * 512 + c0:(tg + 1) * 512])
                    pT = self.rot('mb_pT', [128, 512], BF16, 6)
                    self.act(pT[:, c0:512], sc[:, c0:512], AF.Exp, scale=0.125)
                    if sl_ >= 0:
                        if sl_ * 128 > c0:
                            self.memset('pool', pT[:, c0:sl_ * 128], 0.0)
                        self.tt('pool', pT[:, sl_ * 128:(sl_ + 1) * 128], pT[:, sl_ * 128:(sl_ + 1) * 128], tri[:], ALU.mult)
                    pTs[i] = (pT, c0)

                def back(i):
                    tg, st_ = iters[i]
                    nst = 4 * (tg + 1)
                    if st_ == 0:
                        ots[tg] = self.ps('acc')
                    ot = ots[tg]
                    pT, c0 = pTs.pop(i)
                    self.mm(ot[0:65, c0:512], Vaug[:, st_, :], pT[:, c0:512], start=(st_ == 0), stop=(st_ == nst - 1))
                    if st_ == nst - 1:
                        osb = self.rot('mb_osb', [65, 512], F32, 2)
                        self.cp('dve', osb[:], ot[0:65, 0:512])
                        po = self.ps('ot')
                        for qi in range(4):
                            self.tr(po[:, qi * 65:(qi + 1) * 65], osb[0:65, qi * 128:(qi + 1) * 128], ident65)
                        rd = self.rot('mb_rd', [128, 4, 1], F32, 2)
                        pov = po[:, 0:260].rearrange("p (q d) -> p q d", d=65)
                        self.em.op('dve', lambda e, rd=rd, pov=pov: e.reciprocal(out=rd[:], in_=pov[:, :, 64:65]), reads=[kname(po[:])], writes=[kname(rd[:])])
                        yo = self.rot('mb_yo', [128, 4, 64], F32, 2)
                        self.tt('dve', yo[:], pov[:, :, 0:64], bcast(rd[:], 2, 64), ALU.mult)
                        self.st(self.s_y[tg * 512:(tg + 1) * 512, 256 + h * 64:256 + (h + 1) * 64].rearrange("(q p) d -> p q d", p=128), yo[:], wk=[('s_y_b', h, tg)])

                n = len(iters)
                for i in range(min(LA, n)):
                    front(i)
                for i in range(n):
                    if i + LA < n:
                        front(i + LA)
                    back(i)

            setupA(0)
            setupB(0)
            for h in range(8):
                M_ = self.capture(main, h)
                if h + 1 < 8:
                    SA = self.capture(setupA, h + 1)
                    SB = self.capture(setupB, h + 1)
                    c1 = len(M_) // 20
                    c2 = (len(M_) * 3) // 4
                    seq = M_[:c1] + self.merge_streams(M_[c1:c2], SA) + SB + M_[c2:]
                else:
                    seq = M_
                ex_h = EX[(len(EX) * h) // 8:(len(EX) * (h + 1)) // 8]
                self.emit_interleaved(seq, ex_h)

    def phase_mlstm(self, l, scoped=True, pools=None, GW=512):
        I = self.I
        with (self.scope() if scoped else contextlib.nullcontext()):
            self.set_pools(**(pools or dict(tk=[0, 1], qk=[2, 3], acc=[4, 5], P=[6], misc=[7])))
            em = self.em
            NJ = GW // 64
            NGm = S // GW
            cw = em.sb([64, 4, 8], F32, "ml_cw")
            for j in range(4):
                self.ld(cw[:, j, :], I["mlstm_conv_w"][l][j].rearrange("(c p) -> p c", p=64), allow_slow_non_contiguous=True)
            cb = em.sb([64, 8], F32, "ml_cb")
            self.ld(cb[:], I["mlstm_conv_b"][l].rearrange("(c p) -> p c", p=64), allow_slow_non_contiguous=True)
            ib = em.sb([4, 1], F32, "ml_ib")
            fb = em.sb([4, 1], F32, "ml_fb")
            self.ld(ib[:], I["mlstm_i_b"][l].rearrange("(h o) -> h o", o=1))
            self.ld(fb[:], I["mlstm_f_b"][l].rearrange("(h o) -> h o", o=1))
            nfb = em.sb([4, 1], F32, "ml_nfb")
            self.ts('dve', nfb[:], fb[:], -1.0, None, ALU.mult)
            hng = self.rowbc(I["mlstm_hn_g"][l], 256, 64, "ml_hng")
            selT = em.sb([4, 4, 64], F32, "ml_selT")
            self.memset('pool', selT[:], 0.0)
            self.asel(selT[:], [[-1, 4], [0, 64]], ALU.not_equal, 1.0, 0, 1)
            mask4 = em.sb([64, 4, 64], F32, "ml_mask4")
            self.memset('pool', mask4[:], 1.0)
            self.asel(mask4[:], [[0, 4], [1, 64]], ALU.is_ge, 0.0, 0, -1)
            ones4 = em.sb([4, GW], F32, "ml_ones4")
            zeros4 = em.sb([4, GW], F32, "ml_zeros4")
            self.memset('pool', ones4[:], 1.0)
            self.memset('pool', zeros4[:], 0.0)
            Chat = em.sb([64, 4, 65], F32, "ml_Chat0")
            self.memset('dve', Chat[:], 0.0)
            Fprev = None
            cprev = None
            def prepare(g):
                nonlocal Fprev, cprev
                c0 = PAD + g * GW
                li = self.rot('ml_li', [4, GW], F32, 1)
                lf = self.rot('ml_lf', [4, GW], F32, 1)
                self.ld(li[:], self.s_fm[FM_LIF:FM_LIF + 4, c0:c0 + GW])
                self.ld(lf[:], self.s_fm[FM_LIF + 4:FM_LIF + 8, c0:c0 + GW])
                logi = self.rot('ml_logi', [4, GW], F32, 1)
                self.ts('dve', logi[:], li[:], ib[:, 0:1], None, ALU.add)
                e1 = self.rot('ml_e1', [4, GW], F32, 1)
                self.act(e1[:], lf[:], AF.Exp, bias=nfb[:, 0:1], scale=-1.0)
                self.act(e1[:], e1[:], AF.Ln, bias=1.0)
                logf = self.rot('ml_logf', [4, GW], F32, 1)
                self.ts('dve', logf[:], e1[:], -1.0, None, ALU.mult)
                F = self.rot('ml_F', [4, GW], F32, 2)
                self.em.op('dve', lambda e, F=F, logf=logf, init=(0.0 if Fprev is None else Fprev[:, GW - 1:GW]): e.tensor_tensor_scan(
                    out=F[:], data0=ones4[:], data1=logf[:], initial=init, op0=ALU.mult, op1=ALU.add),
                    reads=[kname(ones4[:]), kname(logf[:])] + ([] if Fprev is None else [kname(Fprev[:])]), writes=[kname(F[:])])
                G = self.rot('ml_G', [4, GW], F32, 1)
                self.tt('dve', G[:], logi[:], F[:], ALU.subtract)
                c = self.rot('ml_c', [4, GW], F32, 2)
                self.em.op('dve', lambda e, c=c, G=G, init=(0.0 if cprev is None else cprev[:, GW - 1:GW]): e.tensor_tensor_scan(
                    out=c[:], data0=zeros4[:], data1=G[:], initial=init, op0=ALU.max, op1=ALU.max),
                    reads=[kname(zeros4[:]), kname(G[:])] + ([] if cprev is None else [kname(cprev[:])]), writes=[kname(c[:])])
                cend = c[:].rearrange("p (j s) -> p j s", s=64)[:, :, 63:64]
                wrow = self.rot('ml_wrow', [4, GW], F32, 1)
                self.tt('dve', wrow[:].rearrange("p (j s) -> p j s", s=64), G[:].rearrange("p (j s) -> p j s", s=64), bcast(cend, 2, 64), ALU.subtract)
                self.act(wrow[:], wrow[:], AF.Exp)
                zrow = self.rot('ml_zrow', [4, GW], F32, 1)
                self.tt('dve', zrow[:].rearrange("p (j s) -> p j s", s=64), F[:].rearrange("p (j s) -> p j s", s=64), bcast(cend, 2, 64), ALU.add)
                self.act(zrow[:], zrow[:], AF.Exp, scale=-1.0)
                cpv = self.rot('ml_cpv', [4, NJ], F32, 1)
                if cprev is None:
                    self.memset('dve', cpv[:, 0:1], 0.0)
                else:
                    self.cp('dve', cpv[:, 0:1], cprev[:, GW - 1:GW])
                cend2 = c[:].rearrange("p (j s) -> p j s", s=64)[:, :, 63]
                self.cp('dve', cpv[:, 1:NJ], cend2[:, 0:NJ - 1], rk=[kname(c[:])])
                crow = self.rot('ml_crow', [4, NJ], F32, 1)
                self.tt('dve', crow[:], cpv[:], cend2, ALU.subtract)
                self.act(crow[:], crow[:], AF.Exp)
                pm = self.ps('misc')
                for j in range(NJ):
                    self.tr(pm[0:64, j * 4:(j + 1) * 4], wrow[0:4, j * 64:(j + 1) * 64], self.identf[0:4, 0:4])
                    self.tr(pm[0:64, 64 + j * 4:64 + (j + 1) * 4], zrow[0:4, j * 64:(j + 1) * 64], self.identf[0:4, 0:4])
                for h in range(4):
                    self.mm(pm[0:64, 128 + h * NJ:128 + (h + 1) * NJ], selT[0:4, h, :], crow[0:4, :])
                TP = self.rot('ml_TP', [64, 128 + 4 * NJ], F32, 2)
                self.cp('act', TP[:], pm[0:64, 0:128 + 4 * NJ])
                TPw = TP[:, 0:4 * NJ].rearrange("p (j h) -> p j h", h=4)
                TPz = TP[:, 64:64 + 4 * NJ].rearrange("p (j h) -> p j h", h=4)
                carry = TP[:, 128:128 + 4 * NJ].rearrange("p (h j) -> p h j", j=NJ)
                Fprev, cprev = F, c
                qk = self.rot('ml_qkraw', [64, 8, GW + 3], F32, 1)
                self.ld(qk[:], self.s_fm[FM_LQK:FM_LQK + 512, c0 - 3:c0 + GW].rearrange("(c p) n -> p c n", p=64))
                qkc = self.rot('ml_qkc', [64, 8, GW], F32, 1)
                for ch in range(8):
                    eng = 'dve'
                    self.act(qkc[:, ch, :], qk[:, ch, 3:GW + 3], AF.Identity, scale=cw[:, 3, ch:ch + 1], bias=cb[:, ch:ch + 1])
                    for j3 in range(3):
                        self.stt(eng, qkc[:, ch, :], qk[:, ch, j3:j3 + GW], cw[:, j3, ch:ch + 1], qkc[:, ch, :], ALU.mult, ALU.add)
                qkb = self.rot('ml_qkb', [64, 8, GW], BF16, 2)
                self.act(qkb[:], qkc[:], AF.Silu)
                self.act(qkb[:, 4:8, :], qkb[:, 4:8, :], AF.Copy, scale=0.125)
                vraw = self.rot('ml_vraw', [64, NJ, 256], F32, 1)
                self.ld(vraw[:], self.s_tm[g * GW:(g + 1) * GW, 512:768].rearrange("(j p) n -> p j n", p=64))
                vo = self.rot('ml_oraw', [64, NJ, 256], F32, 2)
                self.ld(vo[:], self.s_tm[g * GW:(g + 1) * GW, 768:1024].rearrange("(j p) n -> p j n", p=64))
                Vaug = self.rot('ml_Vaug', [64, NJ, 4, 65], BF16, 2)
                self.memset('pool', Vaug[:, :, :, 64:65], 1.0)
                self.cp('pool', Vaug[:, :, :, 0:64], vraw[:].rearrange("p j (h d) -> p j h d", d=64))
                CTX[g] = dict(TP=TP, TPw=TPw, carry=carry, qkb=qkb, Vaug=Vaug, vo=vo)

            def tail(g):
                nonlocal Chat
                c_ = CTX.pop(g)
                TP, TPw, carry, qkb, Vaug, vo = (c_[n_] for n_ in ('TP', 'TPw', 'carry', 'qkb', 'Vaug', 'vo'))
                accs = self.rot('ml_accs', [64, NJ, 4, 65], F32, 1)
                for j in range(NJ):
                    t0 = j * 64
                    pk = self.ps('tk')
                    pkb = pk.bitcast(BF16)
                    for h in range(4):
                        self.tr(pkb[0:64, h * 64:(h + 1) * 64], qkb[0:64, 4 + h, t0:t0 + 64], self.identb[0:64, 0:64])
                    Khat = self.rot('ml_Khat', [64, 4, 64], BF16, 2)
                    wb_ = bcast(TPw[:, j, :].unsqueeze(2), 2, 64)
                    self.tt('dve', Khat[:], pkb[0:64, 0:256].rearrange("p (h d) -> p h d", d=64), wb_, ALU.mult, rk=[kname(pk[:]), kname(TP[:])])
                    pq = self.ps('qk')
                    for h in range(4):
                        self.mm(pq[0:64, h * 64:(h + 1) * 64], qkb[0:64, 4 + h, t0:t0 + 64], qkb[0:64, h, t0:t0 + 64])
                    qkw = self.rot('ml_qkw', [64, 4, 64], BF16, 2)
                    self.tt('dve', qkw[:], pq[0:64, 0:256].rearrange("p (h d) -> p h d", d=64), wb_, ALU.mult, rk=[kname(pq[:]), kname(TP[:])])
                    self.tt('pool', qkw[:], qkw[:], mask4[:], ALU.mult)
                    Cs = self.rot('ml_Cs', [64, 4, 65], F32, 2)
                    self.tt('dve', Cs[:], Chat[:], bcast(carry[:, :, j:j + 1], 2, 65), ALU.mult, rk=[kname(Chat[:]), kname(TP[:])])
                    Csb = self.rot('ml_Csb', [64, 4, 65], BF16, 2)
                    self.cp('act', Csb[:], Cs[:])
                    pa = self.ps('acc')
                    for h in range(4):
                        self.mm(pa[0:64, h * 65:(h + 1) * 65], qkb[0:64, h, t0:t0 + 64], Csb[0:64, h, :], start=True, stop=False)
                        self.mm(pa[0:64, h * 65:(h + 1) * 65], qkw[0:64, h, :], Vaug[0:64, j, h, :], start=False, stop=True)
                    pP = self.ps('P')
                    for h in range(4):
                        self.mm(pP[0:64, h * 65:(h + 1) * 65], Khat[0:64, h, :], Vaug[0:64, j, h, :])
                    Chat = self.rot('ml_Chat', [64, 4, 65], F32, 3)
                    self.tt('dve', Chat[:], Cs[:], pP[0:64, 0:260].rearrange("p (h d) -> p h d", d=65), ALU.add)
                    self.cp('act', accs[:, j, :, :], pa[0:64, 0:260].rearrange("p (h d) -> p h d", d=65))
                NB = NJ * 4
                av = accs[:].rearrange("p j h d -> p (j h) d")
                dn = self.rot('ml_dn', [64, NB, 1], F32, 1)
                self.stt('dve', dn[:], av[:, :, 64:65], -1.0, av[:, :, 64:65], ALU.mult, ALU.max)
                self.tt('dve', dn[:], dn[:], TP[:, 64:64 + NB].unsqueeze(2), ALU.max)
                self.em.op('dve', lambda e, dn=dn: e.reciprocal(out=dn[:], in_=dn[:]), reads=[kname(dn[:])], writes=[kname(dn[:])])
                hh_ = self.rot('ml_hh', [64, NB, 64], F32, 1)
                self.tt('dve', hh_[:], av[:, :, 0:64], bcast(dn[:], 2, 64), ALU.mult)
                s1 = self.rot('ml_s1', [64, NB, 1], F32, 1)
                self.em.op('dve', lambda e, s1=s1, hh_=hh_: e.tensor_reduce(out=s1[:, :, 0], in_=hh_[:], axis=AX.X, op=ALU.add), reads=[kname(hh_[:])], writes=[kname(s1[:])])
                self.ts('dve', s1[:], s1[:], 1.0 / 64, None, ALU.mult)
                self.tt('pool', hh_[:], hh_[:], bcast(s1[:], 2, 64), ALU.subtract)
                sq = self.rot('ml_sq', [64, NB, 64], F32, 1)
                self.tt('pool', sq[:], hh_[:], hh_[:], ALU.mult)
                s2 = self.rot('ml_s2', [64, NB, 1], F32, 1)
                self.em.op('dve', lambda e, s2=s2, sq=sq: e.tensor_reduce(out=s2[:, :, 0], in_=sq[:], axis=AX.X, op=ALU.add), reads=[kname(sq[:])], writes=[kname(s2[:])])
                self.act(s2[:], s2[:], AF.Sqrt, scale=1.0 / 64, bias=1e-6)
                self.em.op('dve', lambda e, s2=s2: e.reciprocal(out=s2[:], in_=s2[:]), reads=[kname(s2[:])], writes=[kname(s2[:])])
                self.tt('dve', hh_[:], hh_[:], bcast(s2[:], 2, 64), ALU.mult)
                sgo_v = sq[:].rearrange("p (j h) d -> p j (h d)", h=4)
                self.act(sgo_v, vo[:], AF.Sigmoid)
                hv = hh_[:].rearrange("p (j h) d -> p j (h d)", h=4)
                self.tt('pool', sgo_v, sgo_v, bcast(hng[:].unsqueeze(1), 1, NJ), ALU.mult)
                self.tt('dve', sgo_v, sgo_v, hv, ALU.mult)
                self.st(self.s_y[g * GW:(g + 1) * GW, 768:1024].rearrange("(j p) n -> p j n", p=64), sgo_v, wk=[('s_y_c', g)])


            CTX = {}
            self.emit_interleaved(self.capture(prepare, 0), [])
            for g in range(NGm):
                nxt = self.capture(prepare, g + 1) if g + 1 < NGm else []
                tl = self.capture(tail, g)
                self.emit_interleaved(nxt, tl)
    def phase_rwkv(self, l, scoped=True, pools=None):
        I = self.I
        with (self.scope() if scoped else contextlib.nullcontext()):
            self.set_pools(**(pools or dict(a=[0, 1, 2, 3], c=[4, 5, 6, 7])))
            em = self.em
            GW = 128
            NJ = GW // 64
            NGR = S // GW
            NB = NJ * 4
            hp = lambda v: v.rearrange("(h p) -> p h", p=64)
            mu = I["rwkv_mu"][l]
            mu3 = em.sb([64, 3, 4], F32, "rw_mu3")
            for X in range(3):
                self.ld(mu3[:, X, :], hp(mu[X * 256:(X + 1) * 256]), allow_slow_non_contiguous=True)
            mu_w = em.sb([64, 1], F32, "rw_muw")
            mu_a = em.sb([64, 1], F32, "rw_mua")
            mu_g = em.sb([128, 1], F32, "rw_mug")
            self.ld(mu_w[:], mu[768:832].rearrange("(p o) -> p o", o=1))
            self.ld(mu_a[:], mu[832:896].rearrange("(p o) -> p o", o=1))
            self.ld(mu_g[:], mu[896:1024].rearrange("(p o) -> p o", o=1))
            def hvec(name):
                t = em.sb([64, 4], F32, "rw_" + name)
                self.ld(t[:], hp(I["rwkv_" + name][l]), allow_slow_non_contiguous=True)
                return t
            w0, a0, k_k, k_a, r_k = hvec("w0"), hvec("a0"), hvec("k_k"), hvec("k_a"), hvec("r_k")
            omka = em.sb([64, 4], F32, "rw_omka")
            self.ts('dve', omka[:], k_a[:], -1.0, 1.0, ALU.mult, ALU.add)
            w2 = em.sb([64, 256], F32, "rw_w2")
            a2 = em.sb([64, 256], F32, "rw_a2")
            g2 = em.sb([128, 256], F32, "rw_g2")
            self.ld(w2[:], I["rwkv_w2"][l])
            self.ld(a2[:], I["rwkv_a2"][l])
            self.ld(g2[:], I["rwkv_g2"][l])
            if l > 0:
                v0 = em.sb([64, 4], F32, "rw_v0")
                self.ld(v0[:], hp(I["rwkv_v0"][l - 1]), allow_slow_non_contiguous=True)
                v1 = em.sb([64, 4, 32], F32, "rw_v1")
                self.ld(v1[:], I["rwkv_v1"][l - 1].rearrange("(h p) r -> p h r", p=64))
                v2 = em.sb([32, 256], F32, "rw_v2")
                self.ld(v2[:], I["rwkv_v2"][l - 1])
            gng = self.rowbc(I["rwkv_gn_g"][l], 256, 64, "rw_gng")
            gnb = self.rowbc(I["rwkv_gn_b"][l], 256, 64, "rw_gnb")
            sl4 = em.sb([64, 4, 64], F32, "rw_sl4")
            su4 = em.sb([64, 4, 64], F32, "rw_su4")
            sui4 = em.sb([64, 4, 64], F32, "rw_sui4")
            for t_, pat, base, cm in ((sl4, [[0, 4], [-1, 64]], -1, 1), (su4, [[0, 4], [1, 64]], -1, -1), (sui4, [[0, 4], [1, 64]], 0, -1)):
                self.memset('pool', t_[:], 1.0)
                self.asel(t_[:], pat, ALU.is_ge, 0.0, base, cm)
            segm = em.sb([64, 4 * GW], F32, "rw_segm")
            self.memset('pool', segm[:], 1.0)
            self.memset('pool', segm[:].rearrange("p (n s) -> p n s", s=64)[:, :, 0:1], 0.0)
            ones64 = em.sb([64, 64], F32, "rw_ones64")
            self.memset('pool', ones64[:], 1.0)
            id64 = self.identf[0:64, 0:64]
            id64b = self.identb[0:64, 0:64]
            I8 = em.sb([64, NB, 64], F32, "rw_I8")
            for b_ in range(NB):
                self.cp('pool', I8[:, b_, :], id64)
            ST = em.sb([64, 4, 64], F32, "rw_ST0")
            self.memset('dve', ST[:], 0.0)
            STb = em.sb([64, 4, 64], BF16, "rw_STb0")
            self.memset('dve', STb[:], 0.0)
            bc3 = lambda t_: bcast(t_[:].unsqueeze(2), 2, GW)
            NEG = -0.6065306597126334
            def prepare(g):
                c0 = PAD + g * GW
                tok0 = g * GW
                raw = self.rot('rw_raw', [64, 3, 4, GW + 1], F32, 1)
                for X in range(3):
                    self.ld(raw[:, X, :, :], self.s_fm[X * 256:(X + 1) * 256, c0 - 1:c0 + GW].rearrange("(h p) n -> p h n", p=64))
                rwa = self.rot('rw_rwa', [64, 2, GW + 1], F32, 1)
                self.ld(rwa[:], self.s_fm[768:896, c0 - 1:c0 + GW].rearrange("(x p) n -> p x n", p=64))
                rg = self.rot('rw_rg', [128, GW + 1], F32, 1)
                self.ld(rg[:], self.s_fm[896:1024, c0 - 1:c0 + GW])
                L3 = self.rot('rw_L3', [64, 3, 4, GW], F32, 1)
                for X in range(3):
                    d = self.rot('rw_d', [64, 4, GW], F32, 1)
                    self.tt('dve', d[:], raw[:, X, :, 0:GW], raw[:, X, :, 1:GW + 1], ALU.subtract)
                    self.tt('pool', d[:], d[:], bc3(mu3[:, X, :]), ALU.mult)
                    self.tt('dve', L3[:, X, :, :], d[:], raw[:, X, :, 1:GW + 1], ALU.add)
                r_, k_, v_ = L3[:, 0, :, :], L3[:, 1, :, :], L3[:, 2, :, :]
                xwa = self.rot('rw_xwa', [64, 2, GW], F32, 1)
                for X, m_ in ((0, mu_w), (1, mu_a)):
                    d = self.rot('rw_d1', [64, GW], F32, 2)
                    self.tt('dve', d[:], rwa[:, X, 0:GW], rwa[:, X, 1:GW + 1], ALU.subtract)
                    self.stt('dve', xwa[:, X, :], d[:], m_[:, 0:1], rwa[:, X, 1:GW + 1], ALU.mult, ALU.add)
                xg = self.rot('rw_xg', [128, GW], F32, 1)
                dg = self.rot('rw_dg', [128, GW], F32, 1)
                self.tt('dve', dg[:], rg[:, 0:GW], rg[:, 1:GW + 1], ALU.subtract)
                self.stt('dve', xg[:], dg[:], mu_g[:, 0:1], rg[:, 1:GW + 1], ALU.mult, ALU.add)
                self.act(xwa[:, 0, :], xwa[:, 0, :], AF.Tanh)
                self.act(xg[:], xg[:], AF.Sigmoid)
                lw = self.rot('rw_lw', [64, 4, GW], F32, 1)
                a_ = self.rot('rw_a', [64, 4, GW], F32, 1)
                gT = self.rot('rw_gT', [64, 4, GW], F32, 1)
                for h in range(4):
                    p1 = self.ps('a')
                    self.mm(p1[0:64, 0:GW], w2[:, h * 64:(h + 1) * 64], xwa[:, 0, :])
                    self.act(lw[:, h, :], p1[0:64, 0:GW], AF.Sigmoid, bias=w0[:, h:h + 1])
                    p2 = self.ps('a')
                    self.mm(p2[0:64, 0:GW], a2[:, h * 64:(h + 1) * 64], xwa[:, 1, :])
                    self.act(a_[:, h, :], p2[0:64, 0:GW], AF.Sigmoid, bias=a0[:, h:h + 1])
                    p3 = self.ps('a')
                    self.mm(p3[0:64, 0:GW], g2[:, h * 64:(h + 1) * 64], xg[:, :])
                    self.cp('act', gT[:, h, :], p3[0:64, 0:GW])
                if l > 0:
                    p4 = self.ps('a')
                    for h in range(4):
                        self.mm(p4[0:32, 0:GW], v1[:, h, :], L3[:, 2, h, :], start=(h == 0), stop=(h == 3))
                    t1 = self.rot('rw_t1', [32, GW], F32, 1)
                    self.cp('act', t1[:], p4[0:32, 0:GW])
                    sgv = self.rot('rw_sgv', [64, 4, GW], F32, 1)
                    for h in range(4):
                        p5 = self.ps('a')
                        self.mm(p5[0:64, 0:GW], v2[:, h * 64:(h + 1) * 64], t1[:, :])
                        self.act(sgv[:, h, :], p5[0:64, 0:GW], AF.Sigmoid, bias=v0[:, h:h + 1])
                    vf = self.rot('rw_vf', [64, 4, GW], F32, 1)
                    self.ld(vf[:], self.vfirst[:, tok0:tok0 + GW].rearrange("(h p) n -> p h n", p=64))
                    self.tt('dve', vf[:], vf[:], v_, ALU.subtract)
                    self.tt('pool', vf[:], vf[:], sgv[:], ALU.mult)
                    self.tt('dve', v_, v_, vf[:], ALU.add)
                else:
                    self.st(self.vfirst[:, tok0:tok0 + GW].rearrange("(h p) n -> p h n", p=64), v_, wk=[('vfirst', g)])
                kk = self.rot('rw_kk', [64, 4, GW], F32, 1)
                self.tt('dve', kk[:], k_, bc3(k_k), ALU.mult)
                sq = self.rot('rw_sq', [64, 4, GW], F32, 1)
                self.tt('pool', sq[:], kk[:], kk[:], ALU.mult)
                nr = self.rot('rw_nr', [64, 4, GW], F32, 1)
                for h in range(4):
                    p6 = self.ps('a')
                    self.mm(p6[0:64, 0:GW], ones64[:, :], sq[:, h, :])
                    self.act(nr[:, h, :], p6[0:64, 0:GW], AF.Ln, bias=1e-30)
                self.act(nr[:], nr[:], AF.Exp, scale=-0.5)
                self.tt('dve', kk[:], kk[:], nr[:], ALU.mult)
                k2 = self.rot('rw_k2', [64, 4, GW], F32, 1)
                self.tt('pool', k2[:], a_[:], bc3(k_a), ALU.mult)
                self.tt('pool', k2[:], k2[:], bc3(omka), ALU.add)
                self.tt('dve', k2[:], k2[:], k_, ALU.mult)
                b_ = self.rot('rw_b', [64, 4, GW], F32, 1)
                self.tt('pool', b_[:], kk[:], a_[:], ALU.mult)
                cw = self.rot('rw_cw', [64, 4, GW], F32, 1)
                self.em.op('dve', lambda e, cw=cw, lw=lw: e.tensor_tensor_scan(out=cw[:].rearrange("p h n -> p (h n)"), data0=segm[:], data1=lw[:].rearrange("p h n -> p (h n)"),
                                                                               initial=0.0, op0=ALU.mult, op1=ALU.add),
                           reads=[kname(segm[:]), kname(lw[:])], writes=[kname(cw[:])])
                ep = self.rot('rw_ep', [64, 4, GW], F32, 2)
                en = self.rot('rw_en', [64, 4, GW], F32, 1)
                epv = self.rot('rw_epv', [64, 4, GW], F32, 1)
                self.act(ep[:], cw[:], AF.Exp, scale=NEG)
                self.act(en[:], cw[:], AF.Exp, scale=-NEG)
                self.tt('dve', epv[:], cw[:], lw[:], ALU.subtract)
                self.act(epv[:], epv[:], AF.Exp, scale=NEG)
                AR = self.rot('rw_AR', [64, 2, 4, GW], BF16, 2)
                at = AR[:, 0, :, :]
                rt = AR[:, 1, :, :]
                kt = self.rot('rw_kt', [64, 4, GW], BF16, 1)
                bt = self.rot('rw_bt', [64, 4, GW], BF16, 1)
                self.tt('dve', rt[:], r_, ep[:], ALU.mult)
                self.tt('pool', kt[:], k2[:], en[:], ALU.mult)
                self.tt('dve', bt[:], b_[:], en[:], ALU.mult)
                self.stt('dve', at[:], kk[:], -1.0, epv[:], ALU.mult, ALU.mult)
                rk = self.rot('rw_rk', [64, 4, GW], F32, 1)
                self.tt('pool', rk[:], r_, k2[:], ALU.mult)
                self.tt('pool', rk[:], rk[:], bc3(r_k), ALU.mult)
                pb = self.ps('a')
                for j in range(NJ):
                    for h in range(4):
                        self.mm(pb[0:64, j * 4 + h:j * 4 + h + 1], rk[:, h, j * 64:(j + 1) * 64], ones64[:, 0:1])
                bon = self.rot('rw_bon', [64, NB, 1], F32, 2)
                self.cp('act', bon[:, :, 0], pb[0:64, 0:NB])
                TM = {}
                for nm, src_ in (('B', bt[:]), ('K', kt[:]), ('V', v_), ('G', gT[:])):
                    isb = nm in ('B', 'K')
                    dst_ = self.rot('rw_tm' + nm, [64, NJ, 4, 64], BF16 if isb else F32, 2)
                    for j in range(NJ):
                        pt = self.ps('a')
                        ptv = pt.bitcast(BF16) if isb else pt
                        for h in range(4):
                            self.tr(ptv[0:64, h * 64:(h + 1) * 64], src_[:, h, j * 64:(j + 1) * 64], id64b if isb else id64)
                        self.cp('act' if j % 2 == 0 else 'dve', dst_[:, j, :, :], ptv[0:64, 0:256].rearrange("p (h d) -> p h d", d=64))
                    TM[nm] = dst_
                Vb = self.rot('rw_Vb', [64, NJ, 4, 64], BF16, 2)
                self.cp('act', Vb[:], TM['V'][:])
                CM = {}
                for nm in ('A', 'Bm', 'Ak', 'Rb', 'Rk'):
                    CM[nm] = self.rot('rw_cm' + nm, [64, NJ, 4, 64], BF16, 2)
                for j in range(NJ):
                    cs = slice(j * 64, (j + 1) * 64)
                    pt = self.ps('a')
                    for h in range(4):
                        self.mm(pt[0:64, h * 64:(h + 1) * 64], at[:, h, cs], bt[:, h, cs])
                    self.tt('dve', CM['A'][:, j, :, :], pt[0:64, 0:256].rearrange("p (h d) -> p h d", d=64), sl4[:], ALU.mult)
                    for lh, n0, n1 in ((bt, 'Bm', 'Rb'), (kt, 'Ak', 'Rk')):
                        pt = self.ps('a')
                        for h in range(4):
                            self.mm(pt[0:64, h * 128:(h + 1) * 128], lh[:, h, cs], AR[:, :, h, cs])
                        pv = pt[0:64, 0:512].rearrange("p (h x d) -> p h x d", x=2, d=64)
                        self.tt('dve', CM[n0][:, j, :, :], pv[:, :, 0, :], su4[:], ALU.mult)
                        self.tt('pool' if False else 'dve', CM[n1][:, j, :, :], pv[:, :, 1, :], sui4[:], ALU.mult)
                P = CM['Bm'][:].rearrange("p j h d -> p (j h) d")
                Q = CM['A'][:].rearrange("p j h d -> p (j h) d")
                M = self.rot('rw_M', [64, NB, 64], F32, 2)
                self.tt('pool', M[:], I8[:], P, ALU.add)
                M = M[:]
                Mb = self.rot('rw_Mb', [64, NB, 64], BF16, 2)
                self.cp('act', Mb[:], M)
                Mb = Mb[:]
                for lev in range(5):
                    last = (lev == 4)
                    pQ = self.ps('a')
                    for b in range(NB):
                        self.mm(pQ[0:64, b * 64:(b + 1) * 64], P[:, b, :], Q[:, b, :])
                    if not last:
                        pP = self.ps('a')
                        for b in range(NB):
                            self.mm(pP[0:64, b * 64:(b + 1) * 64], Q[:, b, :], P[:, b, :])
                    Qn = self.rot('rw_Qn', [64, NB, 64], BF16, 2)
                    self.cp('act', Qn[:], pQ[0:64, 0:NB * 64].rearrange("p (b d) -> p b d", d=64))
                    if not last:
                        Pn = self.rot('rw_Pn', [64, NB, 64], BF16, 2)
                        self.cp('act', Pn[:], pP[0:64, 0:NB * 64].rearrange("p (b d) -> p b d", d=64))
                        P = Pn[:]
                    Q = Qn[:]
                    pM = self.ps('a')
                    for b in range(NB):
                        self.mm(pM[0:64, b * 64:(b + 1) * 64], Q[:, b, :], Mb[:, b, :])
                    Mn = self.rot('rw_M', [64, NB, 64], F32, 2)
                    self.tt('dve', Mn[:], M, pM[0:64, 0:NB * 64].rearrange("p (b d) -> p b d", d=64), ALU.add)
                    M = Mn[:]
                    Mbn = self.rot('rw_Mb', [64, NB, 64], BF16, 2)
                    self.cp('act', Mbn[:], M)
                    Mb = Mbn[:]
                TTt = self.rot('rw_TT', [64, NB, 64], BF16, 2)
                self.cp('act', TTt[:], M)
                TT = TTt[:]
                CTX[g] = dict(CM=CM, TM=TM, Vb=Vb, at=at, rt=rt, TT=TT, ep=ep, bon=bon, tok0=tok0)

            def tail(g):
                nonlocal ST, STb
                c_ = CTX.pop(g)
                CM, TM, Vb, at, rt, TT, ep, bon, tok0 = (c_[n_] for n_ in ('CM', 'TM', 'Vb', 'at', 'rt', 'TT', 'ep', 'bon', 'tok0'))
                Ysb = self.rot('rw_Ysb', [64, NJ, 4, 64], F32, 1)
                V_, B_, K_ = Vb, TM['B'], TM['K']
                Vf_ = TM['V']
                for j in range(NJ):
                    cs = slice(j * 64, (j + 1) * 64)
                    pX = self.ps('c')
                    for h in range(4):
                        self.mm(pX[0:64, h * 64:(h + 1) * 64], CM['Ak'][:, j, h, :], V_[:, j, h, :], start=True, stop=False)
                        self.mm(pX[0:64, h * 64:(h + 1) * 64], at[:, h, cs], STb[:, h, :], start=False, stop=True)
                    Xsb = self.rot('rw_Xsb', [64, 4, 64], BF16, 2)
                    self.cp('act', Xsb[:], pX[0:64, 0:256].rearrange("p (h d) -> p h d", d=64))
                    pU = self.ps('c')
                    for h in range(4):
                        self.mm(pU[0:64, h * 64:(h + 1) * 64], TT[:, j * 4 + h, :], Xsb[:, h, :])
                    Usb = self.rot('rw_Usb', [64, 4, 64], BF16, 2)
                    self.cp('act', Usb[:], pU[0:64, 0:256].rearrange("p (h d) -> p h d", d=64))
                    pS = self.ps('c')
                    for h in range(4):
                        self.mm(pS[0:64, h * 64:(h + 1) * 64], B_[:, j, h, :], Usb[:, h, :], start=True, stop=False)
                        self.mm(pS[0:64, h * 64:(h + 1) * 64], K_[:, j, h, :], V_[:, j, h, :], start=False, stop=True)
                    pY = self.ps('c')
                    for h in range(4):
                        self.mm(pY[0:64, h * 64:(h + 1) * 64], rt[:, h, cs], STb[:, h, :], start=True, stop=False)
                        self.mm(pY[0:64, h * 64:(h + 1) * 64], CM['Rb'][:, j, h, :], Usb[:, h, :], start=False, stop=False)
                        self.mm(pY[0:64, h * 64:(h + 1) * 64], CM['Rk'][:, j, h, :], V_[:, j, h, :], start=False, stop=True)
                    STn = self.rot('rw_ST', [64, 4, 64], F32, 3)
                    self.tt('dve', STn[:], pS[0:64, 0:256].rearrange("p (h d) -> p h d", d=64), ST[:], ALU.add)
                    self.tt('dve', STn[:], STn[:], bcast(ep[:, :, j * 64 + 63:j * 64 + 64], 2, 64), ALU.mult)
                    ST = STn
                    STb = self.rot('rw_STb', [64, 4, 64], BF16, 3)
                    self.cp('act', STb[:], ST[:])
                    self.cp('act', Ysb[:, j, :, :], pY[0:64, 0:256].rearrange("p (h d) -> p h d", d=64))
                yv = Ysb[:].rearrange("p j h d -> p (j h) d")
                s1 = self.rot('rw_s1', [64, NB, 1], F32, 1)
                self.em.op('dve', lambda e, s1=s1, yv=yv: e.tensor_reduce(out=s1[:, :, 0], in_=yv, axis=AX.X, op=ALU.add), reads=[kname(Ysb[:])], writes=[kname(s1[:])])
                self.ts('dve', s1[:], s1[:], 1.0 / 64, None, ALU.mult)
                yc = self.rot('rw_yc', [64, NB, 64], F32, 1)
                self.tt('dve', yc[:], yv, bcast(s1[:], 2, 64), ALU.subtract)
                sq2 = self.rot('rw_sq2', [64, NB, 64], F32, 1)
                self.tt('pool', sq2[:], yc[:], yc[:], ALU.mult)
                s2 = self.rot('rw_s2', [64, NB, 1], F32, 1)
                self.em.op('dve', lambda e, s2=s2, sq2=sq2: e.tensor_reduce(out=s2[:, :, 0], in_=sq2[:], axis=AX.X, op=ALU.add), reads=[kname(sq2[:])], writes=[kname(s2[:])])
                self.act(s2[:], s2[:], AF.Sqrt, scale=1.0 / 64, bias=64e-5)
                self.em.op('dve', lambda e, s2=s2: e.reciprocal(out=s2[:], in_=s2[:]), reads=[kname(s2[:])], writes=[kname(s2[:])])
                self.tt('dve', yc[:], yc[:], bcast(s2[:], 2, 64), ALU.mult)
                y4 = yc[:].rearrange("p (j h) d -> p j (h d)", h=4)
                self.tt('pool', y4, y4, bcast(gng[:].unsqueeze(1), 1, NJ), ALU.mult)
                self.tt('pool', y4, y4, bcast(gnb[:].unsqueeze(1), 1, NJ), ALU.add)
                bv = self.rot('rw_bv', [64, NB, 64], F32, 1)
                self.tt('dve', bv[:], Vf_[:].rearrange("p j h d -> p (j h) d"), bcast(bon[:], 2, 64), ALU.mult)
                self.tt('dve', yc[:], yc[:], bv[:], ALU.add)
                self.tt('pool', yc[:], yc[:], TM['G'][:].rearrange("p j h d -> p (j h) d"), ALU.mult)
                self.st(self.s_y[tok0:tok0 + GW, 0:256].rearrange("(j p) n -> p j n", p=64), y4, rk=[kname(yc[:])], wk=[('s_y_a', g)])

            CTX = {}
            cur = self.capture(prepare, 0)
            self.emit_interleaved(cur, [])
            for g in range(NGR):
                nxt = self.capture(prepare, g + 1) if g + 1 < NGR else []
                tl = self.capture(tail, g)
                self.emit_interleaved(nxt, tl, self.rw_bfrac)

    def phase_rwkv_mlstm(self, l):
        with self.scope():
            A = self.capture(self.phase_rwkv, l, False, dict(a=[0, 1, 2], c=[3, 4]))
            B = self.capture(self.phase_mlstm, l, False, dict(misc=[5], tk=[6], acc=[6], qk=[7], P=[7]), 256)
            self.emit_interleaved(A, B, self.rm_bfrac)

    def phase_moba_merge(self, l):
        I = self.I
        with self.scope():
            wgt = self.em.sb([128, 8, 3072], BF16, "wgt_pf")
            wbr = self.em.sb([128, 8, D], BF16, "wbr_pf")
            wout = self.em.sb([128, 8, D], BF16, "wout_pf")
            g_pre = self.colvec(I["ln_mix_pre"][l], 8, "g_pre_pf")

            def loader():
                for cb in range(6):
                    self.load_w(I["w_in"][l][:, GT_OFF + cb * 512:GT_OFF + (cb + 1) * 512], 8, 512, gcol=g_pre,
                                dst=wgt[:, :, cb * 512:(cb + 1) * 512], kchunk=2, nbuf=2, engs=('dve',), tag='pf')
                for (nm, k0, kn) in (("w_br_rwkv", 0, 2), ("w_br_moba", 2, 4), ("w_br_mlstm", 6, 2)):
                    for cb in range(2):
                        self.load_w(I[nm][l][:, cb * 512:(cb + 1) * 512], kn, 512, dst=wbr[:, k0:k0 + kn, cb * 512:(cb + 1) * 512],
                                    kchunk=2, nbuf=2, engs=('dve',), tag='pf')
                for cb in range(2):
                    self.load_w(I["w_out"][l][:, cb * 512:(cb + 1) * 512], 8, 512, dst=wout[:, :, cb * 512:(cb + 1) * 512],
                                kchunk=2, nbuf=2, engs=('dve',), tag='pf')

            self.phase_moba(l, extra=loader)
            self.phase_merge(l, wgt_pf=wgt, wbr_pf=wbr, wout_pf=wout)


from concourse.bass_utils import run_bass_kernel_spmd


def build_program():
    kb = KB()
    for l in range(2):
        kb.phase_in(l)
        kb.phase_rwkv_mlstm(l)
        kb.phase_moba_merge(l)
        kb.phase_ffn(l)
        if l == 1:
            kb.phase_ple(l, kb.out, "out")
        else:
            kb.phase_ple(l, kb.xres, "xres")
    kb.em.build()
    return kb


def kernel(**inputs):
    kb = build_program()
    in_maps = []
    for b in range(8):
        m = {}
        for name, shape in IN_SPECS:
            a = np.asarray(inputs[name], dtype=np.float32)
            if name == "x":
                a = a[b]
            elif name == "p":
                a = a[:, b]
            m[name] = np.ascontiguousarray(a)
        in_maps.append(m)
    res = run_bass_kernel_spmd(kb.nc, in_maps, core_ids=list(range(8)))
    return np.stack([np.asarray(r["out"], dtype=np.float32) for r in res.results], axis=0)
```

```python
import contextlib
import numpy as np
import concourse.bass as bass
import concourse.mybir as mybir

F32 = mybir.dt.float32
BF16 = mybir.dt.bfloat16
AF = mybir.ActivationFunctionType
ALU = mybir.AluOpType
AX = mybir.AxisListType

SAME_ENGINE_RAW = True


class Em:
    def __init__(self, nc, ndma=8):
        self.nc = nc
        self.engs = ['pe', 'act', 'dve', 'pool', 'sp']
        self.prog = {e: [] for e in self.engs}
        self.cnt = {e: 0 for e in self.engs}
        self.seen = {e: {} for e in self.engs}
        self.res = {}
        self.ndma = ndma
        self.slotval = {}
        self.dnext = {e: 0 for e in self.engs}
        self.stack = contextlib.ExitStack()
        self.nalloc = 0
        self.psum_banks = []
        self.psum_next = 0
        self.epoch = 0

    def sb(self, shape, dtype=F32, name=None):
        self.nalloc += 1
        name = f"{name or 'sb'}_{self.nalloc}"
        return self.stack.enter_context(self.nc.sbuf_tensor(name, list(shape), dtype))

    def init_psum(self):
        for i in range(8):
            t = self.stack.enter_context(self.nc.psum_tensor(f"psb{i}", [128, 512], F32))
            self.psum_banks.append(t)

    def psum(self):
        i = self.psum_next
        self.psum_next = (i + 1) % 8
        return self.psum_banks[i], ('ps', i)

    def _wait(self, eng, tok, kind):
        if tok is None:
            return
        sk, val = tok
        if sk[0] == 'e' and sk[1] == eng:
            if eng in ('pe', 'sp'):
                return
            if not SAME_ENGINE_RAW:
                return
        if self.seen[eng].get(sk, 0) >= val:
            return
        self.seen[eng][sk] = val
        self.prog[eng].append(('wait', sk, val))

    def _deps(self, eng, reads, writes):
        for r in reads:
            e = self.res.get(r)
            if e is not None:
                self._wait(eng, e[0], 'raw')
        for w in writes:
            e = self.res.get(w)
            if e is not None:
                self._wait(eng, e[0], 'waw')
                for sk, val in e[1].items():
                    self._wait(eng, (sk, val), 'war')

    def _mark(self, tok, reads, writes):
        sk, val = tok
        for r in reads:
            e = self.res.setdefault(r, [None, {}])
            e[1][sk] = max(e[1].get(sk, 0), val)
        for w in writes:
            self.res[w] = [tok, {}]

    def op(self, eng, fn, reads=(), writes=(), inc=True):
        self._deps(eng, reads, writes)
        tok = (('e', eng, self.epoch), self.cnt[eng] + 1)
        if inc:
            self.cnt[eng] += 1
        self.prog[eng].append(('op', fn, inc, ('e', eng, self.epoch)))
        self._mark(tok, reads, writes)

    def dma(self, issuer, out, in_, reads=(), writes=(), **kw):
        self._deps(issuer, reads, writes)
        slot = self.dnext[issuer]
        self.dnext[issuer] = (slot + 1) % self.ndma
        sk = ('dma', issuer, slot)
        cur = self.slotval.get(sk, 0)
        if cur:
            self._wait(issuer, (sk, cur), 'raw')
        self.slotval[sk] = cur + 16
        tok = (sk, cur + 16)
        self.prog[issuer].append(('dma', out, in_, sk, kw))
        self._mark(tok, reads, writes)

    def barrier(self):
        toks = [(sk, v) for sk, v in self.slotval.items()]
        toks += [(('e', e, self.epoch), self.cnt[e]) for e in self.engs if self.cnt[e]]
        for e in self.engs:
            for tok in toks:
                if not (tok[0][0] == 'e' and tok[0][1] == e):
                    self._wait(e, tok, 'raw')
        self.res = {}
        self.epoch += 1
        self.cnt = {e: 0 for e in self.engs}

    def build(self):
        nc = self.nc
        for sk, v in self.slotval.items():
            self._wait('sp', (sk, v), 'raw')
        for e in self.engs:
            if e != 'sp' and self.cnt[e]:
                self._wait('sp', (('e', e, self.epoch), self.cnt[e]), 'raw')
        semkeys = set()
        for e in self.engs:
            for it in self.prog[e]:
                if it[0] == 'wait':
                    semkeys.add(it[1])
                elif it[0] == 'dma':
                    semkeys.add(it[3])
                elif it[0] == 'op' and it[2]:
                    semkeys.add(it[3])
        sems = {}
        for i, sk in enumerate(sorted(semkeys, key=str)):
            nm = "s_" + ("_".join(str(x) for x in sk) if isinstance(sk, tuple) else sk)
            sems[sk] = self.stack.enter_context(nc.semaphore(nm))
        prog = self.prog

        def run(eng_name):
            def body(e):
                for it in prog[eng_name]:
                    if it[0] == 'wait':
                        e.wait_ge(sems[it[1]], it[2])
                    elif it[0] == 'op':
                        ins = it[1](e)
                        if it[2]:
                            ins.then_inc(sems[it[3]], 1)
                    else:
                        e.dma_start(out=it[1], in_=it[2], **it[4]).then_inc(sems[it[3]], 16)
            return body

        with nc.Block() as block:
            if prog['sp']:
                block.sync(run('sp'))
            if prog['pe']:
                block.tensor(run('pe'))
            if prog['act']:
                block.scalar(run('act'))
            if prog['dve']:
                block.vector(run('dve'))
            if prog['pool']:
                block.gpsimd(run('pool'))
        self.stack.close()
        n = {e: len(prog[e]) for e in self.engs}
        return n


S = 4096
D = 1024
NTILE = 32
NG = 8
TG = 512
PAD = 4
DFF = 2816
RW_OFF, MQ_OFF, MK_OFF, MV_OFF = 0, 1024, 1536, 2048
LQ_OFF, LK_OFF, LV_OFF, LO_OFF, LI_OFF, LF_OFF, GT_OFF = 2560, 2816, 3072, 3328, 3584, 3588, 3592
INW = 6664
FM_RW, FM_MQK, FM_LQK, FM_LIF = 0, 1024, 2048, 2560
NFM = 2568

IN_SPECS = [
    ("x", [S, D]), ("p", [2, S, 256]),
    ("ln_mix_pre", [2, D]), ("ln_mix_post", [2, D]), ("ln_ffn_pre", [2, D]), ("ln_ffn_post", [2, D]), ("ln_ple", [2, D]),
    ("w_in", [2, D, INW]), ("rwkv_mu", [2, 1024]), ("rwkv_w0", [2, 256]), ("rwkv_w2", [2, 64, 256]), ("rwkv_a0", [2, 256]),
    ("rwkv_a2", [2, 64, 256]), ("rwkv_g2", [2, 128, 256]), ("rwkv_k_k", [2, 256]), ("rwkv_k_a", [2, 256]), ("rwkv_r_k", [2, 256]),
    ("rwkv_gn_g", [2, 256]), ("rwkv_gn_b", [2, 256]), ("rwkv_v0", [1, 256]), ("rwkv_v1", [1, 256, 32]), ("rwkv_v2", [1, 32, 256]),
    ("mlstm_conv_w", [2, 4, 512]), ("mlstm_conv_b", [2, 512]), ("mlstm_i_b", [2, 4]), ("mlstm_f_b", [2, 4]), ("mlstm_hn_g", [2, 256]),
    ("w_br_rwkv", [2, 256, D]), ("w_br_moba", [2, 512, D]), ("w_br_mlstm", [2, 256, D]), ("w_out", [2, D, D]),
    ("ffn_up", [2, D, 2 * DFF]), ("ffn_conv_w", [2, 3, 2 * DFF]), ("ffn_conv_b", [2, 2 * DFF]), ("ffn_down", [2, DFF, D]),
    ("ple_proj", [2, 256, D]), ("ple_gate", [2, D, D]),
]


def bcast(ap, dim, n):
    l = [list(a) for a in ap.ap]
    l[dim] = [0, n]
    return bass.AP(ap.tensor, ap.offset, l)


def kname(ap):
    return ap.tensor.name


class KB:
    def __init__(self, dbg=None, ext_in=(), ext_out=()):
        self.nc = nc = bass.Bass("TRN2", target_bir_lowering=False)
        self.em = Em(nc)
        self.em.init_psum()
        self.I = {}
        for name, shape in IN_SPECS:
            self.I[name] = nc.dram_tensor(name, shape, F32, kind="ExternalInput").ap()
        self.out = nc.dram_tensor("out", [S, D], F32, kind="ExternalOutput").ap()
        self.dbg = dbg or {}
        self.dbg_out = {}
        mk = lambda n, sh, dt=F32: nc.dram_tensor(n, sh, dt, kind=("ExternalInput" if n in ext_in else "ExternalOutput" if n in ext_out else "Internal")).ap()
        self.xres = mk("xres", [S, D])
        self.s_fm = mk("s_fm", [NFM, PAD + S])
        self.s_tm = mk("s_tm", [S, 1024])
        self.s_y = mk("s_y", [S, 1024])
        self.s_hT = mk("s_hT", [128, 8, S], BF16)
        self.vfirst = mk("vfirst", [256, S])
        self.rot_cache = {}
        self.pp = {}
        self.consts()

    def keys(self, aps, override):
        if override is not None:
            return list(override)
        ks = []
        for a in aps:
            if a is None or isinstance(a, (int, float)):
                continue
            k = kname(a)
            if k not in ks:
                ks.append(k)
        return ks

    def rot(self, name, shape, dtype, n):
        key = (name, self.scope_id)
        if key not in self.rot_cache:
            self.rot_cache[key] = [[self.em.sb(shape, dtype, name=f"{name}_{self.scope_id}_{i}") for i in range(n)], 0]
        ent = self.rot_cache[key]
        t = ent[0][ent[1]]
        ent[1] = (ent[1] + 1) % n
        return t

    scope_id = 0
    scope_ctr = 0

    @contextlib.contextmanager
    def scope(self):
        em = self.em
        old = em.stack
        em.stack = contextlib.ExitStack()
        old_id = self.scope_id
        KB.scope_ctr += 1
        self.scope_id = KB.scope_ctr
        try:
            yield
        finally:
            em.barrier()
            em.stack.close()
            em.stack = old
            self.scope_id = old_id

    def ps(self, pool='g'):
        banks = self.pp.setdefault(pool, {'g': [0, 1, 2, 3, 4, 5, 6, 7]}.get(pool))
        st = self.pp.setdefault(pool + '_i', [0])
        b = banks[st[0] % len(banks)]
        st[0] += 1
        return self.em.psum_banks[b]

    def set_pools(self, **pools):
        for k, v in pools.items():
            self.pp[k] = v
            self.pp[k + '_i'] = [0]

    def tt(self, eng, out, in0, in1, op, rk=None, wk=None):
        self.em.op(eng, lambda e: e.tensor_tensor(out=out, in0=in0, in1=in1, op=op), reads=self.keys([in0, in1], rk), writes=self.keys([out], wk))

    def ts(self, eng, out, in0, s1, s2, op0, op1=None, rk=None, wk=None, accum=None):
        def f(e):
            kw = {}
            if accum is not None:
                kw['accum_out'] = accum
            if op1 is None:
                return e.tensor_scalar(out=out, in0=in0, scalar1=s1, scalar2=s2, op0=op0, **kw)
            return e.tensor_scalar(out=out, in0=in0, scalar1=s1, scalar2=s2, op0=op0, op1=op1, **kw)
        self.em.op(eng, f, reads=self.keys([in0, s1, s2], rk), writes=self.keys([out, accum], wk))

    def stt(self, eng, out, in0, sc, in1, op0, op1, rk=None, wk=None):
        self.em.op(eng, lambda e: e.scalar_tensor_tensor(out=out, in0=in0, scalar=sc, in1=in1, op0=op0, op1=op1),
                   reads=self.keys([in0, sc, in1], rk), writes=self.keys([out], wk))

    def act(self, out, in_, func, bias=None, scale=None, accum=None, rk=None, wk=None, eng='act'):
        def f(e):
            kw = {}
            if bias is not None:
                kw['bias'] = bias
            if scale is not None:
                kw['scale'] = scale
            if accum is not None:
                kw['accum_out'] = accum
            return e.activation(out=out, in_=in_, func=func, **kw)
        self.em.op(eng, f, reads=self.keys([in_, bias, scale], rk), writes=self.keys([out, accum], wk))

    def cp(self, eng, out, in_, rk=None, wk=None):
        if eng == 'act':
            self.em.op(eng, lambda e: e.copy(out=out, in_=in_), reads=self.keys([in_], rk), writes=self.keys([out], wk))
        else:
            self.em.op(eng, lambda e: e.tensor_copy(out=out, in_=in_), reads=self.keys([in_], rk), writes=self.keys([out], wk))

    fp32r = False
    rw_bfrac = 1.0
    rm_bfrac = 1.0

    def mm(self, out, lhsT, rhs, start=True, stop=True, rk=None, wk=None):
        reads = self.keys([lhsT, rhs], rk)
        if self.fp32r and lhsT.dtype == F32 and rhs.dtype == F32:
            lhsT = lhsT.bitcast(mybir.dt.float32r)
            rhs = rhs.bitcast(mybir.dt.float32r)
        self.em.op('pe', lambda e: e.matmul(out, lhsT=lhsT, rhs=rhs, start=start, stop=stop),
                   reads=reads, writes=self.keys([out], wk), inc=stop)

    def tr(self, out, in_, ident, rk=None, wk=None):
        self.em.op('pe', lambda e: e.transpose(out=out, in_=in_, identity=ident), reads=self.keys([in_, ident], rk), writes=self.keys([out], wk))

    def memset(self, eng, ap, val, wk=None):
        self.em.op(eng, lambda e: e.memset(ap, val), writes=self.keys([ap], wk))

    def asel(self, out, pattern, cmp, fill, base, cm):
        self.em.op('pool', lambda e: e.affine_select(out=out, in_=out, pattern=pattern, compare_op=cmp, fill=fill, base=base, channel_multiplier=cm),
                   reads=self.keys([out], None), writes=self.keys([out], None))

    def ld(self, out, in_, rk=None, wk=None, q='sp', **kw):
        self.em.dma(q, out, in_, reads=self.keys([in_], rk), writes=self.keys([out], wk), **kw)

    def st(self, out, in_, rk=None, wk=None, q='pool', **kw):
        self.em.dma(q, out, in_, reads=self.keys([in_], rk), writes=self.keys([out], wk), **kw)

    def capture(self, fn, *args):
        real = self.em

        class _Rec:
            def __init__(s):
                s.items = []

            def op(s, *a, **kw):
                s.items.append(('op', a, kw))

            def dma(s, *a, **kw):
                s.items.append(('dma', a, kw))

            def __getattr__(s, name):
                return getattr(real, name)

        rec = _Rec()
        self.em = rec
        try:
            fn(*args)
        finally:
            self.em = real
        return rec.items

    @staticmethod
    def merge_streams(A, B, bfrac=1.0):
        na, nb = len(A), len(B)
        ia = ib = 0
        out = []
        while ia < na or ib < nb:
            if ib >= nb or (ia < na and ia * nb <= ib * na * bfrac):
                out.append(A[ia])
                ia += 1
            else:
                out.append(B[ib])
                ib += 1
        return out

    def emit_interleaved(self, A, B, bfrac=1.0):
        for it in self.merge_streams(A, B, bfrac):
            getattr(self.em, it[0])(*it[1], **it[2])

    def consts(self):
        em = self.em
        self.identf = em.sb([128, 128], F32, "identf")
        self.identb = em.sb([128, 128], BF16, "identb")
        self.memset('pool', self.identf[:], 0.0)
        self.asel(self.identf[:], [[-1, 128]], ALU.not_equal, 1.0, 0, 1)
        self.cp('pool', self.identb[:], self.identf[:])
        self.zeros = em.sb([128, 512], F32, "zeros")
        self.memset('pool', self.zeros[:], 0.0)
        for r0 in range(0, NFM, 128):
            n = min(128, NFM - r0)
            self.st(self.s_fm[r0:r0 + n, 0:PAD], self.zeros[0:n, 0:PAD], wk=[('s_fm_pad', r0)])

    def colvec(self, src_vec, C, name):
        t = self.em.sb([128, C], F32, name)
        self.ld(t[:], src_vec.rearrange("(c p) -> p c", p=128), allow_slow_non_contiguous=True)
        return t

    def rowbc(self, src_vec, n, P, name):
        t = self.em.sb([P, n], F32, name)
        self.ld(t[:], src_vec.partition_broadcast(P))
        return t

    def norm_group(self, src, srckey, g, hT_dst, hT_key, eps=1e-6):
        for tt in range(4):
            t = g * 4 + tt
            xt = self.rot('ng_x', [128, D], F32, 2)
            self.ld(xt[:], src[t * 128:(t + 1) * 128, :], rk=[(srckey, t)])
            sq = self.rot('ng_sq', [128, D], BF16, 2)
            ss = self.rot('ng_ss', [128, 1], F32, 4)
            self.act(sq[:], xt[:], AF.Square, accum=ss[:])
            rs = self.rot('ng_rs', [128, 1], F32, 4)
            self.act(rs[:], ss[:], AF.Sqrt, scale=1.0 / D, bias=eps)
            self.em.op('dve', lambda e, rs=rs: e.reciprocal(out=rs[:], in_=rs[:]), reads=[kname(rs[:])], writes=[kname(rs[:])])
            hb = self.rot('ng_hb', [128, D], BF16, 2)
            self.ts('dve', hb[:], xt[:], rs[:, 0:1], None, ALU.mult)
            pt = self.ps('tr')
            ptb = pt.bitcast(BF16)
            for c in range(8):
                self.tr(ptb[:, c * 128:(c + 1) * 128], hb[:, c * 128:(c + 1) * 128], self.identb[:])
            self.cp('act' if tt % 2 == 0 else 'dve', hT_dst[:, :, tt * 128:(tt + 1) * 128], ptb[:, 0:1024].rearrange("p (c t) -> p c t", c=8),
                    wk=[hT_key])

    def load_w(self, src, K, ncols, gcol=None, dst=None, dst_key=None, eng='pool', kchunk=None, nbuf=2, engs=('act', 'dve', 'act', 'dve', 'pool'), tag=''):
        if dst is None:
            dst = self.rot(f'wb_{K}_{ncols}{tag}', [128, K, ncols], BF16, nbuf)
            dst = dst[:]
        kc = kchunk or max(1, min(K, 4096 // ncols))
        k0 = 0
        while k0 < K:
            kn = min(kc, K - k0)
            wf = self.rot(f'wf_{kc * ncols}{tag}', [128, kc * ncols], F32, nbuf)
            wfv = wf[:, 0:kn * ncols].rearrange("p (c n) -> p c n", n=ncols)
            self.ld(wfv, src[k0 * 128:(k0 + kn) * 128, :].rearrange("(c p) n -> p c n", p=128))
            for c in range(kn):
                wk = None
                self.cast_rr = getattr(self, 'cast_rr', 0) + 1
                e_ = engs[self.cast_rr % len(engs)]
                if gcol is not None and e_ == 'pool':
                    e_ = 'act' if self.cast_rr % 2 == 0 else 'dve'
                if gcol is not None:
                    if e_ == 'act':
                        self.act(dst[:, k0 + c, :], wfv[:, c, :], AF.Copy, scale=gcol[:, k0 + c:k0 + c + 1], wk=wk)
                    else:
                        self.ts(e_, dst[:, k0 + c, :], wfv[:, c, :], gcol[:, k0 + c:k0 + c + 1], None, ALU.mult, wk=wk)
                else:
                    self.cp(e_, dst[:, k0 + c, :], wfv[:, c, :], wk=wk)
            k0 += kn
        return dst

    def phase_in(self, l):
        I = self.I
        src, srckey = (I["x"], "x") if l == 0 else (self.xres, "xres")
        with self.scope():
            self.set_pools(tr=[0, 1], mm=[2, 3, 4, 5, 6, 7])
            g_pre = self.colvec(I["ln_mix_pre"][l], 8, "g_pre")
            hT = self.em.sb([128, 8, S], BF16, "hT_all")
            def norms(gs):
                for g in gs:
                    self.norm_group(src, srckey, g, hT[:, :, g * TG:(g + 1) * TG], ('hT', g))
                    self.st(self.s_hT[:, :, g * TG:(g + 1) * TG], hT[:, :, g * TG:(g + 1) * TG], rk=[('hT', g)], wk=[('s_hT', g)])

            w_in = I["w_in"][l]

            def fm_iter(wb, r0, cc, g):
                pt = self.ps('mm')
                for k in range(8):
                    self.mm(pt[:, 0:TG], wb[:, k, cc * 128:(cc + 1) * 128], hT[:, k, g * TG:(g + 1) * TG], start=(k == 0), stop=(k == 7),
                            rk=[kname(wb), ('hT', g)])
                ev = self.rot('tm_ev', [128, 512], F32, 4)
                self.cp('act' if g % 2 == 0 else 'dve', ev[:], pt[:, 0:TG])
                self.st(self.s_fm[r0 + cc * 128:r0 + (cc + 1) * 128, PAD + g * TG:PAD + (g + 1) * TG], ev[:], wk=[('s_fm', r0 + cc * 128, g)])

            def first_block_g(wb, g):
                for cc in range(4):
                    fm_iter(wb, 0, cc, g)

            norms([0])
            wb0 = self.load_w(w_in[:, 0:512], 8, 512, gcol=g_pre)
            for g in range(NG):
                nxt = self.capture(norms, [g + 1]) if g + 1 < NG else []
                self.emit_interleaved(nxt, self.capture(first_block_g, wb0, g))
            fm_blocks = [(512, 512, 512), (1024, 512, 1024), (1536, 512, 1536), (LQ_OFF, 512, FM_LQK)]
            for (c0, ncols, r0) in fm_blocks:
                wb = self.load_w(w_in[:, c0:c0 + ncols], 8, ncols, gcol=g_pre)
                for cc in range(ncols // 128):
                    for g in range(NG):
                        pt = self.ps('mm')
                        for k in range(8):
                            self.mm(pt[:, 0:TG], wb[:, k, cc * 128:(cc + 1) * 128], hT[:, k, g * TG:(g + 1) * TG], start=(k == 0), stop=(k == 7),
                                    rk=[kname(wb), ('hT', g)])
                        ev = self.rot('tm_ev', [128, 512], F32, 4)
                        self.cp('act' if g % 2 == 0 else 'dve', ev[:], pt[:, 0:TG])
                        self.st(self.s_fm[r0 + cc * 128:r0 + (cc + 1) * 128, PAD + g * TG:PAD + (g + 1) * TG], ev[:], wk=[('s_fm', r0 + cc * 128, g)])
            wb = self.load_w(w_in[:, LI_OFF:LI_OFF + 8], 8, 8, gcol=g_pre)
            for g in range(NG):
                pt = self.ps('mm')
                for k in range(8):
                    self.mm(pt[0:8, 0:TG], wb[:, k, 0:8], hT[:, k, g * TG:(g + 1) * TG], start=(k == 0), stop=(k == 7), rk=[kname(wb), ('hT', g)])
                ev = self.rot('tm_ev', [128, 512], F32, 4)
                self.cp('act' if g % 2 == 0 else 'dve', ev[0:8, :], pt[0:8, 0:TG])
                self.st(self.s_fm[FM_LIF:FM_LIF + 8, PAD + g * TG:PAD + (g + 1) * TG], ev[0:8, :], wk=[('s_fm', FM_LIF, g)])
            for (c0, tc0) in [(MV_OFF, 0), (LV_OFF, 512)]:
                wb = self.load_w(w_in[:, c0:c0 + 512], 8, 512, gcol=g_pre)
                for t in range(NTILE):
                    pt = self.ps('mm')
                    for k in range(8):
                        self.mm(pt[:, 0:512], hT[:, k, t * 128:(t + 1) * 128], wb[:, k, :], start=(k == 0), stop=(k == 7), rk=[kname(wb), ('hT', t // 4)])
                    ev = self.rot('tm_ev', [128, 512], F32, 4)
                    self.cp('act' if t % 2 == 0 else 'dve', ev[:], pt[:, 0:512])
                    self.st(self.s_tm[t * 128:(t + 1) * 128, tc0:tc0 + 512], ev[:], wk=[('s_tm', tc0, t)])

    def resid_epilogue(self, pts, t, gbc, xsrc, xsrckey, dst, dstkey, eps=1e-6):
        ssa = self.rot('ep_ss', [128, 2], F32, 4)
        junk = self.rot('ep_junk', [128, 512], BF16, 2)
        for hh in range(2):
            self.act(junk[:], pts[hh][:, 0:512], AF.Square, accum=ssa[:, hh:hh + 1])
        rs = self.rot('ep_rs', [128, 1], F32, 4)
        self.tt('dve', rs[:], ssa[:, 0:1], ssa[:, 1:2], ALU.add)
        self.act(rs[:], rs[:], AF.Sqrt, scale=1.0 / D, bias=eps)
        self.em.op('dve', lambda e, rs=rs: e.reciprocal(out=rs[:], in_=rs[:]), reads=[kname(rs[:])], writes=[kname(rs[:])])
        xt = self.rot('ep_x', [128, D], F32, 2)
        self.ld(xt[:], xsrc[t * 128:(t + 1) * 128, :], rk=[(xsrckey, t)])
        ot = self.rot('ep_o', [128, D], F32, 2)
        for hh in range(2):
            self.stt('dve', ot[:, hh * 512:(hh + 1) * 512], pts[hh][:, 0:512], rs[:, 0:1], gbc[:, hh * 512:(hh + 1) * 512], ALU.mult, ALU.mult)
        self.tt('pool', ot[:], ot[:], xt[:], ALU.add)
        self.st(dst[t * 128:(t + 1) * 128, :], ot[:], wk=[(dstkey, t)])

    def phase_merge(self, l, wgt_pf=None, wbr_pf=None, wout_pf=None):
        I = self.I
        xsrc, xkey = (I["x"], "x") if l == 0 else (self.xres, "xres")
        with self.scope():
            self.set_pools(tr=[0], mm=[1, 2, 3, 4, 5], o=[6, 7])
            g_pre = self.colvec(I["ln_mix_pre"][l], 8, "g_pre_m")
            gpost = self.rowbc(I["ln_mix_post"][l], D, 128, "gpost_bc")
            wbr = wbr_pf if wbr_pf is not None else self.em.sb([128, 8, D], BF16, "wbr")
            wgt = wgt_pf if wgt_pf is not None else self.em.sb([128, 8, 3072], BF16, "wgt")
            wout = wout_pf if wout_pf is not None else self.em.sb([128, 8, D], BF16, "wout")
            with self.scope():
                if wbr_pf is None:
                    self.load_w(I["w_br_rwkv"][l], 2, D, dst=wbr[:, 0:2, :], dst_key='wbr', nbuf=4)
                    self.load_w(I["w_br_moba"][l], 4, D, dst=wbr[:, 2:6, :], dst_key='wbr', nbuf=4)
                    self.load_w(I["w_br_mlstm"][l], 2, D, dst=wbr[:, 6:8, :], dst_key='wbr', nbuf=4)
                for cb in (range(6) if wgt_pf is None else ()):
                    self.load_w(I["w_in"][l][:, GT_OFF + cb * 512:GT_OFF + (cb + 1) * 512], 8, 512, gcol=g_pre, dst=wgt[:, :, cb * 512:(cb + 1) * 512], dst_key='wgt', nbuf=4)
                for cb in (range(2) if wout_pf is None else ()):
                    self.load_w(I["w_out"][l][:, cb * 512:(cb + 1) * 512], 8, 512, dst=wout[:, :, cb * 512:(cb + 1) * 512], dst_key='wout', nbuf=4)
            kgrp = [(0, 2), (2, 6), (6, 8)]
            def stA(g):
                yT = self.rot('mg_yT', [128, 8, TG], BF16, 1)
                for tt in range(4):
                    t = g * 4 + tt
                    yt = self.rot('mg_y', [128, D], F32, 2)
                    self.ld(yt[:], self.s_y[t * 128:(t + 1) * 128, :], rk=['s_y'])
                    yb = self.rot('mg_yb', [128, D], BF16, 2)
                    self.cp('pool', yb[:], yt[:])
                    pt = self.ps('tr')
                    ptb = pt.bitcast(BF16)
                    for c in range(8):
                        self.tr(ptb[:, c * 128:(c + 1) * 128], yb[:, c * 128:(c + 1) * 128], self.identb[:])
                    self.cp('act', yT[:, :, tt * 128:(tt + 1) * 128], ptb[:, 0:1024].rearrange("p (c t) -> p c t", c=8))
                hTg = self.rot('mg_hT', [128, 8, TG], BF16, 1)
                self.ld(hTg[:], self.s_hT[:, :, g * TG:(g + 1) * TG], rk=[('s_hT', g)])
                mT = self.rot('mg_mT', [128, 8, TG], BF16, 2)
                for fc in range(8):
                    acc = self.rot('mg_acc', [128, TG], F32, 2)
                    for b in range(3):
                        pg = self.ps('mm')
                        for k in range(8):
                            self.mm(pg[:, 0:TG], wgt[:, k, b * 1024 + fc * 128:b * 1024 + (fc + 1) * 128], hTg[:, k, :], start=(k == 0), stop=(k == 7))
                        sg = self.rot('mg_sg', [128, TG], F32, 3)
                        self.act(sg[:], pg[:, 0:TG], AF.Sigmoid)
                        pb = self.ps('mm')
                        k0, k1 = kgrp[b]
                        for k in range(k0, k1):
                            self.mm(pb[:, 0:TG], wbr[:, k, fc * 128:(fc + 1) * 128], yT[:, k, :], start=(k == k0), stop=(k == k1 - 1))
                        if b == 0:
                            self.tt('dve', acc[:], sg[:], pb[:, 0:TG], ALU.mult)
                        else:
                            tmp = self.rot('mg_tmp', [128, TG], F32, 2)
                            self.tt('dve', tmp[:], sg[:], pb[:, 0:TG], ALU.mult)
                            if b == 1:
                                self.tt('pool', acc[:], acc[:], tmp[:], ALU.add)
                            else:
                                self.tt('pool', mT[:, fc, :], acc[:], tmp[:], ALU.add)
                MT[g] = mT

            def stB(g):
                mT = MT.pop(g)
                for tt in range(4):
                    t = g * 4 + tt
                    pts = [self.ps('o'), self.ps('o')]
                    for hh in range(2):
                        for k in range(8):
                            self.mm(pts[hh][:, 0:512], mT[:, k, tt * 128:(tt + 1) * 128], wout[:, k, hh * 512:(hh + 1) * 512], start=(k == 0), stop=(k == 7))
                    self.resid_epilogue(pts, t, gpost, xsrc, xkey, self.xres, "xres")


            MT = {}
            self.emit_interleaved(self.capture(stA, 0), [])
            for g in range(NG):
                nxt = self.capture(stA, g + 1) if g + 1 < NG else []
                self.emit_interleaved(nxt, self.capture(stB, g))
    def phase_ffn(self, l, down=True):
        I = self.I
        if not hasattr(self, 's_aT'):
            self.s_aT = self.nc.dram_tensor("s_aT", [22, 128, S], BF16, kind="Internal").ap()
        with self.scope():
            self.set_pools(tr=[0, 1], mm0=[2, 3, 4], mm1=[5, 6, 7])
            g_pre = self.colvec(I["ln_ffn_pre"][l], 8, "g_ffn")
            cw = self.em.sb([128, 3, 44], F32, "ffn_cw")
            for j3 in range(3):
                self.ld(cw[:, j3, :], I["ffn_conv_w"][l][j3].rearrange("(c p) -> p c", p=128), allow_slow_non_contiguous=True)
            cb = self.colvec(I["ffn_conv_b"][l], 44, "ffn_cb")
            hT = self.em.sb([128, 8, S], BF16, "ffn_hT_all")
            def fnorms(gs):
                for g in gs:
                    self.norm_group(self.xres, "xres", g, hT[:, :, g * TG:(g + 1) * TG], ('fhT', g))

            fnorms([0])
            uprev = {0: {}, 1: {}}
            pend = {0: [], 1: []}

            def back(s):
                aT_, g_, ucs_, j_ = pend[s].pop(0)
                ge = self.rot('ff_ge%d' % s, [128, TG], F32, 2)
                self.act(ge[:], ucs_[0][:], AF.Gelu_apprx_tanh)
                self.tt('pool', aT_[:, g_ * TG:(g_ + 1) * TG], ge[:], ucs_[1][:], ALU.mult)
                if g_ == NG - 1:
                    self.st(self.s_aT[j_], aT_[:], wk=[('s_aT', j_)], q='sp')

            def up_w(s, j):
                wbs = []
                for half in range(2):
                    col0 = half * DFF + j * 128
                    wbs.append(self.load_w(I["ffn_up"][l][:, col0:col0 + 128], 8, 128, gcol=g_pre, nbuf=3, engs=('act', 'act', 'pool'), tag='s%d' % s))
                aT = self.rot('ff_aT%d' % s, [128, S], BF16, 1)
                return wbs, aT

            def up_jg(s, j, g, wbs, aT):
                ucs = []
                for half in range(2):
                    jj = half * 22 + j
                    wb = wbs[half]
                    pt = self.ps('mm%d' % s)
                    for k in range(8):
                        self.mm(pt[:, 0:TG], wb[:, k, :], hT[:, k, g * TG:(g + 1) * TG], start=(k == 0), stop=(k == 7), rk=[kname(wb), ('fhT', g)])
                    u = self.rot('ff_u%d_%d' % (half, s), [128, TG + 2], F32, 3)
                    self.cp('act', u[:, 2:TG + 2], pt[:, 0:TG])
                    if g == 0:
                        self.memset('pool', u[:, 0:2], 0.0)
                    else:
                        self.cp('pool', u[:, 0:2], uprev[s][half][:, TG:TG + 2])
                    uprev[s][half] = u
                    uc = self.rot('ff_uc%d_%d' % (half, s), [128, TG], F32, 3)
                    self.act(uc[:], pt[:, 0:TG], AF.Identity, scale=cw[:, 2, jj:jj + 1], bias=cb[:, jj:jj + 1])
                    self.stt('dve', uc[:], u[:, 1:TG + 1], cw[:, 1, jj:jj + 1], uc[:], ALU.mult, ALU.add)
                    self.stt('dve', uc[:], u[:, 0:TG], cw[:, 0, jj:jj + 1], uc[:], ALU.mult, ALU.add)
                    ucs.append(uc)
                pend[s].append((aT, g, ucs, j))
                if len(pend[s]) > 1:
                    back(s)

            def run_stream(s, js):
                for j in js:
                    wbs_, aT_j = up_w(s, j)
                    for g in range(NG):
                        up_jg(s, j, g, wbs_, aT_j)
                while pend[s]:
                    back(s)

            wbs_, aT_j = up_w(0, 0)
            for g in range(NG):
                nxt = self.capture(fnorms, [g + 1]) if g + 1 < NG else []
                self.emit_interleaved(nxt, self.capture(up_jg, 0, 0, g, wbs_, aT_j))
            self.emit_interleaved(self.capture(run_stream, 0, range(1, 12)), self.capture(run_stream, 1, range(12, 22)))
        if down:
            self.phase_ffn_down(l)

    def phase_ffn_down(self, l, extra=None):
        I = self.I
        with self.scope():
            self.set_pools(o=[0, 1, 2, 3, 4, 5, 6, 7])
            gpost = self.rowbc(I["ln_ffn_post"][l], D, 128, "gffn_post_bc")
            wdn = self.em.sb([128, 22, D], BF16, "wdn")
            for cbk in range(4):
                self.load_w(I["ffn_down"][l][:, cbk * 256:(cbk + 1) * 256], 22, 256, dst=wdn[:, :, cbk * 256:(cbk + 1) * 256], kchunk=8)
            EX = self.capture(extra) if extra is not None else []

            def main2():
              for g in range(NG):
                aTg = self.rot('ff_aTg', [128, 22, TG], BF16, 2)
                self.ld(aTg[:], self.s_aT[:, :, g * TG:(g + 1) * TG].rearrange("j p t -> p j t"), rk=['s_aT'])
                for tt in range(4):
                    t = g * 4 + tt
                    pts = [self.ps('o'), self.ps('o')]
                    for hh in range(2):
                        for j in range(22):
                            self.mm(pts[hh][:, 0:512], aTg[:, j, tt * 128:(tt + 1) * 128], wdn[:, j, hh * 512:(hh + 1) * 512], start=(j == 0), stop=(j == 21))
                    self.resid_epilogue(pts, t, gpost, self.xres, "xres", self.xres, "xres")

            self.emit_interleaved(self.capture(main2), EX)

    def phase_ple(self, l, dst, dstkey, wg_pf=None, wp_pf=None):
        I = self.I
        with self.scope():
            self.set_pools(tr=[0, 1], mm=[2, 3, 4, 5, 6, 7])
            if wg_pf is None:
                g_pre = self.colvec(I["ln_ple"][l], 8, "g_ple")
                wg = self.em.sb([128, 8, D], BF16, "wpg")
                for cbk in range(2):
                    self.load_w(I["ple_gate"][l][:, cbk * 512:(cbk + 1) * 512], 8, 512, gcol=g_pre, dst=wg[:, :, cbk * 512:(cbk + 1) * 512], dst_key='wpg')
                wp = self.em.sb([128, 2, D], BF16, "wpp")
                self.load_w(I["ple_proj"][l], 2, D, dst=wp[:], dst_key='wpp')
            else:
                wg, wp = wg_pf, wp_pf
            HT = {}

            def stA(g):
                hTg = self.rot('pl_hT', [128, 8, TG], BF16, 2)
                self.norm_group(self.xres, "xres", g, hTg[:], kname(hTg[:]))
                HT[g] = hTg

            def stB(g):
                hTg = HT.pop(g)
                for tt in range(4):
                    t = g * 4 + tt
                    pt_ = self.rot('pl_p', [128, 256], F32, 2)
                    self.ld(pt_[:], I["p"][l][t * 128:(t + 1) * 128, :])
                    pb = self.rot('pl_pb', [128, 256], BF16, 2)
                    self.cp('pool', pb[:], pt_[:])
                    ptr = self.ps('tr')
                    ptrb = ptr.bitcast(BF16)
                    for c in range(2):
                        self.tr(ptrb[:, c * 128:(c + 1) * 128], pb[:, c * 128:(c + 1) * 128], self.identb[:])
                    pT = self.rot('pl_pT', [128, 2, 128], BF16, 2)
                    self.cp('act', pT[:], ptrb[:, 0:256].rearrange("p (c t) -> p c t", c=2))
                    xt = self.rot('pl_x', [128, D], F32, 2)
                    self.ld(xt[:], self.xres[t * 128:(t + 1) * 128, :], rk=[("xres", t)])
                    ot = self.rot('pl_o', [128, D], F32, 2)
                    for hh in range(2):
                        pg = self.ps('mm')
                        for k in range(8):
                            self.mm(pg[:, 0:512], hTg[:, k, tt * 128:(tt + 1) * 128], wg[:, k, hh * 512:(hh + 1) * 512], start=(k == 0), stop=(k == 7))
                        sg = self.rot('pl_sg', [128, 512], F32, 2)
                        self.act(sg[:], pg[:, 0:512], AF.Sigmoid)
                        pp = self.ps('mm')
                        for k in range(2):
                            self.mm(pp[:, 0:512], pT[:, k, :], wp[:, k, hh * 512:(hh + 1) * 512], start=(k == 0), stop=(k == 1))
                        self.tt('dve', sg[:], sg[:], pp[:, 0:512], ALU.mult)
                        self.tt('pool', ot[:, hh * 512:(hh + 1) * 512], sg[:], xt[:, hh * 512:(hh + 1) * 512], ALU.add)
                    self.st(dst[t * 128:(t + 1) * 128, :], ot[:], wk=[(dstkey, t)])

            self.emit_interleaved(self.capture(stA, 0), [])
            for g in range(NG):
                nxt = self.capture(stA, g + 1) if g + 1 < NG else []
                self.emit_interleaved(nxt, self.capture(stB, g))

    def phase_moba(self, l, extra=None):
        with self.scope():
            self.set_pools(sc=[0, 1, 2, 3], acc=[4, 5], ot=[6], bs=[7], tr=[7])
            em = self.em
            KQ = []
            for i_ in range(2):
                Ka = em.sb([80, S], BF16, "mb_KaugT%d" % i_)
                Qa = em.sb([80, S], BF16, "mb_QaugT%d" % i_)
                Va = em.sb([128, 32, 65], BF16, "mb_Vaug%d" % i_)
                self.memset('pool', Va[:, :, 64:65], 1.0)
                KQ.append((Ka, Qa, Va))
            with self.scope():
                ohb = em.sb([16, S], BF16, "mb_ohb")
                oh = em.sb([16, S], F32, "mb_oh")
                self.memset('pool', oh[:], 1.0)
                self.asel(oh[:], [[1, S]], ALU.is_ge, 0.0, 0, -256)
                self.asel(oh[:], [[-1, S]], ALU.is_ge, 0.0, 255, 256)
                self.cp('pool', ohb[:], oh[:])
                for i_ in range(2):
                    self.st(KQ[i_][0][64:80, :], ohb[:], q='sp')
            EX = self.capture(extra) if extra is not None else []
            tri = em.sb([128, 128], BF16, "mb_tri")
            trif = em.sb([128, 128], F32, "mb_trif")
            self.memset('pool', trif[:], 1.0)
            self.asel(trif[:], [[1, 128]], ALU.is_ge, 0.0, 0, -1)
            self.cp('pool', tri[:], trif[:])
            pastm = em.sb([128, 32, 16], F32, "mb_pastm")
            ownm = em.sb([128, 32, 16], F32, "mb_ownm")
            negm = em.sb([128, 32, 16], F32, "mb_negm")
            self.memset('pool', pastm[:], 0.0)
            self.memset('pool', ownm[:], 0.0)
            for tt in range(32):
                ob = tt // 2
                if ob > 0:
                    self.memset('pool', pastm[:, tt, 0:ob], 1.0)
                self.memset('pool', ownm[:, tt, ob:ob + 1], 1.0)
            self.ts('pool', negm[:], pastm[:], -1.0, 1e30, ALU.add, ALU.mult)
            ident65 = self.identf[0:65, 0:65]
            HS = {}

            def setupA(h):
                KaugT, QaugT, Vaug = KQ[h % 2]
                qf = self.rot('mb_qf', [64, S], F32, 1)
                kf = self.rot('mb_kf', [64, S], F32, 1)
                r0 = FM_MQK + h * 64
                self.ld(qf[:], self.s_fm[r0:r0 + 64, PAD:PAD + S])
                self.ld(kf[:], self.s_fm[r0 + 512:r0 + 576, PAD:PAD + S])
                vf = self.rot('mb_vf', [128, 32, 64], F32, 1)
                self.ld(vf[:], self.s_tm[:, h * 64:(h + 1) * 64].rearrange("(n p) d -> p n d", p=128))
                self.cp('dve', Vaug[:, :, 0:64], vf[:])
                self.cp('dve', KaugT[0:64, :], kf[:])
                self.cp('dve', QaugT[0:64, :], qf[:])
                km = self.rot('mb_km', [64, 16], F32, 2)
                self.em.op('dve', lambda e, km=km, kf=kf: e.tensor_reduce(out=km[:], in_=kf[:].rearrange("p (j s) -> p j s", s=256), axis=AX.X, op=ALU.add),
                           reads=[kname(kf[:])], writes=[kname(km[:])])
                bs = self.ps('bs')
                for tt in range(32):
                    self.mm(bs[:, tt * 16:(tt + 1) * 16], qf[:, tt * 128:(tt + 1) * 128], km[:, :])
                bsm = self.rot('mb_bsm', [128, 32, 16], F32, 1)
                self.tt('dve', bsm[:], bs[:, 0:512].rearrange("p (t j) -> p t j", j=16), pastm[:], ALU.mult)
                self.tt('dve', bsm[:], bsm[:], negm[:], ALU.add)
                m8 = self.rot('mb_m8', [128, 32, 8], F32, 1)
                for tt in range(32):
                    self.em.op('dve', lambda e, tt=tt, m8=m8, bsm=bsm: e.max(out=m8[:, tt, :], in_=bsm[:, tt, :]),
                               reads=[kname(bsm[:])], writes=[kname(m8[:])])
                sel = self.rot('mb_sel', [128, 32, 16], F32, 2)
                self.tt('dve', sel[:], bsm[:], bcast(m8[:, :, 2:3], 2, 16), ALU.is_ge)
                self.tt('dve', sel[:], sel[:], pastm[:], ALU.mult)
                self.tt('dve', sel[:], sel[:], ownm[:], ALU.add)
                self.ts('dve', sel[:], sel[:], -1.0, 30000.0, ALU.add, ALU.mult)
                HS[h] = sel

            def setupB(h):
                KaugT, QaugT, Vaug = KQ[h % 2]
                sel = HS[h]
                mbT = self.rot('mb_mbT', [16, S], BF16, 1)
                for t4 in range(8):
                    pt = self.ps('tr')
                    for q in range(4):
                        tt = t4 * 4 + q
                        self.tr(pt[0:16, q * 128:(q + 1) * 128], sel[:, tt, :], self.identf[:])
                    self.cp('dve', mbT[:, t4 * 512:(t4 + 1) * 512], pt[0:16, 0:512])
                self.st(QaugT[64:80, :], mbT[:], q='sp')

            def main(h):
                KaugT, QaugT, Vaug = KQ[h % 2]
                iters = [(tg, st_) for tg in range(8) for st_ in range(4 * (tg + 1))]
                LA = 3
                pTs = {}
                ots = {}

                def front(i):
                    tg, st_ = iters[i]
                    sl_ = st_ - 4 * tg
                    c0 = 256 if sl_ >= 2 else 0
                    sc = self.ps('sc')
                    self.mm(sc[:, c0:512], KaugT[0:80, st_ * 128:(st_ + 1) * 128], QaugT[0:80, tg * 512 + c0:(tg + 1) * 512])
                    pT = self.rot('mb_pT', [128, 512], BF16, 6)
                    self.act(pT[:, c0:512], sc[:, c0:512], AF.Exp, scale=0.125)
                    if sl_ >= 0:
                        if sl_ * 128 > c0:
                            self.memset('pool', pT[:, c0:sl_ * 128], 0.0)
                        self.tt('pool', pT[:, sl_ * 128:(sl_ + 1) * 128], pT[:, sl_ * 128:(sl_ + 1) * 128], tri[:], ALU.mult)
                    pTs[i] = (pT, c0)

                def back(i):
                    tg, st_ = iters[i]
                    nst = 4 * (tg + 1)
                    if st_ == 0:
                        ots[tg] = self.ps('acc')
                    ot = ots[tg]
                    pT, c0 = pTs.pop(i)
                    self.mm(ot[0:65, c0:512], Vaug[:, st_, :], pT[:, c0:512], start=(st_ == 0), stop=(st_ == nst - 1))
                    if st_ == nst - 1:
                        osb = self.rot('mb_osb', [65, 512], F32, 2)
                        self.cp('dve', osb[:], ot[0:65, 0:512])
                        po = self.ps('ot')
                        for qi in range(4):
                            self.tr(po[:, qi * 65:(qi + 1) * 65], osb[0:65, qi * 128:(qi + 1) * 128], ident65)
                        rd = self.rot('mb_rd', [128, 4, 1], F32, 2)
                        pov = po[:, 0:260].rearrange("p (q d) -> p q d", d=65)
                        self.em.op('dve', lambda e, rd=rd, pov=pov: e.reciprocal(out=rd[:], in_=pov[:, :, 64:65]), reads=[kname(po[:])], writes=[kname(rd[:])])
                        yo = self.rot('mb_yo', [128, 4, 64], F32, 2)
                        self.tt('dve', yo[:], pov[:, :, 0:64], bcast(rd[:], 2, 64), ALU.mult)
                        self.st(self.s_y[tg * 512:(tg + 1) * 512, 256 + h * 64:256 + (h + 1) * 64].rearrange("(q p) d -> p q d", p=128), yo[:], wk=[('s_y_b', h, tg)])

                n = len(iters)
                for i in range(min(LA, n)):
                    front(i)
                for i in range(n):
                    if i + LA < n:
                        front(i + LA)
                    back(i)

            setupA(0)
            setupB(0)
            for h in range(8):
                M_ = self.capture(main, h)
                if h + 1 < 8:
                    SA = self.capture(setupA, h + 1)
                    SB = self.capture(setupB, h + 1)
                    c1 = len(M_) // 20
                    c2 = (len(M_) * 3) // 4
                    seq = M_[:c1] + self.merge_streams(M_[c1:c2], SA) + SB + M_[c2:]
                else:
                    seq = M_
                ex_h = EX[(len(EX) * h) // 8:(len(EX) * (h + 1)) // 8]
                self.emit_interleaved(seq, ex_h)

    def phase_mlstm(self, l, scoped=True, pools=None, GW=512):
        I = self.I
        with (self.scope() if scoped else contextlib.nullcontext()):
            self.set_pools(**(pools or dict(tk=[0, 1], qk=[2, 3], acc=[4, 5], P=[6], misc=[7])))
            em = self.em
            NJ = GW // 64
            NGm = S // GW
            cw = em.sb([64, 4, 8], F32, "ml_cw")
            for j in range(4):
                self.ld(cw[:, j, :], I["mlstm_conv_w"][l][j].rearrange("(c p) -> p c", p=64), allow_slow_non_contiguous=True)
            cb = em.sb([64, 8], F32, "ml_cb")
            self.ld(cb[:], I["mlstm_conv_b"][l].rearrange("(c p) -> p c", p=64), allow_slow_non_contiguous=True)
            ib = em.sb([4, 1], F32, "ml_ib")
            fb = em.sb([4, 1], F32, "ml_fb")
            self.ld(ib[:], I["mlstm_i_b"][l].rearrange("(h o) -> h o", o=1))
            self.ld(fb[:], I["mlstm_f_b"][l].rearrange("(h o) -> h o", o=1))
            nfb = em.sb([4, 1], F32, "ml_nfb")
            self.ts('dve', nfb[:], fb[:], -1.0, None, ALU.mult)
            hng = self.rowbc(I["mlstm_hn_g"][l], 256, 64, "ml_hng")
            selT = em.sb([4, 4, 64], F32, "ml_selT")
            self.memset('pool', selT[:], 0.0)
            self.asel(selT[:], [[-1, 4], [0, 64]], ALU.not_equal, 1.0, 0, 1)
            mask4 = em.sb([64, 4, 64], F32, "ml_mask4")
            self.memset('pool', mask4[:], 1.0)
            self.asel(mask4[:], [[0, 4], [1, 64]], ALU.is_ge, 0.0, 0, -1)
            ones4 = em.sb([4, GW], F32, "ml_ones4")
            zeros4 = em.sb([4, GW], F32, "ml_zeros4")
            self.memset('pool', ones4[:], 1.0)
            self.memset('pool', zeros4[:], 0.0)
            Chat = em.sb([64, 4, 65], F32, "ml_Chat0")
            self.memset('dve', Chat[:], 0.0)
            Fprev = None
            cprev = None
            def prepare(g):
                nonlocal Fprev, cprev
                c0 = PAD + g * GW
                li = self.rot('ml_li', [4, GW], F32, 1)
                lf = self.rot('ml_lf', [4, GW], F32, 1)
                self.ld(li[:], self.s_fm[FM_LIF:FM_LIF + 4, c0:c0 + GW])
                self.ld(lf[:], self.s_fm[FM_LIF + 4:FM_LIF + 8, c0:c0 + GW])
                logi = self.rot('ml_logi', [4, GW], F32, 1)
                self.ts('dve', logi[:], li[:], ib[:, 0:1], None, ALU.add)
                e1 = self.rot('ml_e1', [4, GW], F32, 1)
                self.act(e1[:], lf[:], AF.Exp, bias=nfb[:, 0:1], scale=-1.0)
                self.act(e1[:], e1[:], AF.Ln, bias=1.0)
                logf = self.rot('ml_logf', [4, GW], F32, 1)
                self.ts('dve', logf[:], e1[:], -1.0, None, ALU.mult)
                F = self.rot('ml_F', [4, GW], F32, 2)
                self.em.op('dve', lambda e, F=F, logf=logf, init=(0.0 if Fprev is None else Fprev[:, GW - 1:GW]): e.tensor_tensor_scan(
                    out=F[:], data0=ones4[:], data1=logf[:], initial=init, op0=ALU.mult, op1=ALU.add),
                    reads=[kname(ones4[:]), kname(logf[:])] + ([] if Fprev is None else [kname(Fprev[:])]), writes=[kname(F[:])])
                G = self.rot('ml_G', [4, GW], F32, 1)
                self.tt('dve', G[:], logi[:], F[:], ALU.subtract)
                c = self.rot('ml_c', [4, GW], F32, 2)
                self.em.op('dve', lambda e, c=c, G=G, init=(0.0 if cprev is None else cprev[:, GW - 1:GW]): e.tensor_tensor_scan(
                    out=c[:], data0=zeros4[:], data1=G[:], initial=init, op0=ALU.max, op1=ALU.max),
                    reads=[kname(zeros4[:]), kname(G[:])] + ([] if cprev is None else [kname(cprev[:])]), writes=[kname(c[:])])
                cend = c[:].rearrange("p (j s) -> p j s", s=64)[:, :, 63:64]
                wrow = self.rot('ml_wrow', [4, GW], F32, 1)
                self.tt('dve', wrow[:].rearrange("p (j s) -> p j s", s=64), G[:].rearrange("p (j s) -> p j s", s=64), bcast(cend, 2, 64), ALU.subtract)
                self.act(wrow[:], wrow[:], AF.Exp)
                zrow = self.rot('ml_zrow', [4, GW], F32, 1)
                self.tt('dve', zrow[:].rearrange("p (j s) -> p j s", s=64), F[:].rearrange("p (j s) -> p j s", s=64), bcast(cend, 2, 64), ALU.add)
                self.act(zrow[:], zrow[:], AF.Exp, scale=-1.0)
                cpv = self.rot('ml_cpv', [4, NJ], F32, 1)
                if cprev is None:
                    self.memset('dve', cpv[:, 0:1], 0.0)
                else:
                    self.cp('dve', cpv[:, 0:1], cprev[:, GW - 1:GW])
                cend2 = c[:].rearrange("p (j s) -> p j s", s=64)[:, :, 63]
                self.cp('dve', cpv[:, 1:NJ], cend2[:, 0:NJ - 1], rk=[kname(c[:])])
                crow = self.rot('ml_crow', [4, NJ], F32, 1)
                self.tt('dve', crow[:], cpv[:], cend2, ALU.subtract)
                self.act(crow[:], crow[:], AF.Exp)
                pm = self.ps('misc')
                for j in range(NJ):
                    self.tr(pm[0:64, j * 4:(j + 1) * 4], wrow[0:4, j * 64:(j + 1) * 64], self.identf[0:4, 0:4])
                    self.tr(pm[0:64, 64 + j * 4:64 + (j + 1) * 4], zrow[0:4, j * 64:(j + 1) * 64], self.identf[0:4, 0:4])
                for h in range(4):
                    self.mm(pm[0:64, 128 + h * NJ:128 + (h + 1) * NJ], selT[0:4, h, :], crow[0:4, :])
                TP = self.rot('ml_TP', [64, 128 + 4 * NJ], F32, 2)
                self.cp('act', TP[:], pm[0:64, 0:128 + 4 * NJ])
                TPw = TP[:, 0:4 * NJ].rearrange("p (j h) -> p j h", h=4)
                TPz = TP[:, 64:64 + 4 * NJ].rearrange("p (j h) -> p j h", h=4)
                carry = TP[:, 128:128 + 4 * NJ].rearrange("p (h j) -> p h j", j=NJ)
                Fprev, cprev = F, c
                qk = self.rot('ml_qkraw', [64, 8, GW + 3], F32, 1)
                self.ld(qk[:], self.s_fm[FM_LQK:FM_LQK + 512, c0 - 3:c0 + GW].rearrange("(c p) n -> p c n", p=64))
                qkc = self.rot('ml_qkc', [64, 8, GW], F32, 1)
                for ch in range(8):
                    eng = 'dve'
                    self.act(qkc[:, ch, :], qk[:, ch, 3:GW + 3], AF.Identity, scale=cw[:, 3, ch:ch + 1], bias=cb[:, ch:ch + 1])
                    for j3 in range(3):
                        self.stt(eng, qkc[:, ch, :], qk[:, ch, j3:j3 + GW], cw[:, j3, ch:ch + 1], qkc[:, ch, :], ALU.mult, ALU.add)
                qkb = self.rot('ml_qkb', [64, 8, GW], BF16, 2)
                self.act(qkb[:], qkc[:], AF.Silu)
                self.act(qkb[:, 4:8, :], qkb[:, 4:8, :], AF.Copy, scale=0.125)
                vraw = self.rot('ml_vraw', [64, NJ, 256], F32, 1)
                self.ld(vraw[:], self.s_tm[g * GW:(g + 1) * GW, 512:768].rearrange("(j p) n -> p j n", p=64))
                vo = self.rot('ml_oraw', [64, NJ, 256], F32, 2)
                self.ld(vo[:], self.s_tm[g * GW:(g + 1) * GW, 768:1024].rearrange("(j p) n -> p j n", p=64))
                Vaug = self.rot('ml_Vaug', [64, NJ, 4, 65], BF16, 2)
                self.memset('pool', Vaug[:, :, :, 64:65], 1.0)
                self.cp('pool', Vaug[:, :, :, 0:64], vraw[:].rearrange("p j (h d) -> p j h d", d=64))
                CTX[g] = dict(TP=TP, TPw=TPw, carry=carry, qkb=qkb, Vaug=Vaug, vo=vo)

            def tail(g):
                nonlocal Chat
                c_ = CTX.pop(g)
                TP, TPw, carry, qkb, Vaug, vo = (c_[n_] for n_ in ('TP', 'TPw', 'carry', 'qkb', 'Vaug', 'vo'))
                accs = self.rot('ml_accs', [64, NJ, 4, 65], F32, 1)
                for j in range(NJ):
                    t0 = j * 64
                    pk = self.ps('tk')
                    pkb = pk.bitcast(BF16)
                    for h in range(4):
                        self.tr(pkb[0:64, h * 64:(h + 1) * 64], qkb[0:64, 4 + h, t0:t0 + 64], self.identb[0:64, 0:64])
                    Khat = self.rot('ml_Khat', [64, 4, 64], BF16, 2)
                    wb_ = bcast(TPw[:, j, :].unsqueeze(2), 2, 64)
                    self.tt('dve', Khat[:], pkb[0:64, 0:256].rearrange("p (h d) -> p h d", d=64), wb_, ALU.mult, rk=[kname(pk[:]), kname(TP[:])])
                    pq = self.ps('qk')
                    for h in range(4):
                        self.mm(pq[0:64, h * 64:(h + 1) * 64], qkb[0:64, 4 + h, t0:t0 + 64], qkb[0:64, h, t0:t0 + 64])
                    qkw = self.rot('ml_qkw', [64, 4, 64], BF16, 2)
                    self.tt('dve', qkw[:], pq[0:64, 0:256].rearrange("p (h d) -> p h d", d=64), wb_, ALU.mult, rk=[kname(pq[:]), kname(TP[:])])
                    self.tt('pool', qkw[:], qkw[:], mask4[:], ALU.mult)
                    Cs = self.rot('ml_Cs', [64, 4, 65], F32, 2)
                    self.tt('dve', Cs[:], Chat[:], bcast(carry[:, :, j:j + 1], 2, 65), ALU.mult, rk=[kname(Chat[:]), kname(TP[:])])
                    Csb = self.rot('ml_Csb', [64, 4, 65], BF16, 2)
                    self.cp('act', Csb[:], Cs[:])
                    pa = self.ps('acc')
                    for h in range(4):
                        self.mm(pa[0:64, h * 65:(h + 1) * 65], qkb[0:64, h, t0:t0 + 64], Csb[0:64, h, :], start=True, stop=False)
                        self.mm(pa[0:64, h * 65:(h + 1) * 65], qkw[0:64, h, :], Vaug[0:64, j, h, :], start=False, stop=True)
                    pP = self.ps('P')
                    for h in range(4):
                        self.mm(pP[0:64, h * 65:(h + 1) * 65], Khat[0:64, h, :], Vaug[0:64, j, h, :])
                    Chat = self.rot('ml_Chat', [64, 4, 65], F32, 3)
                    self.tt('dve', Chat[:], Cs[:], pP[0:64, 0:260].rearrange("p (h d) -> p h d", d=65), ALU.add)
                    self.cp('act', accs[:, j, :, :], pa[0:64, 0:260].rearrange("p (h d) -> p h d", d=65))
                NB = NJ * 4
                av = accs[:].rearrange("p j h d -> p (j h) d")
                dn = self.rot('ml_dn', [64, NB, 1], F32, 1)
                self.stt('dve', dn[:], av[:, :, 64:65], -1.0, av[:, :, 64:65], ALU.mult, ALU.max)
                self.tt('dve', dn[:], dn[:], TP[:, 64:64 + NB].unsqueeze(2), ALU.max)
                self.em.op('dve', lambda e, dn=dn: e.reciprocal(out=dn[:], in_=dn[:]), reads=[kname(dn[:])], writes=[kname(dn[:])])
                hh_ = self.rot('ml_hh', [64, NB, 64], F32, 1)
                self.tt('dve', hh_[:], av[:, :, 0:64], bcast(dn[:], 2, 64), ALU.mult)
                s1 = self.rot('ml_s1', [64, NB, 1], F32, 1)
                self.em.op('dve', lambda e, s1=s1, hh_=hh_: e.tensor_reduce(out=s1[:, :, 0], in_=hh_[:], axis=AX.X, op=ALU.add), reads=[kname(hh_[:])], writes=[kname(s1[:])])
                self.ts('dve', s1[:], s1[:], 1.0 / 64, None, ALU.mult)
                self.tt('pool', hh_[:], hh_[:], bcast(s1[:], 2, 64), ALU.subtract)
                sq = self.rot('ml_sq', [64, NB, 64], F32, 1)
                self.tt('pool', sq[:], hh_[:], hh_[:], ALU.mult)
                s2 = self.rot('ml_s2', [64, NB, 1], F32, 1)
                self.em.op('dve', lambda e, s2=s2, sq=sq: e.tensor_reduce(out=s2[:, :, 0], in_=sq[:], axis=AX.X, op=ALU.add), reads=[kname(sq[:])], writes=[kname(s2[:])])
                self.act(s2[:], s2[:], AF.Sqrt, scale=1.0 / 64, bias=1e-6)
                self.em.op('dve', lambda e, s2=s2: e.reciprocal(out=s2[:], in_=s2[:]), reads=[kname(s2[:])], writes=[kname(s2[:])])
                self.tt('dve', hh_[:], hh_[:], bcast(s2[:], 2, 64), ALU.mult)
                sgo_v = sq[:].rearrange("p (j h) d -> p j (h d)", h=4)
                self.act(sgo_v, vo[:], AF.Sigmoid)
                hv = hh_[:].rearrange("p (j h) d -> p j (h d)", h=4)
                self.tt('pool', sgo_v, sgo_v, bcast(hng[:].unsqueeze(1), 1, NJ), ALU.mult)
                self.tt('dve', sgo_v, sgo_v, hv, ALU.mult)
                self.st(self.s_y[g * GW:(g + 1) * GW, 768:1024].rearrange("(j p) n -> p j n", p=64), sgo_v, wk=[('s_y_c', g)])


            CTX = {}
            self.emit_interleaved(self.capture(prepare, 0), [])
            for g in range(NGm):
                nxt = self.capture(prepare, g + 1) if g + 1 < NGm else []
                tl = self.capture(tail, g)
                self.emit_interleaved(nxt, tl)
    def phase_rwkv(self, l, scoped=True, pools=None):
        I = self.I
        with (self.scope() if scoped else contextlib.nullcontext()):
            self.set_pools(**(pools or dict(a=[0, 1, 2, 3], c=[4, 5, 6, 7])))
            em = self.em
            GW = 128
            NJ = GW // 64
            NGR = S // GW
            NB = NJ * 4
            hp = lambda v: v.rearrange("(h p) -> p h", p=64)
            mu = I["rwkv_mu"][l]
            mu3 = em.sb([64, 3, 4], F32, "rw_mu3")
            for X in range(3):
                self.ld(mu3[:, X, :], hp(mu[X * 256:(X + 1) * 256]), allow_slow_non_contiguous=True)
            mu_w = em.sb([64, 1], F32, "rw_muw")
            mu_a = em.sb([64, 1], F32, "rw_mua")
            mu_g = em.sb([128, 1], F32, "rw_mug")
            self.ld(mu_w[:], mu[768:832].rearrange("(p o) -> p o", o=1))
            self.ld(mu_a[:], mu[832:896].rearrange("(p o) -> p o", o=1))
            self.ld(mu_g[:], mu[896:1024].rearrange("(p o) -> p o", o=1))
            def hvec(name):
                t = em.sb([64, 4], F32, "rw_" + name)
                self.ld(t[:], hp(I["rwkv_" + name][l]), allow_slow_non_contiguous=True)
                return t
            w0, a0, k_k, k_a, r_k = hvec("w0"), hvec("a0"), hvec("k_k"), hvec("k_a"), hvec("r_k")
            omka = em.sb([64, 4], F32, "rw_omka")
            self.ts('dve', omka[:], k_a[:], -1.0, 1.0, ALU.mult, ALU.add)
            w2 = em.sb([64, 256], F32, "rw_w2")
            a2 = em.sb([64, 256], F32, "rw_a2")
            g2 = em.sb([128, 256], F32, "rw_g2")
            self.ld(w2[:], I["rwkv_w2"][l])
            self.ld(a2[:], I["rwkv_a2"][l])
            self.ld(g2[:], I["rwkv_g2"][l])
            if l > 0:
                v0 = em.sb([64, 4], F32, "rw_v0")
                self.ld(v0[:], hp(I["rwkv_v0"][l - 1]), allow_slow_non_contiguous=True)
                v1 = em.sb([64, 4, 32], F32, "rw_v1")
                self.ld(v1[:], I["rwkv_v1"][l - 1].rearrange("(h p) r -> p h r", p=64))
                v2 = em.sb([32, 256], F32, "rw_v2")
                self.ld(v2[:], I["rwkv_v2"][l - 1])
            gng = self.rowbc(I["rwkv_gn_g"][l], 256, 64, "rw_gng")
            gnb = self.rowbc(I["rwkv_gn_b"][l], 256, 64, "rw_gnb")
            sl4 = em.sb([64, 4, 64], F32, "rw_sl4")
            su4 = em.sb([64, 4, 64], F32, "rw_su4")
            sui4 = em.sb([64, 4, 64], F32, "rw_sui4")
            for t_, pat, base, cm in ((sl4, [[0, 4], [-1, 64]], -1, 1), (su4, [[0, 4], [1, 64]], -1, -1), (sui4, [[0, 4], [1, 64]], 0, -1)):
                self.memset('pool', t_[:], 1.0)
                self.asel(t_[:], pat, ALU.is_ge, 0.0, base, cm)
            segm = em.sb([64, 4 * GW], F32, "rw_segm")
            self.memset('pool', segm[:], 1.0)
            self.memset('pool', segm[:].rearrange("p (n s) -> p n s", s=64)[:, :, 0:1], 0.0)
            ones64 = em.sb([64, 64], F32, "rw_ones64")
            self.memset('pool', ones64[:], 1.0)
            id64 = self.identf[0:64, 0:64]
            id64b = self.identb[0:64, 0:64]
            I8 = em.sb([64, NB, 64], F32, "rw_I8")
            for b_ in range(NB):
                self.cp('pool', I8[:, b_, :], id64)
            ST = em.sb([64, 4, 64], F32, "rw_ST0")
            self.memset('dve', ST[:], 0.0)
            STb = em.sb([64, 4, 64], BF16, "rw_STb0")
            self.memset('dve', STb[:], 0.0)
            bc3 = lambda t_: bcast(t_[:].unsqueeze(2), 2, GW)
            NEG = -0.6065306597126334
            def prepare(g):
                c0 = PAD + g * GW
                tok0 = g * GW
                raw = self.rot('rw_raw', [64, 3, 4, GW + 1], F32, 1)
                for X in range(3):
                    self.ld(raw[:, X, :, :], self.s_fm[X * 256:(X + 1) * 256, c0 - 1:c0 + GW].rearrange("(h p) n -> p h n", p=64))
                rwa = self.rot('rw_rwa', [64, 2, GW + 1], F32, 1)
                self.ld(rwa[:], self.s_fm[768:896, c0 - 1:c0 + GW].rearrange("(x p) n -> p x n", p=64))
                rg = self.rot('rw_rg', [128, GW + 1], F32, 1)
                self.ld(rg[:], self.s_fm[896:1024, c0 - 1:c0 + GW])
                L3 = self.rot('rw_L3', [64, 3, 4, GW], F32, 1)
                for X in range(3):
                    d = self.rot('rw_d', [64, 4, GW], F32, 1)
                    self.tt('dve', d[:], raw[:, X, :, 0:GW], raw[:, X, :, 1:GW + 1], ALU.subtract)
                    self.tt('pool', d[:], d[:], bc3(mu3[:, X, :]), ALU.mult)
                    self.tt('dve', L3[:, X, :, :], d[:], raw[:, X, :, 1:GW + 1], ALU.add)
                r_, k_, v_ = L3[:, 0, :, :], L3[:, 1, :, :], L3[:, 2, :, :]
                xwa = self.rot('rw_xwa', [64, 2, GW], F32, 1)
                for X, m_ in ((0, mu_w), (1, mu_a)):
                    d = self.rot('rw_d1', [64, GW], F32, 2)
                    self.tt('dve', d[:], rwa[:, X, 0:GW], rwa[:, X, 1:GW + 1], ALU.subtract)
                    self.stt('dve', xwa[:, X, :], d[:], m_[:, 0:1], rwa[:, X, 1:GW + 1], ALU.mult, ALU.add)
                xg = self.rot('rw_xg', [128, GW], F32, 1)
                dg = self.rot('rw_dg', [128, GW], F32, 1)
                self.tt('dve', dg[:], rg[:, 0:GW], rg[:, 1:GW + 1], ALU.subtract)
                self.stt('dve', xg[:], dg[:], mu_g[:, 0:1], rg[:, 1:GW + 1], ALU.mult, ALU.add)
                self.act(xwa[:, 0, :], xwa[:, 0, :], AF.Tanh)
                self.act(xg[:], xg[:], AF.Sigmoid)
                lw = self.rot('rw_lw', [64, 4, GW], F32, 1)
                a_ = self.rot('rw_a', [64, 4, GW], F32, 1)
                gT = self.rot('rw_gT', [64, 4, GW], F32, 1)
                for h in range(4):
                    p1 = self.ps('a')
                    self.mm(p1[0:64, 0:GW], w2[:, h * 64:(h + 1) * 64], xwa[:, 0, :])
                    self.act(lw[:, h, :], p1[0:64, 0:GW], AF.Sigmoid, bias=w0[:, h:h + 1])
                    p2 = self.ps('a')
                    self.mm(p2[0:64, 0:GW], a2[:, h * 64:(h + 1) * 64], xwa[:, 1, :])
                    self.act(a_[:, h, :], p2[0:64, 0:GW], AF.Sigmoid, bias=a0[:, h:h + 1])
                    p3 = self.ps('a')
                    self.mm(p3[0:64, 0:GW], g2[:, h * 64:(h + 1) * 64], xg[:, :])
                    self.cp('act', gT[:, h, :], p3[0:64, 0:GW])
                if l > 0:
                    p4 = self.ps('a')
                    for h in range(4):
                        self.mm(p4[0:32, 0:GW], v1[:, h, :], L3[:, 2, h, :], start=(h == 0), stop=(h == 3))
                    t1 = self.rot('rw_t1', [32, GW], F32, 1)
                    self.cp('act', t1[:], p4[0:32, 0:GW])
                    sgv = self.rot('rw_sgv', [64, 4, GW], F32, 1)
                    for h in range(4):
                        p5 = self.ps('a')
                        self.mm(p5[0:64, 0:GW], v2[:, h * 64:(h + 1) * 64], t1[:, :])
                        self.act(sgv[:, h, :], p5[0:64, 0:GW], AF.Sigmoid, bias=v0[:, h:h + 1])
                    vf = self.rot('rw_vf', [64, 4, GW], F32, 1)
                    self.ld(vf[:], self.vfirst[:, tok0:tok0 + GW].rearrange("(h p) n -> p h n", p=64))
                    self.tt('dve', vf[:], vf[:], v_, ALU.subtract)
                    self.tt('pool', vf[:], vf[:], sgv[:], ALU.mult)
                    self.tt('dve', v_, v_, vf[:], ALU.add)
                else:
                    self.st(self.vfirst[:, tok0:tok0 + GW].rearrange("(h p) n -> p h n", p=64), v_, wk=[('vfirst', g)])
                kk = self.rot('rw_kk', [64, 4, GW], F32, 1)
                self.tt('dve', kk[:], k_, bc3(k_k), ALU.mult)
                sq = self.rot('rw_sq', [64, 4, GW], F32, 1)
                self.tt('pool', sq[:], kk[:], kk[:], ALU.mult)
                nr = self.rot('rw_nr', [64, 4, GW], F32, 1)
                for h in range(4):
                    p6 = self.ps('a')
                    self.mm(p6[0:64, 0:GW], ones64[:, :], sq[:, h, :])
                    self.act(nr[:, h, :], p6[0:64, 0:GW], AF.Ln, bias=1e-30)
                self.act(nr[:], nr[:], AF.Exp, scale=-0.5)
                self.tt('dve', kk[:], kk[:], nr[:], ALU.mult)
                k2 = self.rot('rw_k2', [64, 4, GW], F32, 1)
                self.tt('pool', k2[:], a_[:], bc3(k_a), ALU.mult)
                self.tt('pool', k2[:], k2[:], bc3(omka), ALU.add)
                self.tt('dve', k2[:], k2[:], k_, ALU.mult)
                b_ = self.rot('rw_b', [64, 4, GW], F32, 1)
                self.tt('pool', b_[:], kk[:], a_[:], ALU.mult)
                cw = self.rot('rw_cw', [64, 4, GW], F32, 1)
                self.em.op('dve', lambda e, cw=cw, lw=lw: e.tensor_tensor_scan(out=cw[:].rearrange("p h n -> p (h n)"), data0=segm[:], data1=lw[:].rearrange("p h n -> p (h n)"),
                                                                               initial=0.0, op0=ALU.mult, op1=ALU.add),
                           reads=[kname(segm[:]), kname(lw[:])], writes=[kname(cw[:])])
                ep = self.rot('rw_ep', [64, 4, GW], F32, 2)
                en = self.rot('rw_en', [64, 4, GW], F32, 1)
                epv = self.rot('rw_epv', [64, 4, GW], F32, 1)
                self.act(ep[:], cw[:], AF.Exp, scale=NEG)
                self.act(en[:], cw[:], AF.Exp, scale=-NEG)
                self.tt('dve', epv[:], cw[:], lw[:], ALU.subtract)
                self.act(epv[:], epv[:], AF.Exp, scale=NEG)
                AR = self.rot('rw_AR', [64, 2, 4, GW], BF16, 2)
                at = AR[:, 0, :, :]
                rt = AR[:, 1, :, :]
                kt = self.rot('rw_kt', [64, 4, GW], BF16, 1)
                bt = self.rot('rw_bt', [64, 4, GW], BF16, 1)
                self.tt('dve', rt[:], r_, ep[:], ALU.mult)
                self.tt('pool', kt[:], k2[:], en[:], ALU.mult)
                self.tt('dve', bt[:], b_[:], en[:], ALU.mult)
                self.stt('dve', at[:], kk[:], -1.0, epv[:], ALU.mult, ALU.mult)
                rk = self.rot('rw_rk', [64, 4, GW], F32, 1)
                self.tt('pool', rk[:], r_, k2[:], ALU.mult)
                self.tt('pool', rk[:], rk[:], bc3(r_k), ALU.mult)
                pb = self.ps('a')
                for j in range(NJ):
                    for h in range(4):
                        self.mm(pb[0:64, j * 4 + h:j * 4 + h + 1], rk[:, h, j * 64:(j + 1) * 64], ones64[:, 0:1])
                bon = self.rot('rw_bon', [64, NB, 1], F32, 2)
                self.cp('act', bon[:, :, 0], pb[0:64, 0:NB])
                TM = {}
                for nm, src_ in (('B', bt[:]), ('K', kt[:]), ('V', v_), ('G', gT[:])):
                    isb = nm in ('B', 'K')
                    dst_ = self.rot('rw_tm' + nm, [64, NJ, 4, 64], BF16 if isb else F32, 2)
                    for j in range(NJ):
                        pt = self.ps('a')
                        ptv = pt.bitcast(BF16) if isb else pt
                        for h in range(4):
                            self.tr(ptv[0:64, h * 64:(h + 1) * 64], src_[:, h, j * 64:(j + 1) * 64], id64b if isb else id64)
                        self.cp('act' if j % 2 == 0 else 'dve', dst_[:, j, :, :], ptv[0:64, 0:256].rearrange("p (h d) -> p h d", d=64))
                    TM[nm] = dst_
                Vb = self.rot('rw_Vb', [64, NJ, 4, 64], BF16, 2)
                self.cp('act', Vb[:], TM['V'][:])
                CM = {}
                for nm in ('A', 'Bm', 'Ak', 'Rb', 'Rk'):
                    CM[nm] = self.rot('rw_cm' + nm, [64, NJ, 4, 64], BF16, 2)
                for j in range(NJ):
                    cs = slice(j * 64, (j + 1) * 64)
                    pt = self.ps('a')
                    for h in range(4):
                        self.mm(pt[0:64, h * 64:(h + 1) * 64], at[:, h, cs], bt[:, h, cs])
                    self.tt('dve', CM['A'][:, j, :, :], pt[0:64, 0:256].rearrange("p (h d) -> p h d", d=64), sl4[:], ALU.mult)
                    for lh, n0, n1 in ((bt, 'Bm', 'Rb'), (kt, 'Ak', 'Rk')):
                        pt = self.ps('a')
                        for h in range(4):
                            self.mm(pt[0:64, h * 128:(h + 1) * 128], lh[:, h, cs], AR[:, :, h, cs])
                        pv = pt[0:64, 0:512].rearrange("p (h x d) -> p h x d", x=2, d=64)
                        self.tt('dve', CM[n0][:, j, :, :], pv[:, :, 0, :], su4[:], ALU.mult)
                        self.tt('pool' if False else 'dve', CM[n1][:, j, :, :], pv[:, :, 1, :], sui4[:], ALU.mult)
                P = CM['Bm'][:].rearrange("p j h d -> p (j h) d")
                Q = CM['A'][:].rearrange("p j h d -> p (j h) d")
                M = self.rot('rw_M', [64, NB, 64], F32, 2)
                self.tt('pool', M[:], I8[:], P, ALU.add)
                M = M[:]
                Mb = self.rot('rw_Mb', [64, NB, 64], BF16, 2)
                self.cp('act', Mb[:], M)
                Mb = Mb[:]
                for lev in range(5):
                    last = (lev == 4)
                    pQ = self.ps('a')
                    for b in range(NB):
                        self.mm(pQ[0:64, b * 64:(b + 1) * 64], P[:, b, :], Q[:, b, :])
                    if not last:
                        pP = self.ps('a')
                        for b in range(NB):
                            self.mm(pP[0:64, b * 64:(b + 1) * 64], Q[:, b, :], P[:, b, :])
                    Qn = self.rot('rw_Qn', [64, NB, 64], BF16, 2)
                    self.cp('act', Qn[:], pQ[0:64, 0:NB * 64].rearrange("p (b d) -> p b d", d=64))
                    if not last:
                        Pn = self.rot('rw_Pn', [64, NB, 64], BF16, 2)
                        self.cp('act', Pn[:], pP[0:64, 0:NB * 64].rearrange("p (b d) -> p b d", d=64))
                        P = Pn[:]
                    Q = Qn[:]
                    pM = self.ps('a')
                    for b in range(NB):
                        self.mm(pM[0:64, b * 64:(b + 1) * 64], Q[:, b, :], Mb[:, b, :])
                    Mn = self.rot('rw_M', [64, NB, 64], F32, 2)
                    self.tt('dve', Mn[:], M, pM[0:64, 0:NB * 64].rearrange("p (b d) -> p b d", d=64), ALU.add)
                    M = Mn[:]
                    Mbn = self.rot('rw_Mb', [64, NB, 64], BF16, 2)
                    self.cp('act', Mbn[:], M)
                    Mb = Mbn[:]
                TTt = self.rot('rw_TT', [64, NB, 64], BF16, 2)
                self.cp('act', TTt[:], M)
                TT = TTt[:]
                CTX[g] = dict(CM=CM, TM=TM, Vb=Vb, at=at, rt=rt, TT=TT, ep=ep, bon=bon, tok0=tok0)

            def tail(g):
                nonlocal ST, STb
                c_ = CTX.pop(g)
                CM, TM, Vb, at, rt, TT, ep, bon, tok0 = (c_[n_] for n_ in ('CM', 'TM', 'Vb', 'at', 'rt', 'TT', 'ep', 'bon', 'tok0'))
                Ysb = self.rot('rw_Ysb', [64, NJ, 4, 64], F32, 1)
                V_, B_, K_ = Vb, TM['B'], TM['K']
                Vf_ = TM['V']
                for j in range(NJ):
                    cs = slice(j * 64, (j + 1) * 64)
                    pX = self.ps('c')
                    for h in range(4):
                        self.mm(pX[0:64, h * 64:(h + 1) * 64], CM['Ak'][:, j, h, :], V_[:, j, h, :], start=True, stop=False)
                        self.mm(pX[0:64, h * 64:(h + 1) * 64], at[:, h, cs], STb[:, h, :], start=False, stop=True)
                    Xsb = self.rot('rw_Xsb', [64, 4, 64], BF16, 2)
                    self.cp('act', Xsb[:], pX[0:64, 0:256].rearrange("p (h d) -> p h d", d=64))
                    pU = self.ps('c')
                    for h in range(4):
                        self.mm(pU[0:64, h * 64:(h + 1) * 64], TT[:, j * 4 + h, :], Xsb[:, h, :])
                    Usb = self.rot('rw_Usb', [64, 4, 64], BF16, 2)
                    self.cp('act', Usb[:], pU[0:64, 0:256].rearrange("p (h d) -> p h d", d=64))
                    pS = self.ps('c')
                    for h in range(4):
                        self.mm(pS[0:64, h * 64:(h + 1) * 64], B_[:, j, h, :], Usb[:, h, :], start=True, stop=False)
                        self.mm(pS[0:64, h * 64:(h + 1) * 64], K_[:, j, h, :], V_[:, j, h, :], start=False, stop=True)
                    pY = self.ps('c')
                    for h in range(4):
                        self.mm(pY[0:64, h * 64:(h + 1) * 64], rt[:, h, cs], STb[:, h, :], start=True, stop=False)
                        self.mm(pY[0:64, h * 64:(h + 1) * 64], CM['Rb'][:, j, h, :], Usb[:, h, :], start=False, stop=False)
                        self.mm(pY[0:64, h * 64:(h + 1) * 64], CM['Rk'][:, j, h, :], V_[:, j, h, :], start=False, stop=True)
                    STn = self.rot('rw_ST', [64, 4, 64], F32, 3)
                    self.tt('dve', STn[:], pS[0:64, 0:256].rearrange("p (h d) -> p h d", d=64), ST[:], ALU.add)
                    self.tt('dve', STn[:], STn[:], bcast(ep[:, :, j * 64 + 63:j * 64 + 64], 2, 64), ALU.mult)
                    ST = STn
                    STb = self.rot('rw_STb', [64, 4, 64], BF16, 3)
                    self.cp('act', STb[:], ST[:])
                    self.cp('act', Ysb[:, j, :, :], pY[0:64, 0:256].rearrange("p (h d) -> p h d", d=64))
                yv = Ysb[:].rearrange("p j h d -> p (j h) d")
                s1 = self.rot('rw_s1', [64, NB, 1], F32, 1)
                self.em.op('dve', lambda e, s1=s1, yv=yv: e.tensor_reduce(out=s1[:, :, 0], in_=yv, axis=AX.X, op=ALU.add), reads=[kname(Ysb[:])], writes=[kname(s1[:])])
                self.ts('dve', s1[:], s1[:], 1.0 / 64, None, ALU.mult)
                yc = self.rot('rw_yc', [64, NB, 64], F32, 1)
                self.tt('dve', yc[:], yv, bcast(s1[:], 2, 64), ALU.subtract)
                sq2 = self.rot('rw_sq2', [64, NB, 64], F32, 1)
                self.tt('pool', sq2[:], yc[:], yc[:], ALU.mult)
                s2 = self.rot('rw_s2', [64, NB, 1], F32, 1)
                self.em.op('dve', lambda e, s2=s2, sq2=sq2: e.tensor_reduce(out=s2[:, :, 0], in_=sq2[:], axis=AX.X, op=ALU.add), reads=[kname(sq2[:])], writes=[kname(s2[:])])
                self.act(s2[:], s2[:], AF.Sqrt, scale=1.0 / 64, bias=64e-5)
                self.em.op('dve', lambda e, s2=s2: e.reciprocal(out=s2[:], in_=s2[:]), reads=[kname(s2[:])], writes=[kname(s2[:])])
                self.tt('dve', yc[:], yc[:], bcast(s2[:], 2, 64), ALU.mult)
                y4 = yc[:].rearrange("p (j h) d -> p j (h d)", h=4)
                self.tt('pool', y4, y4, bcast(gng[:].unsqueeze(1), 1, NJ), ALU.mult)
                self.tt('pool', y4, y4, bcast(gnb[:].unsqueeze(1), 1, NJ), ALU.add)
                bv = self.rot('rw_bv', [64, NB, 64], F32, 1)
                self.tt('dve', bv[:], Vf_[:].rearrange("p j h d -> p (j h) d"), bcast(bon[:], 2, 64), ALU.mult)
                self.tt('dve', yc[:], yc[:], bv[:], ALU.add)
                self.tt('pool', yc[:], yc[:], TM['G'][:].rearrange("p j h d -> p (j h) d"), ALU.mult)
                self.st(self.s_y[tok0:tok0 + GW, 0:256].rearrange("(j p) n -> p j n", p=64), y4, rk=[kname(yc[:])], wk=[('s_y_a', g)])

            CTX = {}
            cur = self.capture(prepare, 0)
            self.emit_interleaved(cur, [])
            for g in range(NGR):
                nxt = self.capture(prepare, g + 1) if g + 1 < NGR else []
                tl = self.capture(tail, g)
                self.emit_interleaved(nxt, tl, self.rw_bfrac)

    def phase_rwkv_mlstm(self, l):
        with self.scope():
            A = self.capture(self.phase_rwkv, l, False, dict(a=[0, 1, 2], c=[3, 4]))
            B = self.capture(self.phase_mlstm, l, False, dict(misc=[5], tk=[6], acc=[6], qk=[7], P=[7]), 256)
            self.emit_interleaved(A, B, self.rm_bfrac)

    def phase_moba_merge(self, l):
        I = self.I
        with self.scope():
            wgt = self.em.sb([128, 8, 3072], BF16, "wgt_pf")
            wbr = self.em.sb([128, 8, D], BF16, "wbr_pf")
            wout = self.em.sb([128, 8, D], BF16, "wout_pf")
            g_pre = self.colvec(I["ln_mix_pre"][l], 8, "g_pre_pf")

            def loader():
                for cb in range(6):
                    self.load_w(I["w_in"][l][:, GT_OFF + cb * 512:GT_OFF + (cb + 1) * 512], 8, 512, gcol=g_pre,
                                dst=wgt[:, :, cb * 512:(cb + 1) * 512], kchunk=2, nbuf=2, engs=('dve',), tag='pf')
                for (nm, k0, kn) in (("w_br_rwkv", 0, 2), ("w_br_moba", 2, 4), ("w_br_mlstm", 6, 2)):
                    for cb in range(2):
                        self.load_w(I[nm][l][:, cb * 512:(cb + 1) * 512], kn, 512, dst=wbr[:, k0:k0 + kn, cb * 512:(cb + 1) * 512],
                                    kchunk=2, nbuf=2, engs=('dve',), tag='pf')
                for cb in range(2):
                    self.load_w(I["w_out"][l][:, cb * 512:(cb + 1) * 512], 8, 512, dst=wout[:, :, cb * 512:(cb + 1) * 512],
                                kchunk=2, nbuf=2, engs=('dve',), tag='pf')

            self.phase_moba(l, extra=loader)
            self.phase_merge(l, wgt_pf=wgt, wbr_pf=wbr, wout_pf=wout)

    def phase_ffn_ple(self, l, dst, dstkey):
        I = self.I
        self.phase_ffn(l, down=False)
        with self.scope():
            wg = self.em.sb([128, 8, D], BF16, "wpg_pf")
            wp = self.em.sb([128, 2, D], BF16, "wpp_pf")
            g_ple = self.colvec(I["ln_ple"][l], 8, "g_ple_pf")

            def loader():
                for cbk in range(2):
                    self.load_w(I["ple_gate"][l][:, cbk * 512:(cbk + 1) * 512], 8, 512, gcol=g_ple, dst=wg[:, :, cbk * 512:(cbk + 1) * 512],
                                kchunk=2, nbuf=2, engs=('act', 'dve'), tag='pf2')
                for cbk in range(2):
                    self.load_w(I["ple_proj"][l][:, cbk * 512:(cbk + 1) * 512], 2, 512, dst=wp[:, :, cbk * 512:(cbk + 1) * 512],
                                kchunk=2, nbuf=2, engs=('act', 'dve'), tag='pf2')

            self.phase_ffn_down(l, extra=loader)
            self.phase_ple(l, dst, dstkey, wg_pf=wg, wp_pf=wp)


from concourse.bass_utils import run_bass_kernel_spmd


def build_program():
    kb = KB()
    for l in range(2):
        kb.phase_in(l)
        kb.phase_rwkv_mlstm(l)
        kb.phase_moba_merge(l)
        if l == 1:
            kb.phase_ffn_ple(l, kb.out, "out")
        else:
            kb.phase_ffn_ple(l, kb.xres, "xres")
    kb.em.build()
    return kb


def kernel(**inputs):
    kb = build_program()
    in_maps = []
    for b in range(8):
        m = {}
        for name, shape in IN_SPECS:
            a = np.asarray(inputs[name], dtype=np.float32)
            if name == "x":
                a = a[b]
            elif name == "p":
                a = a[:, b]
            m[name] = np.ascontiguousarray(a)
        in_maps.append(m)
    res = run_bass_kernel_spmd(kb.nc, in_maps, core_ids=list(range(8)))
    return np.stack([np.asarray(r["out"], dtype=np.float32) for r in res.results], axis=0)
```

```python
import contextlib
import numpy as np
import concourse.bass as bass
import concourse.mybir as mybir

F32 = mybir.dt.float32
BF16 = mybir.dt.bfloat16
AF = mybir.ActivationFunctionType
ALU = mybir.AluOpType
AX = mybir.AxisListType

SAME_ENGINE_RAW = True


class Em:
    def __init__(self, nc, ndma=8):
        self.nc = nc
        self.engs = ['pe', 'act', 'dve', 'pool', 'sp']
        self.prog = {e: [] for e in self.engs}
        self.cnt = {e: 0 for e in self.engs}
        self.seen = {e: {} for e in self.engs}
        self.res = {}
        self.ndma = ndma
        self.slotval = {}
        self.dnext = {e: 0 for e in self.engs}
        self.stack = contextlib.ExitStack()
        self.nalloc = 0
        self.psum_banks = []
        self.psum_next = 0
        self.epoch = 0

    def sb(self, shape, dtype=F32, name=None):
        self.nalloc += 1
        name = f"{name or 'sb'}_{self.nalloc}"
        return self.stack.enter_context(self.nc.sbuf_tensor(name, list(shape), dtype))

    def init_psum(self):
        for i in range(8):
            t = self.stack.enter_context(self.nc.psum_tensor(f"psb{i}", [128, 512], F32))
            self.psum_banks.append(t)

    def psum(self):
        i = self.psum_next
        self.psum_next = (i + 1) % 8
        return self.psum_banks[i], ('ps', i)

    def _wait(self, eng, tok, kind):
        if tok is None:
            return
        sk, val = tok
        if sk[0] == 'e' and sk[1] == eng:
            if eng in ('pe', 'sp'):
                return
            if not SAME_ENGINE_RAW:
                return
        if self.seen[eng].get(sk, 0) >= val:
            return
        self.seen[eng][sk] = val
        self.prog[eng].append(('wait', sk, val))

    def _deps(self, eng, reads, writes):
        for r in reads:
            e = self.res.get(r)
            if e is not None:
                self._wait(eng, e[0], 'raw')
        for w in writes:
            e = self.res.get(w)
            if e is not None:
                self._wait(eng, e[0], 'waw')
                for sk, val in e[1].items():
                    self._wait(eng, (sk, val), 'war')

    def _mark(self, tok, reads, writes):
        sk, val = tok
        for r in reads:
            e = self.res.setdefault(r, [None, {}])
            e[1][sk] = max(e[1].get(sk, 0), val)
        for w in writes:
            self.res[w] = [tok, {}]

    def op(self, eng, fn, reads=(), writes=(), inc=True):
        self._deps(eng, reads, writes)
        tok = (('e', eng, self.epoch), self.cnt[eng] + 1)
        if inc:
            self.cnt[eng] += 1
        self.prog[eng].append(('op', fn, inc, ('e', eng, self.epoch)))
        self._mark(tok, reads, writes)

    def dma(self, issuer, out, in_, reads=(), writes=(), **kw):
        self._deps(issuer, reads, writes)
        slot = self.dnext[issuer]
        self.dnext[issuer] = (slot + 1) % self.ndma
        sk = ('dma', issuer, slot)
        cur = self.slotval.get(sk, 0)
        if cur:
            self._wait(issuer, (sk, cur), 'raw')
        self.slotval[sk] = cur + 16
        tok = (sk, cur + 16)
        self.prog[issuer].append(('dma', out, in_, sk, kw))
        self._mark(tok, reads, writes)

    def barrier(self):
        toks = [(sk, v) for sk, v in self.slotval.items()]
        toks += [(('e', e, self.epoch), self.cnt[e]) for e in self.engs if self.cnt[e]]
        for e in self.engs:
            for tok in toks:
                if not (tok[0][0] == 'e' and tok[0][1] == e):
                    self._wait(e, tok, 'raw')
        self.res = {}
        self.epoch += 1
        self.cnt = {e: 0 for e in self.engs}

    def build(self):
        nc = self.nc
        for sk, v in self.slotval.items():
            self._wait('sp', (sk, v), 'raw')
        for e in self.engs:
            if e != 'sp' and self.cnt[e]:
                self._wait('sp', (('e', e, self.epoch), self.cnt[e]), 'raw')
        semkeys = set()
        for e in self.engs:
            for it in self.prog[e]:
                if it[0] == 'wait':
                    semkeys.add(it[1])
                elif it[0] == 'dma':
                    semkeys.add(it[3])
                elif it[0] == 'op' and it[2]:
                    semkeys.add(it[3])
        sems = {}
        for i, sk in enumerate(sorted(semkeys, key=str)):
            nm = "s_" + ("_".join(str(x) for x in sk) if isinstance(sk, tuple) else sk)
            sems[sk] = self.stack.enter_context(nc.semaphore(nm))
        prog = self.prog

        def run(eng_name):
            def body(e):
                for it in prog[eng_name]:
                    if it[0] == 'wait':
                        e.wait_ge(sems[it[1]], it[2])
                    elif it[0] == 'op':
                        ins = it[1](e)
                        if it[2]:
                            ins.then_inc(sems[it[3]], 1)
                    else:
                        e.dma_start(out=it[1], in_=it[2], **it[4]).then_inc(sems[it[3]], 16)
            return body

        with nc.Block() as block:
            if prog['sp']:
                block.sync(run('sp'))
            if prog['pe']:
                block.tensor(run('pe'))
            if prog['act']:
                block.scalar(run('act'))
            if prog['dve']:
                block.vector(run('dve'))
            if prog['pool']:
                block.gpsimd(run('pool'))
        self.stack.close()
        n = {e: len(prog[e]) for e in self.engs}
        return n


S = 4096
D = 1024
NTILE = 32
NG = 8
TG = 512
PAD = 4
DFF = 2816
RW_OFF, MQ_OFF, MK_OFF, MV_OFF = 0, 1024, 1536, 2048
LQ_OFF, LK_OFF, LV_OFF, LO_OFF, LI_OFF, LF_OFF, GT_OFF = 2560, 2816, 3072, 3328, 3584, 3588, 3592
INW = 6664
FM_RW, FM_MQK, FM_LQK, FM_LIF = 0, 1024, 2048, 2560
NFM = 2568

IN_SPECS = [
    ("x", [S, D]), ("p", [2, S, 256]),
    ("ln_mix_pre", [2, D]), ("ln_mix_post", [2, D]), ("ln_ffn_pre", [2, D]), ("ln_ffn_post", [2, D]), ("ln_ple", [2, D]),
    ("w_in", [2, D, INW]), ("rwkv_mu", [2, 1024]), ("rwkv_w0", [2, 256]), ("rwkv_w2", [2, 64, 256]), ("rwkv_a0", [2, 256]),
    ("rwkv_a2", [2, 64, 256]), ("rwkv_g2", [2, 128, 256]), ("rwkv_k_k", [2, 256]), ("rwkv_k_a", [2, 256]), ("rwkv_r_k", [2, 256]),
    ("rwkv_gn_g", [2, 256]), ("rwkv_gn_b", [2, 256]), ("rwkv_v0", [1, 256]), ("rwkv_v1", [1, 256, 32]), ("rwkv_v2", [1, 32, 256]),
    ("mlstm_conv_w", [2, 4, 512]), ("mlstm_conv_b", [2, 512]), ("mlstm_i_b", [2, 4]), ("mlstm_f_b", [2, 4]), ("mlstm_hn_g", [2, 256]),
    ("w_br_rwkv", [2, 256, D]), ("w_br_moba", [2, 512, D]), ("w_br_mlstm", [2, 256, D]), ("w_out", [2, D, D]),
    ("ffn_up", [2, D, 2 * DFF]), ("ffn_conv_w", [2, 3, 2 * DFF]), ("ffn_conv_b", [2, 2 * DFF]), ("ffn_down", [2, DFF, D]),
    ("ple_proj", [2, 256, D]), ("ple_gate", [2, D, D]),
]


def bcast(ap, dim, n):
    l = [list(a) for a in ap.ap]
    l[dim] = [0, n]
    return bass.AP(ap.tensor, ap.offset, l)


def kname(ap):
    return ap.tensor.name


class KB:
    def __init__(self, dbg=None, ext_in=(), ext_out=()):
        self.nc = nc = bass.Bass("TRN2", target_bir_lowering=False)
        self.em = Em(nc)
        self.em.init_psum()
        self.I = {}
        for name, shape in IN_SPECS:
            self.I[name] = nc.dram_tensor(name, shape, F32, kind="ExternalInput").ap()
        self.out = nc.dram_tensor("out", [S, D], F32, kind="ExternalOutput").ap()
        self.dbg = dbg or {}
        self.dbg_out = {}
        mk = lambda n, sh, dt=F32: nc.dram_tensor(n, sh, dt, kind=("ExternalInput" if n in ext_in else "ExternalOutput" if n in ext_out else "Internal")).ap()
        self.xres = mk("xres", [S, D])
        self.s_fm = mk("s_fm", [NFM, PAD + S])
        self.s_tm = mk("s_tm", [S, 1024])
        self.s_y = mk("s_y", [S, 1024])
        self.s_hT = mk("s_hT", [128, 8, S], BF16)
        self.vfirst = mk("vfirst", [256, S])
        self.rot_cache = {}
        self.pp = {}
        self.consts()

    def keys(self, aps, override):
        if override is not None:
            return list(override)
        ks = []
        for a in aps:
            if a is None or isinstance(a, (int, float)):
                continue
            k = kname(a)
            if k not in ks:
                ks.append(k)
        return ks

    def rot(self, name, shape, dtype, n):
        key = (name, self.scope_id)
        if key not in self.rot_cache:
            self.rot_cache[key] = [[self.em.sb(shape, dtype, name=f"{name}_{self.scope_id}_{i}") for i in range(n)], 0]
        ent = self.rot_cache[key]
        t = ent[0][ent[1]]
        ent[1] = (ent[1] + 1) % n
        return t

    scope_id = 0
    scope_ctr = 0

    @contextlib.contextmanager
    def scope(self):
        em = self.em
        old = em.stack
        em.stack = contextlib.ExitStack()
        old_id = self.scope_id
        KB.scope_ctr += 1
        self.scope_id = KB.scope_ctr
        try:
            yield
        finally:
            em.barrier()
            em.stack.close()
            em.stack = old
            self.scope_id = old_id

    def ps(self, pool='g'):
        banks = self.pp.setdefault(pool, {'g': [0, 1, 2, 3, 4, 5, 6, 7]}.get(pool))
        st = self.pp.setdefault(pool + '_i', [0])
        b = banks[st[0] % len(banks)]
        st[0] += 1
        return self.em.psum_banks[b]

    def set_pools(self, **pools):
        for k, v in pools.items():
            self.pp[k] = v
            self.pp[k + '_i'] = [0]

    def tt(self, eng, out, in0, in1, op, rk=None, wk=None):
        self.em.op(eng, lambda e: e.tensor_tensor(out=out, in0=in0, in1=in1, op=op), reads=self.keys([in0, in1], rk), writes=self.keys([out], wk))

    def ts(self, eng, out, in0, s1, s2, op0, op1=None, rk=None, wk=None, accum=None):
        def f(e):
            kw = {}
            if accum is not None:
                kw['accum_out'] = accum
            if op1 is None:
                return e.tensor_scalar(out=out, in0=in0, scalar1=s1, scalar2=s2, op0=op0, **kw)
            return e.tensor_scalar(out=out, in0=in0, scalar1=s1, scalar2=s2, op0=op0, op1=op1, **kw)
        self.em.op(eng, f, reads=self.keys([in0, s1, s2], rk), writes=self.keys([out, accum], wk))

    def stt(self, eng, out, in0, sc, in1, op0, op1, rk=None, wk=None):
        self.em.op(eng, lambda e: e.scalar_tensor_tensor(out=out, in0=in0, scalar=sc, in1=in1, op0=op0, op1=op1),
                   reads=self.keys([in0, sc, in1], rk), writes=self.keys([out], wk))

    def act(self, out, in_, func, bias=None, scale=None, accum=None, rk=None, wk=None, eng='act'):
        def f(e):
            kw = {}
            if bias is not None:
                kw['bias'] = bias
            if scale is not None:
                kw['scale'] = scale
            if accum is not None:
                kw['accum_out'] = accum
            return e.activation(out=out, in_=in_, func=func, **kw)
        self.em.op(eng, f, reads=self.keys([in_, bias, scale], rk), writes=self.keys([out, accum], wk))

    def cp(self, eng, out, in_, rk=None, wk=None):
        if eng == 'act':
            self.em.op(eng, lambda e: e.copy(out=out, in_=in_), reads=self.keys([in_], rk), writes=self.keys([out], wk))
        else:
            self.em.op(eng, lambda e: e.tensor_copy(out=out, in_=in_), reads=self.keys([in_], rk), writes=self.keys([out], wk))

    fp32r = False
    rw_bfrac = 1.0
    rm_bfrac = 1.0

    def mm(self, out, lhsT, rhs, start=True, stop=True, rk=None, wk=None):
        reads = self.keys([lhsT, rhs], rk)
        if self.fp32r and lhsT.dtype == F32 and rhs.dtype == F32:
            lhsT = lhsT.bitcast(mybir.dt.float32r)
            rhs = rhs.bitcast(mybir.dt.float32r)
        self.em.op('pe', lambda e: e.matmul(out, lhsT=lhsT, rhs=rhs, start=start, stop=stop),
                   reads=reads, writes=self.keys([out], wk), inc=stop)

    def tr(self, out, in_, ident, rk=None, wk=None):
        self.em.op('pe', lambda e: e.transpose(out=out, in_=in_, identity=ident), reads=self.keys([in_, ident], rk), writes=self.keys([out], wk))

    def memset(self, eng, ap, val, wk=None):
        self.em.op(eng, lambda e: e.memset(ap, val), writes=self.keys([ap], wk))

    def asel(self, out, pattern, cmp, fill, base, cm):
        self.em.op('pool', lambda e: e.affine_select(out=out, in_=out, pattern=pattern, compare_op=cmp, fill=fill, base=base, channel_multiplier=cm),
                   reads=self.keys([out], None), writes=self.keys([out], None))

    def ld(self, out, in_, rk=None, wk=None, q='sp', **kw):
        self.em.dma(q, out, in_, reads=self.keys([in_], rk), writes=self.keys([out], wk), **kw)

    def st(self, out, in_, rk=None, wk=None, q='pool', **kw):
        self.em.dma(q, out, in_, reads=self.keys([in_], rk), writes=self.keys([out], wk), **kw)

    def capture(self, fn, *args):
        real = self.em

        class _Rec:
            def __init__(s):
                s.items = []

            def op(s, *a, **kw):
                s.items.append(('op', a, kw))

            def dma(s, *a, **kw):
                s.items.append(('dma', a, kw))

            def __getattr__(s, name):
                return getattr(real, name)

        rec = _Rec()
        self.em = rec
        try:
            fn(*args)
        finally:
            self.em = real
        return rec.items

    @staticmethod
    def merge_streams(A, B, bfrac=1.0):
        na, nb = len(A), len(B)
        ia = ib = 0
        out = []
        while ia < na or ib < nb:
            if ib >= nb or (ia < na and ia * nb <= ib * na * bfrac):
                out.append(A[ia])
                ia += 1
            else:
                out.append(B[ib])
                ib += 1
        return out

    def emit_interleaved(self, A, B, bfrac=1.0):
        for it in self.merge_streams(A, B, bfrac):
            getattr(self.em, it[0])(*it[1], **it[2])

    def consts(self):
        em = self.em
        self.identf = em.sb([128, 128], F32, "identf")
        self.identb = em.sb([128, 128], BF16, "identb")
        self.memset('pool', self.identf[:], 0.0)
        self.asel(self.identf[:], [[-1, 128]], ALU.not_equal, 1.0, 0, 1)
        self.cp('pool', self.identb[:], self.identf[:])
        self.zeros = em.sb([128, 512], F32, "zeros")
        self.memset('pool', self.zeros[:], 0.0)
        for r0 in range(0, NFM, 128):
            n = min(128, NFM - r0)
            self.st(self.s_fm[r0:r0 + n, 0:PAD], self.zeros[0:n, 0:PAD], wk=[('s_fm_pad', r0)])

    def colvec(self, src_vec, C, name):
        t = self.em.sb([128, C], F32, name)
        self.ld(t[:], src_vec.rearrange("(c p) -> p c", p=128), allow_slow_non_contiguous=True)
        return t

    def rowbc(self, src_vec, n, P, name):
        t = self.em.sb([P, n], F32, name)
        self.ld(t[:], src_vec.partition_broadcast(P))
        return t

    def norm_group(self, src, srckey, g, hT_dst, hT_key, eps=1e-6):
        for tt in range(4):
            t = g * 4 + tt
            xt = self.rot('ng_x', [128, D], F32, 2)
            self.ld(xt[:], src[t * 128:(t + 1) * 128, :], rk=[(srckey, t)])
            sq = self.rot('ng_sq', [128, D], BF16, 2)
            ss = self.rot('ng_ss', [128, 1], F32, 4)
            self.act(sq[:], xt[:], AF.Square, accum=ss[:])
            rs = self.rot('ng_rs', [128, 1], F32, 4)
            self.act(rs[:], ss[:], AF.Sqrt, scale=1.0 / D, bias=eps)
            self.em.op('dve', lambda e, rs=rs: e.reciprocal(out=rs[:], in_=rs[:]), reads=[kname(rs[:])], writes=[kname(rs[:])])
            hb = self.rot('ng_hb', [128, D], BF16, 2)
            self.ts('dve', hb[:], xt[:], rs[:, 0:1], None, ALU.mult)
            pt = self.ps('tr')
            ptb = pt.bitcast(BF16)
            for c in range(8):
                self.tr(ptb[:, c * 128:(c + 1) * 128], hb[:, c * 128:(c + 1) * 128], self.identb[:])
            self.cp('act' if tt % 2 == 0 else 'dve', hT_dst[:, :, tt * 128:(tt + 1) * 128], ptb[:, 0:1024].rearrange("p (c t) -> p c t", c=8),
                    wk=[hT_key])

    def load_w(self, src, K, ncols, gcol=None, dst=None, dst_key=None, eng='pool', kchunk=None, nbuf=2, engs=('act', 'dve', 'act', 'dve', 'pool'), tag=''):
        if dst is None:
            dst = self.rot(f'wb_{K}_{ncols}{tag}', [128, K, ncols], BF16, nbuf)
            dst = dst[:]
        kc = kchunk or max(1, min(K, 4096 // ncols))
        k0 = 0
        while k0 < K:
            kn = min(kc, K - k0)
            wf = self.rot(f'wf_{kc * ncols}{tag}', [128, kc * ncols], F32, nbuf)
            wfv = wf[:, 0:kn * ncols].rearrange("p (c n) -> p c n", n=ncols)
            self.ld(wfv, src[k0 * 128:(k0 + kn) * 128, :].rearrange("(c p) n -> p c n", p=128))
            for c in range(kn):
                wk = None
                self.cast_rr = getattr(self, 'cast_rr', 0) + 1
                e_ = engs[self.cast_rr % len(engs)]
                if gcol is not None and e_ == 'pool':
                    e_ = 'act' if self.cast_rr % 2 == 0 else 'dve'
                if gcol is not None:
                    if e_ == 'act':
                        self.act(dst[:, k0 + c, :], wfv[:, c, :], AF.Copy, scale=gcol[:, k0 + c:k0 + c + 1], wk=wk)
                    else:
                        self.ts(e_, dst[:, k0 + c, :], wfv[:, c, :], gcol[:, k0 + c:k0 + c + 1], None, ALU.mult, wk=wk)
                else:
                    self.cp(e_, dst[:, k0 + c, :], wfv[:, c, :], wk=wk)
            k0 += kn
        return dst

    def phase_in(self, l):
        I = self.I
        src, srckey = (I["x"], "x") if l == 0 else (self.xres, "xres")
        with self.scope():
            self.set_pools(tr=[0, 1], mm=[2, 3, 4, 5, 6, 7])
            g_pre = self.colvec(I["ln_mix_pre"][l], 8, "g_pre")
            hT = self.em.sb([128, 8, S], BF16, "hT_all")
            def norms(gs):
                for g in gs:
                    self.norm_group(src, srckey, g, hT[:, :, g * TG:(g + 1) * TG], ('hT', g))
                    self.st(self.s_hT[:, :, g * TG:(g + 1) * TG], hT[:, :, g * TG:(g + 1) * TG], rk=[('hT', g)], wk=[('s_hT', g)])

            w_in = I["w_in"][l]

            def fm_iter(wb, r0, cc, g):
                pt = self.ps('mm')
                for k in range(8):
                    self.mm(pt[:, 0:TG], wb[:, k, cc * 128:(cc + 1) * 128], hT[:, k, g * TG:(g + 1) * TG], start=(k == 0), stop=(k == 7),
                            rk=[kname(wb), ('hT', g)])
                ev = self.rot('tm_ev', [128, 512], F32, 4)
                self.cp('act' if g % 2 == 0 else 'dve', ev[:], pt[:, 0:TG])
                self.st(self.s_fm[r0 + cc * 128:r0 + (cc + 1) * 128, PAD + g * TG:PAD + (g + 1) * TG], ev[:], wk=[('s_fm', r0 + cc * 128, g)])

            def first_block_g(wb, g):
                for cc in range(4):
                    fm_iter(wb, 0, cc, g)

            norms([0])
            wb0 = self.load_w(w_in[:, 0:512], 8, 512, gcol=g_pre)
            for g in range(NG):
                nxt = self.capture(norms, [g + 1]) if g + 1 < NG else []
                self.emit_interleaved(nxt, self.capture(first_block_g, wb0, g))
            fm_blocks = [(512, 512, 512), (1024, 512, 1024), (1536, 512, 1536), (LQ_OFF, 512, FM_LQK)]
            for (c0, ncols, r0) in fm_blocks:
                wb = self.load_w(w_in[:, c0:c0 + ncols], 8, ncols, gcol=g_pre)
                for cc in range(ncols // 128):
                    for g in range(NG):
                        pt = self.ps('mm')
                        for k in range(8):
                            self.mm(pt[:, 0:TG], wb[:, k, cc * 128:(cc + 1) * 128], hT[:, k, g * TG:(g + 1) * TG], start=(k == 0), stop=(k == 7),
                                    rk=[kname(wb), ('hT', g)])
                        ev = self.rot('tm_ev', [128, 512], F32, 4)
                        self.cp('act' if g % 2 == 0 else 'dve', ev[:], pt[:, 0:TG])
                        self.st(self.s_fm[r0 + cc * 128:r0 + (cc + 1) * 128, PAD + g * TG:PAD + (g + 1) * TG], ev[:], wk=[('s_fm', r0 + cc * 128, g)])
            wb = self.load_w(w_in[:, LI_OFF:LI_OFF + 8], 8, 8, gcol=g_pre)
            for g in range(NG):
                pt = self.ps('mm')
                for k in range(8):
                    self.mm(pt[0:8, 0:TG], wb[:, k, 0:8], hT[:, k, g * TG:(g + 1) * TG], start=(k == 0), stop=(k == 7), rk=[kname(wb), ('hT', g)])
                ev = self.rot('tm_ev', [128, 512], F32, 4)
                self.cp('act' if g % 2 == 0 else 'dve', ev[0:8, :], pt[0:8, 0:TG])
                self.st(self.s_fm[FM_LIF:FM_LIF + 8, PAD + g * TG:PAD + (g + 1) * TG], ev[0:8, :], wk=[('s_fm', FM_LIF, g)])
            for (c0, tc0) in [(MV_OFF, 0), (LV_OFF, 512)]:
                wb = self.load_w(w_in[:, c0:c0 + 512], 8, 512, gcol=g_pre)
                for t in range(NTILE):
                    pt = self.ps('mm')
                    for k in range(8):
                        self.mm(pt[:, 0:512], hT[:, k, t * 128:(t + 1) * 128], wb[:, k, :], start=(k == 0), stop=(k == 7), rk=[kname(wb), ('hT', t // 4)])
                    ev = self.rot('tm_ev', [128, 512], F32, 4)
                    self.cp('act' if t % 2 == 0 else 'dve', ev[:], pt[:, 0:512])
                    self.st(self.s_tm[t * 128:(t + 1) * 128, tc0:tc0 + 512], ev[:], wk=[('s_tm', tc0, t)])

    def resid_epilogue(self, pts, t, gbc, xsrc, xsrckey, dst, dstkey, eps=1e-6):
        ssa = self.rot('ep_ss', [128, 2], F32, 4)
        junk = self.rot('ep_junk', [128, 512], BF16, 2)
        for hh in range(2):
            self.act(junk[:], pts[hh][:, 0:512], AF.Square, accum=ssa[:, hh:hh + 1])
        rs = self.rot('ep_rs', [128, 1], F32, 4)
        self.tt('dve', rs[:], ssa[:, 0:1], ssa[:, 1:2], ALU.add)
        self.act(rs[:], rs[:], AF.Sqrt, scale=1.0 / D, bias=eps)
        self.em.op('dve', lambda e, rs=rs: e.reciprocal(out=rs[:], in_=rs[:]), reads=[kname(rs[:])], writes=[kname(rs[:])])
        xt = self.rot('ep_x', [128, D], F32, 2)
        self.ld(xt[:], xsrc[t * 128:(t + 1) * 128, :], rk=[(xsrckey, t)])
        ot = self.rot('ep_o', [128, D], F32, 2)
        for hh in range(2):
            self.stt('dve', ot[:, hh * 512:(hh + 1) * 512], pts[hh][:, 0:512], rs[:, 0:1], gbc[:, hh * 512:(hh + 1) * 512], ALU.mult, ALU.mult)
        self.tt('pool', ot[:], ot[:], xt[:], ALU.add)
        self.st(dst[t * 128:(t + 1) * 128, :], ot[:], wk=[(dstkey, t)])

    def phase_merge(self, l, wgt_pf=None, wbr_pf=None, wout_pf=None):
        I = self.I
        xsrc, xkey = (I["x"], "x") if l == 0 else (self.xres, "xres")
        with self.scope():
            self.set_pools(tr=[0], mm=[1, 2, 3, 4, 5], o=[6, 7])
            g_pre = self.colvec(I["ln_mix_pre"][l], 8, "g_pre_m")
            gpost = self.rowbc(I["ln_mix_post"][l], D, 128, "gpost_bc")
            wbr = wbr_pf if wbr_pf is not None else self.em.sb([128, 8, D], BF16, "wbr")
            wgt = wgt_pf if wgt_pf is not None else self.em.sb([128, 8, 3072], BF16, "wgt")
            wout = wout_pf if wout_pf is not None else self.em.sb([128, 8, D], BF16, "wout")
            with self.scope():
                if wbr_pf is None:
                    self.load_w(I["w_br_rwkv"][l], 2, D, dst=wbr[:, 0:2, :], dst_key='wbr', nbuf=4)
                    self.load_w(I["w_br_moba"][l], 4, D, dst=wbr[:, 2:6, :], dst_key='wbr', nbuf=4)
                    self.load_w(I["w_br_mlstm"][l], 2, D, dst=wbr[:, 6:8, :], dst_key='wbr', nbuf=4)
                for cb in (range(6) if wgt_pf is None else ()):
                    self.load_w(I["w_in"][l][:, GT_OFF + cb * 512:GT_OFF + (cb + 1) * 512], 8, 512, gcol=g_pre, dst=wgt[:, :, cb * 512:(cb + 1) * 512], dst_key='wgt', nbuf=4)
                for cb in (range(2) if wout_pf is None else ()):
                    self.load_w(I["w_out"][l][:, cb * 512:(cb + 1) * 512], 8, 512, dst=wout[:, :, cb * 512:(cb + 1) * 512], dst_key='wout', nbuf=4)
            kgrp = [(0, 2), (2, 6), (6, 8)]
            def stA(g):
                yT = self.rot('mg_yT', [128, 8, TG], BF16, 1)
                for tt in range(4):
                    t = g * 4 + tt
                    yt = self.rot('mg_y', [128, D], F32, 2)
                    self.ld(yt[:], self.s_y[t * 128:(t + 1) * 128, :], rk=['s_y'])
                    yb = self.rot('mg_yb', [128, D], BF16, 2)
                    self.cp('pool', yb[:], yt[:])
                    pt = self.ps('tr')
                    ptb = pt.bitcast(BF16)
                    for c in range(8):
                        self.tr(ptb[:, c * 128:(c + 1) * 128], yb[:, c * 128:(c + 1) * 128], self.identb[:])
                    self.cp('act', yT[:, :, tt * 128:(tt + 1) * 128], ptb[:, 0:1024].rearrange("p (c t) -> p c t", c=8))
                hTg = self.rot('mg_hT', [128, 8, TG], BF16, 1)
                self.ld(hTg[:], self.s_hT[:, :, g * TG:(g + 1) * TG], rk=[('s_hT', g)])
                mT = self.rot('mg_mT', [128, 8, TG], BF16, 2)
                for fc in range(8):
                    acc = self.rot('mg_acc', [128, TG], F32, 2)
                    for b in range(3):
                        pg = self.ps('mm')
                        for k in range(8):
                            self.mm(pg[:, 0:TG], wgt[:, k, b * 1024 + fc * 128:b * 1024 + (fc + 1) * 128], hTg[:, k, :], start=(k == 0), stop=(k == 7))
                        sg = self.rot('mg_sg', [128, TG], F32, 3)
                        self.act(sg[:], pg[:, 0:TG], AF.Sigmoid)
                        pb = self.ps('mm')
                        k0, k1 = kgrp[b]
                        for k in range(k0, k1):
                            self.mm(pb[:, 0:TG], wbr[:, k, fc * 128:(fc + 1) * 128], yT[:, k, :], start=(k == k0), stop=(k == k1 - 1))
                        if b == 0:
                            self.tt('dve', acc[:], sg[:], pb[:, 0:TG], ALU.mult)
                        else:
                            tmp = self.rot('mg_tmp', [128, TG], F32, 2)
                            self.tt('dve', tmp[:], sg[:], pb[:, 0:TG], ALU.mult)
                            if b == 1:
                                self.tt('pool', acc[:], acc[:], tmp[:], ALU.add)
                            else:
                                self.tt('pool', mT[:, fc, :], acc[:], tmp[:], ALU.add)
                MT[g] = mT

            def stB(g):
                mT = MT.pop(g)
                for tt in range(4):
                    t = g * 4 + tt
                    pts = [self.ps('o'), self.ps('o')]
                    for hh in range(2):
                        for k in range(8):
                            self.mm(pts[hh][:, 0:512], mT[:, k, tt * 128:(tt + 1) * 128], wout[:, k, hh * 512:(hh + 1) * 512], start=(k == 0), stop=(k == 7))
                    self.resid_epilogue(pts, t, gpost, xsrc, xkey, self.xres, "xres")


            MT = {}
            self.emit_interleaved(self.capture(stA, 0), [])
            for g in range(NG):
                nxt = self.capture(stA, g + 1) if g + 1 < NG else []
                self.emit_interleaved(nxt, self.capture(stB, g))
    def phase_ffn(self, l):
        I = self.I
        if not hasattr(self, 's_aT'):
            self.s_aT = self.nc.dram_tensor("s_aT", [22, 128, S], BF16, kind="Internal").ap()
        with self.scope():
            self.set_pools(tr=[0, 1], mm0=[2, 3, 4], mm1=[5, 6, 7])
            g_pre = self.colvec(I["ln_ffn_pre"][l], 8, "g_ffn")
            cw = self.em.sb([128, 3, 44], F32, "ffn_cw")
            for j3 in range(3):
                self.ld(cw[:, j3, :], I["ffn_conv_w"][l][j3].rearrange("(c p) -> p c", p=128), allow_slow_non_contiguous=True)
            cb = self.colvec(I["ffn_conv_b"][l], 44, "ffn_cb")
            hT = self.em.sb([128, 8, S], BF16, "ffn_hT_all")
            def fnorms(gs):
                for g in gs:
                    self.norm_group(self.xres, "xres", g, hT[:, :, g * TG:(g + 1) * TG], ('fhT', g))

            fnorms([0])
            uprev = {0: {}, 1: {}}
            pend = {0: [], 1: []}

            def back(s):
                aT_, g_, ucs_, j_ = pend[s].pop(0)
                ge = self.rot('ff_ge%d' % s, [128, TG], F32, 2)
                self.act(ge[:], ucs_[0][:], AF.Gelu_apprx_tanh)
                self.tt('pool', aT_[:, g_ * TG:(g_ + 1) * TG], ge[:], ucs_[1][:], ALU.mult)
                if g_ == NG - 1:
                    self.st(self.s_aT[j_], aT_[:], wk=[('s_aT', j_)], q='sp')

            def up_w(s, j):
                wbs = []
                for half in range(2):
                    col0 = half * DFF + j * 128
                    wbs.append(self.load_w(I["ffn_up"][l][:, col0:col0 + 128], 8, 128, gcol=g_pre, nbuf=3, engs=('act', 'act', 'pool'), tag='s%d' % s))
                aT = self.rot('ff_aT%d' % s, [128, S], BF16, 1)
                return wbs, aT

            def up_jg(s, j, g, wbs, aT):
                ucs = []
                for half in range(2):
                    jj = half * 22 + j
                    wb = wbs[half]
                    pt = self.ps('mm%d' % s)
                    for k in range(8):
                        self.mm(pt[:, 0:TG], wb[:, k, :], hT[:, k, g * TG:(g + 1) * TG], start=(k == 0), stop=(k == 7), rk=[kname(wb), ('fhT', g)])
                    u = self.rot('ff_u%d_%d' % (half, s), [128, TG + 2], F32, 3)
                    self.cp('act', u[:, 2:TG + 2], pt[:, 0:TG])
                    if g == 0:
                        self.memset('pool', u[:, 0:2], 0.0)
                    else:
                        self.cp('pool', u[:, 0:2], uprev[s][half][:, TG:TG + 2])
                    uprev[s][half] = u
                    uc = self.rot('ff_uc%d_%d' % (half, s), [128, TG], F32, 3)
                    self.act(uc[:], pt[:, 0:TG], AF.Identity, scale=cw[:, 2, jj:jj + 1], bias=cb[:, jj:jj + 1])
                    self.stt('dve', uc[:], u[:, 1:TG + 1], cw[:, 1, jj:jj + 1], uc[:], ALU.mult, ALU.add)
                    self.stt('dve', uc[:], u[:, 0:TG], cw[:, 0, jj:jj + 1], uc[:], ALU.mult, ALU.add)
                    ucs.append(uc)
                pend[s].append((aT, g, ucs, j))
                if len(pend[s]) > 1:
                    back(s)

            def run_stream(s, js):
                for j in js:
                    wbs_, aT_j = up_w(s, j)
                    for g in range(NG):
                        up_jg(s, j, g, wbs_, aT_j)
                while pend[s]:
                    back(s)

            wbs_, aT_j = up_w(0, 0)
            for g in range(NG):
                nxt = self.capture(fnorms, [g + 1]) if g + 1 < NG else []
                self.emit_interleaved(nxt, self.capture(up_jg, 0, 0, g, wbs_, aT_j))
            self.emit_interleaved(self.capture(run_stream, 0, range(1, 12)), self.capture(run_stream, 1, range(12, 22)))
        with self.scope():
            self.set_pools(o=[0, 1, 2, 3, 4, 5, 6, 7])
            gpost = self.rowbc(I["ln_ffn_post"][l], D, 128, "gffn_post_bc")
            wdn = self.em.sb([128, 22, D], BF16, "wdn")
            for cbk in range(4):
                self.load_w(I["ffn_down"][l][:, cbk * 256:(cbk + 1) * 256], 22, 256, dst=wdn[:, :, cbk * 256:(cbk + 1) * 256], kchunk=8)
            for g in range(NG):
                aTg = self.rot('ff_aTg', [128, 22, TG], BF16, 2)
                self.ld(aTg[:], self.s_aT[:, :, g * TG:(g + 1) * TG].rearrange("j p t -> p j t"), rk=['s_aT'])
                for tt in range(4):
                    t = g * 4 + tt
                    pts = [self.ps('o'), self.ps('o')]
                    for hh in range(2):
                        for j in range(22):
                            self.mm(pts[hh][:, 0:512], aTg[:, j, tt * 128:(tt + 1) * 128], wdn[:, j, hh * 512:(hh + 1) * 512], start=(j == 0), stop=(j == 21))
                    self.resid_epilogue(pts, t, gpost, self.xres, "xres", self.xres, "xres")

    def phase_ple(self, l, dst, dstkey):
        I = self.I
        with self.scope():
            self.set_pools(tr=[0, 1], mm=[2, 3, 4, 5, 6, 7])
            g_pre = self.colvec(I["ln_ple"][l], 8, "g_ple")
            wg = self.em.sb([128, 8, D], BF16, "wpg")
            for cbk in range(2):
                self.load_w(I["ple_gate"][l][:, cbk * 512:(cbk + 1) * 512], 8, 512, gcol=g_pre, dst=wg[:, :, cbk * 512:(cbk + 1) * 512], dst_key='wpg')
            wp = self.em.sb([128, 2, D], BF16, "wpp")
            self.load_w(I["ple_proj"][l], 2, D, dst=wp[:], dst_key='wpp')
            HT = {}

            def stA(g):
                hTg = self.rot('pl_hT', [128, 8, TG], BF16, 2)
                self.norm_group(self.xres, "xres", g, hTg[:], kname(hTg[:]))
                HT[g] = hTg

            def stB(g):
                hTg = HT.pop(g)
                for tt in range(4):
                    t = g * 4 + tt
                    pt_ = self.rot('pl_p', [128, 256], F32, 2)
                    self.ld(pt_[:], I["p"][l][t * 128:(t + 1) * 128, :])
                    pb = self.rot('pl_pb', [128, 256], BF16, 2)
                    self.cp('pool', pb[:], pt_[:])
                    ptr = self.ps('tr')
                    ptrb = ptr.bitcast(BF16)
                    for c in range(2):
                        self.tr(ptrb[:, c * 128:(c + 1) * 128], pb[:, c * 128:(c + 1) * 128], self.identb[:])
                    pT = self.rot('pl_pT', [128, 2, 128], BF16, 2)
                    self.cp('act', pT[:], ptrb[:, 0:256].rearrange("p (c t) -> p c t", c=2))
                    xt = self.rot('pl_x', [128, D], F32, 2)
                    self.ld(xt[:], self.xres[t * 128:(t + 1) * 128, :], rk=[("xres", t)])
                    ot = self.rot('pl_o', [128, D], F32, 2)
                    for hh in range(2):
                        pg = self.ps('mm')
                        for k in range(8):
                            self.mm(pg[:, 0:512], hTg[:, k, tt * 128:(tt + 1) * 128], wg[:, k, hh * 512:(hh + 1) * 512], start=(k == 0), stop=(k == 7))
                        sg = self.rot('pl_sg', [128, 512], F32, 2)
                        self.act(sg[:], pg[:, 0:512], AF.Sigmoid)
                        pp = self.ps('mm')
                        for k in range(2):
                            self.mm(pp[:, 0:512], pT[:, k, :], wp[:, k, hh * 512:(hh + 1) * 512], start=(k == 0), stop=(k == 1))
                        self.tt('dve', sg[:], sg[:], pp[:, 0:512], ALU.mult)
                        self.tt('pool', ot[:, hh * 512:(hh + 1) * 512], sg[:], xt[:, hh * 512:(hh + 1) * 512], ALU.add)
                    self.st(dst[t * 128:(t + 1) * 128, :], ot[:], wk=[(dstkey, t)])

            self.emit_interleaved(self.capture(stA, 0), [])
            for g in range(NG):
                nxt = self.capture(stA, g + 1) if g + 1 < NG else []
                self.emit_interleaved(nxt, self.capture(stB, g))

    def phase_moba(self, l, extra=None):
        with self.scope():
            self.set_pools(sc=[0, 1, 2, 3], acc=[4, 5], ot=[6], bs=[7], tr=[7])
            em = self.em
            KQ = []
            for i_ in range(2):
                Ka = em.sb([80, S], BF16, "mb_KaugT%d" % i_)
                Qa = em.sb([80, S], BF16, "mb_QaugT%d" % i_)
                Va = em.sb([128, 32, 65], BF16, "mb_Vaug%d" % i_)
                self.memset('pool', Va[:, :, 64:65], 1.0)
                KQ.append((Ka, Qa, Va))
            with self.scope():
                ohb = em.sb([16, S], BF16, "mb_ohb")
                oh = em.sb([16, S], F32, "mb_oh")
                self.memset('pool', oh[:], 1.0)
                self.asel(oh[:], [[1, S]], ALU.is_ge, 0.0, 0, -256)
                self.asel(oh[:], [[-1, S]], ALU.is_ge, 0.0, 255, 256)
                self.cp('pool', ohb[:], oh[:])
                for i_ in range(2):
                    self.st(KQ[i_][0][64:80, :], ohb[:], q='sp')
            EX = self.capture(extra) if extra is not None else []
            tri = em.sb([128, 128], BF16, "mb_tri")
            trif = em.sb([128, 128], F32, "mb_trif")
            self.memset('pool', trif[:], 1.0)
            self.asel(trif[:], [[1, 128]], ALU.is_ge, 0.0, 0, -1)
            self.cp('pool', tri[:], trif[:])
            pastm = em.sb([128, 32, 16], F32, "mb_pastm")
            ownm = em.sb([128, 32, 16], F32, "mb_ownm")
            negm = em.sb([128, 32, 16], F32, "mb_negm")
            self.memset('pool', pastm[:], 0.0)
            self.memset('pool', ownm[:], 0.0)
            for tt in range(32):
                ob = tt // 2
                if ob > 0:
                    self.memset('pool', pastm[:, tt, 0:ob], 1.0)
                self.memset('pool', ownm[:, tt, ob:ob + 1], 1.0)
            self.ts('pool', negm[:], pastm[:], -1.0, 1e30, ALU.add, ALU.mult)
            ident65 = self.identf[0:65, 0:65]
            HS = {}

            def setupA(h):
                KaugT, QaugT, Vaug = KQ[h % 2]
                qf = self.rot('mb_qf', [64, S], F32, 1)
                kf = self.rot('mb_kf', [64, S], F32, 1)
                r0 = FM_MQK + h * 64
                self.ld(qf[:], self.s_fm[r0:r0 + 64, PAD:PAD + S])
                self.ld(kf[:], self.s_fm[r0 + 512:r0 + 576, PAD:PAD + S])
                vf = self.rot('mb_vf', [128, 32, 64], F32, 1)
                self.ld(vf[:], self.s_tm[:, h * 64:(h + 1) * 64].rearrange("(n p) d -> p n d", p=128))
                self.cp('dve', Vaug[:, :, 0:64], vf[:])
                self.cp('dve', KaugT[0:64, :], kf[:])
                self.cp('dve', QaugT[0:64, :], qf[:])
                km = self.rot('mb_km', [64, 16], F32, 2)
                self.em.op('dve', lambda e, km=km, kf=kf: e.tensor_reduce(out=km[:], in_=kf[:].rearrange("p (j s) -> p j s", s=256), axis=AX.X, op=ALU.add),
                           reads=[kname(kf[:])], writes=[kname(km[:])])
                bs = self.ps('bs')
                for tt in range(32):
                    self.mm(bs[:, tt * 16:(tt + 1) * 16], qf[:, tt * 128:(tt + 1) * 128], km[:, :])
                bsm = self.rot('mb_bsm', [128, 32, 16], F32, 1)
                self.tt('dve', bsm[:], bs[:, 0:512].rearrange("p (t j) -> p t j", j=16), pastm[:], ALU.mult)
                self.tt('dve', bsm[:], bsm[:], negm[:], ALU.add)
                m8 = self.rot('mb_m8', [128, 32, 8], F32, 1)
                for tt in range(32):
                    self.em.op('dve', lambda e, tt=tt, m8=m8, bsm=bsm: e.max(out=m8[:, tt, :], in_=bsm[:, tt, :]),
                               reads=[kname(bsm[:])], writes=[kname(m8[:])])
                sel = self.rot('mb_sel', [128, 32, 16], F32, 2)
                self.tt('dve', sel[:], bsm[:], bcast(m8[:, :, 2:3], 2, 16), ALU.is_ge)
                self.tt('dve', sel[:], sel[:], pastm[:], ALU.mult)
                self.tt('dve', sel[:], sel[:], ownm[:], ALU.add)
                self.ts('dve', sel[:], sel[:], -1.0, 30000.0, ALU.add, ALU.mult)
                HS[h] = sel

            def setupB(h):
                KaugT, QaugT, Vaug = KQ[h % 2]
                sel = HS[h]
                mbT = self.rot('mb_mbT', [16, S], BF16, 1)
                for t4 in range(8):
                    pt = self.ps('tr')
                    for q in range(4):
                        tt = t4 * 4 + q
                        self.tr(pt[0:16, q * 128:(q + 1) * 128], sel[:, tt, :], self.identf[:])
                    self.cp('dve', mbT[:, t4 * 512:(t4 + 1) * 512], pt[0:16, 0:512])
                self.st(QaugT[64:80, :], mbT[:], q='sp')

            def main(h):
                KaugT, QaugT, Vaug = KQ[h % 2]
                iters = [(tg, st_) for tg in range(8) for st_ in range(4 * (tg + 1))]
                LA = 3
                pTs = {}
                ots = {}

                def front(i):
                    tg, st_ = iters[i]
                    sl_ = st_ - 4 * tg
                    c0 = 256 if sl_ >= 2 else 0
                    sc = self.ps('sc')
                    self.mm(sc[:, c0:512], KaugT[0:80, st_ * 128:(st_ + 1) * 128], QaugT[0:80, tg * 512 + c0:(tg + 1) * 512])
                    pT = self.rot('mb_pT', [128, 512], BF16, 6)
                    self.act(pT[:, c0:512], sc[:, c0:512], AF.Exp, scale=0.125)
                    if sl_ >= 0:
                        if sl_ * 128 > c0:
                            self.memset('pool', pT[:, c0:sl_ * 128], 0.0)
                        self.tt('pool', pT[:, sl_ * 128:(sl_ + 1) * 128], pT[:, sl_ * 128:(sl_ + 1) * 128], tri[:], ALU.mult)
                    pTs[i] = (pT, c0)

                def back(i):
                    tg, st_ = iters[i]
                    nst = 4 * (tg + 1)
                    if st_ == 0:
                        ots[tg] = self.ps('acc')
                    ot = ots[tg]
                    pT, c0 = pTs.pop(i)
                    self.mm(ot[0:65, c0:512], Vaug[:, st_, :], pT[:, c0:512], start=(st_ == 0), stop=(st_ == nst - 1))
                    if st_ == nst - 1:
                        osb = self.rot('mb_osb', [65, 512], F32, 2)
                        self.cp('dve', osb[:], ot[0:65, 0:512])
                        po = self.ps('ot')
                        for qi in range(4):
                            self.tr(po[:, qi * 65:(qi + 1) * 65], osb[0:65, qi * 128:(qi + 1) * 128], ident65)
                        rd = self.rot('mb_rd', [128, 4, 1], F32, 2)
                        pov = po[:, 0:260].rearrange("p (q d) -> p q d", d=65)
                        self.em.op('dve', lambda e, rd=rd, pov=pov: e.reciprocal(out=rd[:], in_=pov[:, :, 64:65]), reads=[kname(po[:])], writes=[kname(rd[:])])
                        yo = self.rot('mb_yo', [128, 4, 64], F32, 2)
                        self.tt('dve', yo[:], pov[:, :, 0:64], bcast(rd[:], 2, 64), ALU.mult)
                        self.st(self.s_y[tg * 512:(tg + 1) * 512, 256 + h * 64:256 + (h + 1) * 64].rearrange("(q p) d -> p q d", p=128), yo[:], wk=[('s_y_b', h, tg)])

                n = len(iters)
                for i in range(min(LA, n)):
                    front(i)
                for i in range(n):
                    if i + LA < n:
                        front(i + LA)
                    back(i)

            setupA(0)
            setupB(0)
            for h in range(8):
                M_ = self.capture(main, h)
                if h + 1 < 8:
                    SA = self.capture(setupA, h + 1)
                    SB = self.capture(setupB, h + 1)
                    c1 = len(M_) // 20
                    c2 = (len(M_) * 3) // 4
                    seq = M_[:c1] + self.merge_streams(M_[c1:c2], SA) + SB + M_[c2:]
                else:
                    seq = M_
                ex_h = EX[(len(EX) * h) // 8:(len(EX) * (h + 1)) // 8]
                self.emit_interleaved(seq, ex_h)

    def phase_mlstm(self, l, scoped=True, pools=None, GW=512):
        I = self.I
        with (self.scope() if scoped else contextlib.nullcontext()):
            self.set_pools(**(pools or dict(tk=[0, 1], qk=[2, 3], acc=[4, 5], P=[6], misc=[7])))
            em = self.em
            NJ = GW // 64
            NGm = S // GW
            cw = em.sb([64, 4, 8], F32, "ml_cw")
            for j in range(4):
                self.ld(cw[:, j, :], I["mlstm_conv_w"][l][j].rearrange("(c p) -> p c", p=64), allow_slow_non_contiguous=True)
            cb = em.sb([64, 8], F32, "ml_cb")
            self.ld(cb[:], I["mlstm_conv_b"][l].rearrange("(c p) -> p c", p=64), allow_slow_non_contiguous=True)
            ib = em.sb([4, 1], F32, "ml_ib")
            fb = em.sb([4, 1], F32, "ml_fb")
            self.ld(ib[:], I["mlstm_i_b"][l].rearrange("(h o) -> h o", o=1))
            self.ld(fb[:], I["mlstm_f_b"][l].rearrange("(h o) -> h o", o=1))
            nfb = em.sb([4, 1], F32, "ml_nfb")
            self.ts('dve', nfb[:], fb[:], -1.0, None, ALU.mult)
            hng = self.rowbc(I["mlstm_hn_g"][l], 256, 64, "ml_hng")
            selT = em.sb([4, 4, 64], F32, "ml_selT")
            self.memset('pool', selT[:], 0.0)
            self.asel(selT[:], [[-1, 4], [0, 64]], ALU.not_equal, 1.0, 0, 1)
            mask4 = em.sb([64, 4, 64], F32, "ml_mask4")
            self.memset('pool', mask4[:], 1.0)
            self.asel(mask4[:], [[0, 4], [1, 64]], ALU.is_ge, 0.0, 0, -1)
            ones4 = em.sb([4, GW], F32, "ml_ones4")
            zeros4 = em.sb([4, GW], F32, "ml_zeros4")
            self.memset('pool', ones4[:], 1.0)
            self.memset('pool', zeros4[:], 0.0)
            Chat = em.sb([64, 4, 65], F32, "ml_Chat0")
            self.memset('dve', Chat[:], 0.0)
            Fprev = None
            cprev = None
            def prepare(g):
                nonlocal Fprev, cprev
                c0 = PAD + g * GW
                li = self.rot('ml_li', [4, GW], F32, 1)
                lf = self.rot('ml_lf', [4, GW], F32, 1)
                self.ld(li[:], self.s_fm[FM_LIF:FM_LIF + 4, c0:c0 + GW])
                self.ld(lf[:], self.s_fm[FM_LIF + 4:FM_LIF + 8, c0:c0 + GW])
                logi = self.rot('ml_logi', [4, GW], F32, 1)
                self.ts('dve', logi[:], li[:], ib[:, 0:1], None, ALU.add)
                e1 = self.rot('ml_e1', [4, GW], F32, 1)
                self.act(e1[:], lf[:], AF.Exp, bias=nfb[:, 0:1], scale=-1.0)
                self.act(e1[:], e1[:], AF.Ln, bias=1.0)
                logf = self.rot('ml_logf', [4, GW], F32, 1)
                self.ts('dve', logf[:], e1[:], -1.0, None, ALU.mult)
                F = self.rot('ml_F', [4, GW], F32, 2)
                self.em.op('dve', lambda e, F=F, logf=logf, init=(0.0 if Fprev is None else Fprev[:, GW - 1:GW]): e.tensor_tensor_scan(
                    out=F[:], data0=ones4[:], data1=logf[:], initial=init, op0=ALU.mult, op1=ALU.add),
                    reads=[kname(ones4[:]), kname(logf[:])] + ([] if Fprev is None else [kname(Fprev[:])]), writes=[kname(F[:])])
                G = self.rot('ml_G', [4, GW], F32, 1)
                self.tt('dve', G[:], logi[:], F[:], ALU.subtract)
                c = self.rot('ml_c', [4, GW], F32, 2)
                self.em.op('dve', lambda e, c=c, G=G, init=(0.0 if cprev is None else cprev[:, GW - 1:GW]): e.tensor_tensor_scan(
                    out=c[:], data0=zeros4[:], data1=G[:], initial=init, op0=ALU.max, op1=ALU.max),
                    reads=[kname(zeros4[:]), kname(G[:])] + ([] if cprev is None else [kname(cprev[:])]), writes=[kname(c[:])])
                cend = c[:].rearrange("p (j s) -> p j s", s=64)[:, :, 63:64]
                wrow = self.rot('ml_wrow', [4, GW], F32, 1)
                self.tt('dve', wrow[:].rearrange("p (j s) -> p j s", s=64), G[:].rearrange("p (j s) -> p j s", s=64), bcast(cend, 2, 64), ALU.subtract)
                self.act(wrow[:], wrow[:], AF.Exp)
                zrow = self.rot('ml_zrow', [4, GW], F32, 1)
                self.tt('dve', zrow[:].rearrange("p (j s) -> p j s", s=64), F[:].rearrange("p (j s) -> p j s", s=64), bcast(cend, 2, 64), ALU.add)
                self.act(zrow[:], zrow[:], AF.Exp, scale=-1.0)
                cpv = self.rot('ml_cpv', [4, NJ], F32, 1)
                if cprev is None:
                    self.memset('dve', cpv[:, 0:1], 0.0)
                else:
                    self.cp('dve', cpv[:, 0:1], cprev[:, GW - 1:GW])
                cend2 = c[:].rearrange("p (j s) -> p j s", s=64)[:, :, 63]
                self.cp('dve', cpv[:, 1:NJ], cend2[:, 0:NJ - 1], rk=[kname(c[:])])
                crow = self.rot('ml_crow', [4, NJ], F32, 1)
                self.tt('dve', crow[:], cpv[:], cend2, ALU.subtract)
                self.act(crow[:], crow[:], AF.Exp)
                pm = self.ps('misc')
                for j in range(NJ):
                    self.tr(pm[0:64, j * 4:(j + 1) * 4], wrow[0:4, j * 64:(j + 1) * 64], self.identf[0:4, 0:4])
                    self.tr(pm[0:64, 64 + j * 4:64 + (j + 1) * 4], zrow[0:4, j * 64:(j + 1) * 64], self.identf[0:4, 0:4])
                for h in range(4):
                    self.mm(pm[0:64, 128 + h * NJ:128 + (h + 1) * NJ], selT[0:4, h, :], crow[0:4, :])
                TP = self.rot('ml_TP', [64, 128 + 4 * NJ], F32, 2)
                self.cp('act', TP[:], pm[0:64, 0:128 + 4 * NJ])
                TPw = TP[:, 0:4 * NJ].rearrange("p (j h) -> p j h", h=4)
                TPz = TP[:, 64:64 + 4 * NJ].rearrange("p (j h) -> p j h", h=4)
                carry = TP[:, 128:128 + 4 * NJ].rearrange("p (h j) -> p h j", j=NJ)
                Fprev, cprev = F, c
                qk = self.rot('ml_qkraw', [64, 8, GW + 3], F32, 1)
                self.ld(qk[:], self.s_fm[FM_LQK:FM_LQK + 512, c0 - 3:c0 + GW].rearrange("(c p) n -> p c n", p=64))
                qkc = self.rot('ml_qkc', [64, 8, GW], F32, 1)
                for ch in range(8):
                    eng = 'dve'
                    self.act(qkc[:, ch, :], qk[:, ch, 3:GW + 3], AF.Identity, scale=cw[:, 3, ch:ch + 1], bias=cb[:, ch:ch + 1])
                    for j3 in range(3):
                        self.stt(eng, qkc[:, ch, :], qk[:, ch, j3:j3 + GW], cw[:, j3, ch:ch + 1], qkc[:, ch, :], ALU.mult, ALU.add)
                qkb = self.rot('ml_qkb', [64, 8, GW], BF16, 2)
                self.act(qkb[:], qkc[:], AF.Silu)
                self.act(qkb[:, 4:8, :], qkb[:, 4:8, :], AF.Copy, scale=0.125)
                vraw = self.rot('ml_vraw', [64, NJ, 256], F32, 1)
                self.ld(vraw[:], self.s_tm[g * GW:(g + 1) * GW, 512:768].rearrange("(j p) n -> p j n", p=64))
                vo = self.rot('ml_oraw', [64, NJ, 256], F32, 2)
                self.ld(vo[:], self.s_tm[g * GW:(g + 1) * GW, 768:1024].rearrange("(j p) n -> p j n", p=64))
                Vaug = self.rot('ml_Vaug', [64, NJ, 4, 65], BF16, 2)
                self.memset('pool', Vaug[:, :, :, 64:65], 1.0)
                self.cp('pool', Vaug[:, :, :, 0:64], vraw[:].rearrange("p j (h d) -> p j h d", d=64))
                CTX[g] = dict(TP=TP, TPw=TPw, carry=carry, qkb=qkb, Vaug=Vaug, vo=vo)

            def tail(g):
                nonlocal Chat
                c_ = CTX.pop(g)
                TP, TPw, carry, qkb, Vaug, vo = (c_[n_] for n_ in ('TP', 'TPw', 'carry', 'qkb', 'Vaug', 'vo'))
                accs = self.rot('ml_accs', [64, NJ, 4, 65], F32, 1)
                for j in range(NJ):
                    t0 = j * 64
                    pk = self.ps('tk')
                    pkb = pk.bitcast(BF16)
                    for h in range(4):
                        self.tr(pkb[0:64, h * 64:(h + 1) * 64], qkb[0:64, 4 + h, t0:t0 + 64], self.identb[0:64, 0:64])
                    Khat = self.rot('ml_Khat', [64, 4, 64], BF16, 2)
                    wb_ = bcast(TPw[:, j, :].unsqueeze(2), 2, 64)
                    for h in range(4):
                        self.act(Khat[:, h, :], pkb[0:64, h * 64:(h + 1) * 64], AF.Copy, scale=TPw[:, j, h:h + 1], rk=[kname(pk[:]), kname(TP[:])])
                    pq = self.ps('qk')
                    for h in range(4):
                        self.mm(pq[0:64, h * 64:(h + 1) * 64], qkb[0:64, 4 + h, t0:t0 + 64], qkb[0:64, h, t0:t0 + 64])
                    qkw = self.rot('ml_qkw', [64, 4, 64], BF16, 2)
                    self.tt('dve', qkw[:], pq[0:64, 0:256].rearrange("p (h d) -> p h d", d=64), wb_, ALU.mult, rk=[kname(pq[:]), kname(TP[:])])
                    self.tt('pool', qkw[:], qkw[:], mask4[:], ALU.mult)
                    Cs = self.rot('ml_Cs', [64, 4, 65], F32, 2)
                    self.tt('dve', Cs[:], Chat[:], bcast(carry[:, :, j:j + 1], 2, 65), ALU.mult, rk=[kname(Chat[:]), kname(TP[:])])
                    Csb = self.rot('ml_Csb', [64, 4, 65], BF16, 2)
                    self.cp('act', Csb[:], Cs[:])
                    pa = self.ps('acc')
                    for h in range(4):
                        self.mm(pa[0:64, h * 65:(h + 1) * 65], qkb[0:64, h, t0:t0 + 64], Csb[0:64, h, :], start=True, stop=False)
                        self.mm(pa[0:64, h * 65:(h + 1) * 65], qkw[0:64, h, :], Vaug[0:64, j, h, :], start=False, stop=True)
                    pP = self.ps('P')
                    for h in range(4):
                        self.mm(pP[0:64, h * 65:(h + 1) * 65], Khat[0:64, h, :], Vaug[0:64, j, h, :])
                    Chat = self.rot('ml_Chat', [64, 4, 65], F32, 3)
                    self.tt('dve', Chat[:], Cs[:], pP[0:64, 0:260].rearrange("p (h d) -> p h d", d=65), ALU.add)
                    self.cp('act', accs[:, j, :, :], pa[0:64, 0:260].rearrange("p (h d) -> p h d", d=65))
                NB = NJ * 4
                av = accs[:].rearrange("p j h d -> p (j h) d")
                dn = self.rot('ml_dn', [64, NB, 1], F32, 1)
                self.stt('dve', dn[:], av[:, :, 64:65], -1.0, av[:, :, 64:65], ALU.mult, ALU.max)
                self.tt('dve', dn[:], dn[:], TP[:, 64:64 + NB].unsqueeze(2), ALU.max)
                self.em.op('dve', lambda e, dn=dn: e.reciprocal(out=dn[:], in_=dn[:]), reads=[kname(dn[:])], writes=[kname(dn[:])])
                hh_ = self.rot('ml_hh', [64, NB, 64], F32, 1)
                self.tt('dve', hh_[:], av[:, :, 0:64], bcast(dn[:], 2, 64), ALU.mult)
                s1 = self.rot('ml_s1', [64, NB, 1], F32, 1)
                self.em.op('dve', lambda e, s1=s1, hh_=hh_: e.tensor_reduce(out=s1[:, :, 0], in_=hh_[:], axis=AX.X, op=ALU.add), reads=[kname(hh_[:])], writes=[kname(s1[:])])
                self.ts('dve', s1[:], s1[:], 1.0 / 64, None, ALU.mult)
                self.tt('pool', hh_[:], hh_[:], bcast(s1[:], 2, 64), ALU.subtract)
                sq = self.rot('ml_sq', [64, NB, 64], F32, 1)
                self.tt('pool', sq[:], hh_[:], hh_[:], ALU.mult)
                s2 = self.rot('ml_s2', [64, NB, 1], F32, 1)
                self.em.op('dve', lambda e, s2=s2, sq=sq: e.tensor_reduce(out=s2[:, :, 0], in_=sq[:], axis=AX.X, op=ALU.add), reads=[kname(sq[:])], writes=[kname(s2[:])])
                self.act(s2[:], s2[:], AF.Sqrt, scale=1.0 / 64, bias=1e-6)
                self.em.op('dve', lambda e, s2=s2: e.reciprocal(out=s2[:], in_=s2[:]), reads=[kname(s2[:])], writes=[kname(s2[:])])
                self.tt('dve', hh_[:], hh_[:], bcast(s2[:], 2, 64), ALU.mult)
                sgo_v = sq[:].rearrange("p (j h) d -> p j (h d)", h=4)
                self.act(sgo_v, vo[:], AF.Sigmoid)
                hv = hh_[:].rearrange("p (j h) d -> p j (h d)", h=4)
                self.tt('pool', sgo_v, sgo_v, bcast(hng[:].unsqueeze(1), 1, NJ), ALU.mult)
                self.tt('dve', sgo_v, sgo_v, hv, ALU.mult)
                self.st(self.s_y[g * GW:(g + 1) * GW, 768:1024].rearrange("(j p) n -> p j n", p=64), sgo_v, wk=[('s_y_c', g)])


            CTX = {}
            self.emit_interleaved(self.capture(prepare, 0), [])
            for g in range(NGm):
                nxt = self.capture(prepare, g + 1) if g + 1 < NGm else []
                tl = self.capture(tail, g)
                self.emit_interleaved(nxt, tl)
    def phase_rwkv(self, l, scoped=True, pools=None):
        I = self.I
        with (self.scope() if scoped else contextlib.nullcontext()):
            self.set_pools(**(pools or dict(a=[0, 1, 2, 3], c=[4, 5, 6, 7])))
            em = self.em
            GW = 128
            NJ = GW // 64
            NGR = S // GW
            NB = NJ * 4
            hp = lambda v: v.rearrange("(h p) -> p h", p=64)
            mu = I["rwkv_mu"][l]
            mu3 = em.sb([64, 3, 4], F32, "rw_mu3")
            for X in range(3):
                self.ld(mu3[:, X, :], hp(mu[X * 256:(X + 1) * 256]), allow_slow_non_contiguous=True)
            mu_w = em.sb([64, 1], F32, "rw_muw")
            mu_a = em.sb([64, 1], F32, "rw_mua")
            mu_g = em.sb([128, 1], F32, "rw_mug")
            self.ld(mu_w[:], mu[768:832].rearrange("(p o) -> p o", o=1))
            self.ld(mu_a[:], mu[832:896].rearrange("(p o) -> p o", o=1))
            self.ld(mu_g[:], mu[896:1024].rearrange("(p o) -> p o", o=1))
            def hvec(name):
                t = em.sb([64, 4], F32, "rw_" + name)
                self.ld(t[:], hp(I["rwkv_" + name][l]), allow_slow_non_contiguous=True)
                return t
            w0, a0, k_k, k_a, r_k = hvec("w0"), hvec("a0"), hvec("k_k"), hvec("k_a"), hvec("r_k")
            omka = em.sb([64, 4], F32, "rw_omka")
            self.ts('dve', omka[:], k_a[:], -1.0, 1.0, ALU.mult, ALU.add)
            w2 = em.sb([64, 256], F32, "rw_w2")
            a2 = em.sb([64, 256], F32, "rw_a2")
            g2 = em.sb([128, 256], F32, "rw_g2")
            self.ld(w2[:], I["rwkv_w2"][l])
            self.ld(a2[:], I["rwkv_a2"][l])
            self.ld(g2[:], I["rwkv_g2"][l])
            if l > 0:
                v0 = em.sb([64, 4], F32, "rw_v0")
                self.ld(v0[:], hp(I["rwkv_v0"][l - 1]), allow_slow_non_contiguous=True)
                v1 = em.sb([64, 4, 32], F32, "rw_v1")
                self.ld(v1[:], I["rwkv_v1"][l - 1].rearrange("(h p) r -> p h r", p=64))
                v2 = em.sb([32, 256], F32, "rw_v2")
                self.ld(v2[:], I["rwkv_v2"][l - 1])
            gng = self.rowbc(I["rwkv_gn_g"][l], 256, 64, "rw_gng")
            gnb = self.rowbc(I["rwkv_gn_b"][l], 256, 64, "rw_gnb")
            sl4 = em.sb([64, 4, 64], F32, "rw_sl4")
            su4 = em.sb([64, 4, 64], F32, "rw_su4")
            sui4 = em.sb([64, 4, 64], F32, "rw_sui4")
            for t_, pat, base, cm in ((sl4, [[0, 4], [-1, 64]], -1, 1), (su4, [[0, 4], [1, 64]], -1, -1), (sui4, [[0, 4], [1, 64]], 0, -1)):
                self.memset('pool', t_[:], 1.0)
                self.asel(t_[:], pat, ALU.is_ge, 0.0, base, cm)
            segm = em.sb([64, 4 * GW], F32, "rw_segm")
            self.memset('pool', segm[:], 1.0)
            self.memset('pool', segm[:].rearrange("p (n s) -> p n s", s=64)[:, :, 0:1], 0.0)
            ones64 = em.sb([64, 64], F32, "rw_ones64")
            self.memset('pool', ones64[:], 1.0)
            id64 = self.identf[0:64, 0:64]
            id64b = self.identb[0:64, 0:64]
            I8 = em.sb([64, NB, 64], F32, "rw_I8")
            for b_ in range(NB):
                self.cp('pool', I8[:, b_, :], id64)
            ST = em.sb([64, 4, 64], F32, "rw_ST0")
            self.memset('dve', ST[:], 0.0)
            STb = em.sb([64, 4, 64], BF16, "rw_STb0")
            self.memset('dve', STb[:], 0.0)
            bc3 = lambda t_: bcast(t_[:].unsqueeze(2), 2, GW)
            NEG = -0.6065306597126334
            def prepare(g):
                c0 = PAD + g * GW
                tok0 = g * GW
                raw = self.rot('rw_raw', [64, 3, 4, GW + 1], F32, 1)
                for X in range(3):
                    self.ld(raw[:, X, :, :], self.s_fm[X * 256:(X + 1) * 256, c0 - 1:c0 + GW].rearrange("(h p) n -> p h n", p=64))
                rwa = self.rot('rw_rwa', [64, 2, GW + 1], F32, 1)
                self.ld(rwa[:], self.s_fm[768:896, c0 - 1:c0 + GW].rearrange("(x p) n -> p x n", p=64))
                rg = self.rot('rw_rg', [128, GW + 1], F32, 1)
                self.ld(rg[:], self.s_fm[896:1024, c0 - 1:c0 + GW])
                L3 = self.rot('rw_L3', [64, 3, 4, GW], F32, 1)
                for X in range(3):
                    d = self.rot('rw_d', [64, 4, GW], F32, 1)
                    self.tt('dve', d[:], raw[:, X, :, 0:GW], raw[:, X, :, 1:GW + 1], ALU.subtract)
                    self.tt('pool', d[:], d[:], bc3(mu3[:, X, :]), ALU.mult)
                    self.tt('dve', L3[:, X, :, :], d[:], raw[:, X, :, 1:GW + 1], ALU.add)
                r_, k_, v_ = L3[:, 0, :, :], L3[:, 1, :, :], L3[:, 2, :, :]
                xwa = self.rot('rw_xwa', [64, 2, GW], F32, 1)
                for X, m_ in ((0, mu_w), (1, mu_a)):
                    d = self.rot('rw_d1', [64, GW], F32, 2)
                    self.tt('dve', d[:], rwa[:, X, 0:GW], rwa[:, X, 1:GW + 1], ALU.subtract)
                    self.stt('dve', xwa[:, X, :], d[:], m_[:, 0:1], rwa[:, X, 1:GW + 1], ALU.mult, ALU.add)
                xg = self.rot('rw_xg', [128, GW], F32, 1)
                dg = self.rot('rw_dg', [128, GW], F32, 1)
                self.tt('dve', dg[:], rg[:, 0:GW], rg[:, 1:GW + 1], ALU.subtract)
                self.stt('dve', xg[:], dg[:], mu_g[:, 0:1], rg[:, 1:GW + 1], ALU.mult, ALU.add)
                self.act(xwa[:, 0, :], xwa[:, 0, :], AF.Tanh)
                self.act(xg[:], xg[:], AF.Sigmoid)
                lw = self.rot('rw_lw', [64, 4, GW], F32, 1)
                a_ = self.rot('rw_a', [64, 4, GW], F32, 1)
                gT = self.rot('rw_gT', [64, 4, GW], F32, 1)
                for h in range(4):
                    p1 = self.ps('a')
                    self.mm(p1[0:64, 0:GW], w2[:, h * 64:(h + 1) * 64], xwa[:, 0, :])
                    self.act(lw[:, h, :], p1[0:64, 0:GW], AF.Sigmoid, bias=w0[:, h:h + 1])
                    p2 = self.ps('a')
                    self.mm(p2[0:64, 0:GW], a2[:, h * 64:(h + 1) * 64], xwa[:, 1, :])
                    self.act(a_[:, h, :], p2[0:64, 0:GW], AF.Sigmoid, bias=a0[:, h:h + 1])
                    p3 = self.ps('a')
                    self.mm(p3[0:64, 0:GW], g2[:, h * 64:(h + 1) * 64], xg[:, :])
                    self.cp('act', gT[:, h, :], p3[0:64, 0:GW])
                if l > 0:
                    p4 = self.ps('a')
                    for h in range(4):
                        self.mm(p4[0:32, 0:GW], v1[:, h, :], L3[:, 2, h, :], start=(h == 0), stop=(h == 3))
                    t1 = self.rot('rw_t1', [32, GW], F32, 1)
                    self.cp('act', t1[:], p4[0:32, 0:GW])
                    sgv = self.rot('rw_sgv', [64, 4, GW], F32, 1)
                    for h in range(4):
                        p5 = self.ps('a')
                        self.mm(p5[0:64, 0:GW], v2[:, h * 64:(h + 1) * 64], t1[:, :])
                        self.act(sgv[:, h, :], p5[0:64, 0:GW], AF.Sigmoid, bias=v0[:, h:h + 1])
                    vf = self.rot('rw_vf', [64, 4, GW], F32, 1)
                    self.ld(vf[:], self.vfirst[:, tok0:tok0 + GW].rearrange("(h p) n -> p h n", p=64))
                    self.tt('dve', vf[:], vf[:], v_, ALU.subtract)
                    self.tt('pool', vf[:], vf[:], sgv[:], ALU.mult)
                    self.tt('dve', v_, v_, vf[:], ALU.add)
                else:
                    self.st(self.vfirst[:, tok0:tok0 + GW].rearrange("(h p) n -> p h n", p=64), v_, wk=[('vfirst', g)])
                kk = self.rot('rw_kk', [64, 4, GW], F32, 1)
                self.tt('dve', kk[:], k_, bc3(k_k), ALU.mult)
                sq = self.rot('rw_sq', [64, 4, GW], F32, 1)
                self.tt('pool', sq[:], kk[:], kk[:], ALU.mult)
                nr = self.rot('rw_nr', [64, 4, GW], F32, 1)
                for h in range(4):
                    p6 = self.ps('a')
                    self.mm(p6[0:64, 0:GW], ones64[:, :], sq[:, h, :])
                    self.act(nr[:, h, :], p6[0:64, 0:GW], AF.Ln, bias=1e-30)
                self.act(nr[:], nr[:], AF.Exp, scale=-0.5)
                self.tt('dve', kk[:], kk[:], nr[:], ALU.mult)
                k2 = self.rot('rw_k2', [64, 4, GW], F32, 1)
                self.tt('pool', k2[:], a_[:], bc3(k_a), ALU.mult)
                self.tt('pool', k2[:], k2[:], bc3(omka), ALU.add)
                self.tt('dve', k2[:], k2[:], k_, ALU.mult)
                b_ = self.rot('rw_b', [64, 4, GW], F32, 1)
                self.tt('pool', b_[:], kk[:], a_[:], ALU.mult)
                cw = self.rot('rw_cw', [64, 4, GW], F32, 1)
                self.em.op('dve', lambda e, cw=cw, lw=lw: e.tensor_tensor_scan(out=cw[:].rearrange("p h n -> p (h n)"), data0=segm[:], data1=lw[:].rearrange("p h n -> p (h n)"),
                                                                               initial=0.0, op0=ALU.mult, op1=ALU.add),
                           reads=[kname(segm[:]), kname(lw[:])], writes=[kname(cw[:])])
                ep = self.rot('rw_ep', [64, 4, GW], F32, 2)
                en = self.rot('rw_en', [64, 4, GW], F32, 1)
                epv = self.rot('rw_epv', [64, 4, GW], F32, 1)
                self.act(ep[:], cw[:], AF.Exp, scale=NEG)
                self.act(en[:], cw[:], AF.Exp, scale=-NEG)
                self.tt('dve', epv[:], cw[:], lw[:], ALU.subtract)
                self.act(epv[:], epv[:], AF.Exp, scale=NEG)
                AR = self.rot('rw_AR', [64, 2, 4, GW], BF16, 2)
                at = AR[:, 0, :, :]
                rt = AR[:, 1, :, :]
                kt = self.rot('rw_kt', [64, 4, GW], BF16, 1)
                bt = self.rot('rw_bt', [64, 4, GW], BF16, 1)
                self.tt('dve', rt[:], r_, ep[:], ALU.mult)
                self.tt('pool', kt[:], k2[:], en[:], ALU.mult)
                self.tt('dve', bt[:], b_[:], en[:], ALU.mult)
                self.stt('dve', at[:], kk[:], -1.0, epv[:], ALU.mult, ALU.mult)
                rk = self.rot('rw_rk', [64, 4, GW], F32, 1)
                self.tt('pool', rk[:], r_, k2[:], ALU.mult)
                self.tt('pool', rk[:], rk[:], bc3(r_k), ALU.mult)
                pb = self.ps('a')
                for j in range(NJ):
                    for h in range(4):
                        self.mm(pb[0:64, j * 4 + h:j * 4 + h + 1], rk[:, h, j * 64:(j + 1) * 64], ones64[:, 0:1])
                bon = self.rot('rw_bon', [64, NB, 1], F32, 2)
                self.cp('act', bon[:, :, 0], pb[0:64, 0:NB])
                TM = {}
                for nm, src_ in (('B', bt[:]), ('K', kt[:]), ('V', v_), ('G', gT[:])):
                    isb = nm in ('B', 'K')
                    dst_ = self.rot('rw_tm' + nm, [64, NJ, 4, 64], BF16 if isb else F32, 2)
                    for j in range(NJ):
                        pt = self.ps('a')
                        ptv = pt.bitcast(BF16) if isb else pt
                        for h in range(4):
                            self.tr(ptv[0:64, h * 64:(h + 1) * 64], src_[:, h, j * 64:(j + 1) * 64], id64b if isb else id64)
                        self.cp('act' if j % 2 == 0 else 'dve', dst_[:, j, :, :], ptv[0:64, 0:256].rearrange("p (h d) -> p h d", d=64))
                    TM[nm] = dst_
                Vb = self.rot('rw_Vb', [64, NJ, 4, 64], BF16, 2)
                self.cp('act', Vb[:], TM['V'][:])
                CM = {}
                for nm in ('A', 'Bm', 'Ak', 'Rb', 'Rk'):
                    CM[nm] = self.rot('rw_cm' + nm, [64, NJ, 4, 64], BF16, 2)
                for j in range(NJ):
                    cs = slice(j * 64, (j + 1) * 64)
                    pt = self.ps('a')
                    for h in range(4):
                        self.mm(pt[0:64, h * 64:(h + 1) * 64], at[:, h, cs], bt[:, h, cs])
                    self.tt('dve', CM['A'][:, j, :, :], pt[0:64, 0:256].rearrange("p (h d) -> p h d", d=64), sl4[:], ALU.mult)
                    for lh, n0, n1 in ((bt, 'Bm', 'Rb'), (kt, 'Ak', 'Rk')):
                        pt = self.ps('a')
                        for h in range(4):
                            self.mm(pt[0:64, h * 128:(h + 1) * 128], lh[:, h, cs], AR[:, :, h, cs])
                        pv = pt[0:64, 0:512].rearrange("p (h x d) -> p h x d", x=2, d=64)
                        self.tt('dve', CM[n0][:, j, :, :], pv[:, :, 0, :], su4[:], ALU.mult)
                        self.tt('pool' if False else 'dve', CM[n1][:, j, :, :], pv[:, :, 1, :], sui4[:], ALU.mult)
                P = CM['Bm'][:].rearrange("p j h d -> p (j h) d")
                Q = CM['A'][:].rearrange("p j h d -> p (j h) d")
                M = self.rot('rw_M', [64, NB, 64], F32, 2)
                self.tt('pool', M[:], I8[:], P, ALU.add)
                M = M[:]
                Mb = self.rot('rw_Mb', [64, NB, 64], BF16, 2)
                self.cp('act', Mb[:], M)
                Mb = Mb[:]
                for lev in range(5):
                    last = (lev == 4)
                    pQ = self.ps('a')
                    for b in range(NB):
                        self.mm(pQ[0:64, b * 64:(b + 1) * 64], P[:, b, :], Q[:, b, :])
                    if not last:
                        pP = self.ps('a')
                        for b in range(NB):
                            self.mm(pP[0:64, b * 64:(b + 1) * 64], Q[:, b, :], P[:, b, :])
                    Qn = self.rot('rw_Qn', [64, NB, 64], BF16, 2)
                    self.cp('act', Qn[:], pQ[0:64, 0:NB * 64].rearrange("p (b d) -> p b d", d=64))
                    if not last:
                        Pn = self.rot('rw_Pn', [64, NB, 64], BF16, 2)
                        self.cp('act', Pn[:], pP[0:64, 0:NB * 64].rearrange("p (b d) -> p b d", d=64))
                        P = Pn[:]
                    Q = Qn[:]
                    pM = self.ps('a')
                    for b in range(NB):
                        self.mm(pM[0:64, b * 64:(b + 1) * 64], Q[:, b, :], Mb[:, b, :])
                    Mn = self.rot('rw_M', [64, NB, 64], F32, 2)
                    self.tt('dve', Mn[:], M, pM[0:64, 0:NB * 64].rearrange("p (b d) -> p b d", d=64), ALU.add)
                    M = Mn[:]
                    Mbn = self.rot('rw_Mb', [64, NB, 64], BF16, 2)
                    self.cp('act', Mbn[:], M)
                    Mb = Mbn[:]
                TTt = self.rot('rw_TT', [64, NB, 64], BF16, 2)
                self.cp('act', TTt[:], M)
                TT = TTt[:]
                CTX[g] = dict(CM=CM, TM=TM, Vb=Vb, at=at, rt=rt, TT=TT, ep=ep, bon=bon, tok0=tok0)

            def tail(g):
                nonlocal ST, STb
                c_ = CTX.pop(g)
                CM, TM, Vb, at, rt, TT, ep, bon, tok0 = (c_[n_] for n_ in ('CM', 'TM', 'Vb', 'at', 'rt', 'TT', 'ep', 'bon', 'tok0'))
                Ysb = self.rot('rw_Ysb', [64, NJ, 4, 64], F32, 1)
                V_, B_, K_ = Vb, TM['B'], TM['K']
                Vf_ = TM['V']
                for j in range(NJ):
                    cs = slice(j * 64, (j + 1) * 64)
                    pX = self.ps('c')
                    for h in range(4):
                        self.mm(pX[0:64, h * 64:(h + 1) * 64], CM['Ak'][:, j, h, :], V_[:, j, h, :], start=True, stop=False)
                        self.mm(pX[0:64, h * 64:(h + 1) * 64], at[:, h, cs], STb[:, h, :], start=False, stop=True)
                    Xsb = self.rot('rw_Xsb', [64, 4, 64], BF16, 2)
                    self.cp('act', Xsb[:], pX[0:64, 0:256].rearrange("p (h d) -> p h d", d=64))
                    pU = self.ps('c')
                    for h in range(4):
                        self.mm(pU[0:64, h * 64:(h + 1) * 64], TT[:, j * 4 + h, :], Xsb[:, h, :])
                    Usb = self.rot('rw_Usb', [64, 4, 64], BF16, 2)
                    self.cp('act', Usb[:], pU[0:64, 0:256].rearrange("p (h d) -> p h d", d=64))
                    pS = self.ps('c')
                    for h in range(4):
                        self.mm(pS[0:64, h * 64:(h + 1) * 64], B_[:, j, h, :], Usb[:, h, :], start=True, stop=False)
                        self.mm(pS[0:64, h * 64:(h + 1) * 64], K_[:, j, h, :], V_[:, j, h, :], start=False, stop=True)
                    pY = self.ps('c')
                    for h in range(4):
                        self.mm(pY[0:64, h * 64:(h + 1) * 64], rt[:, h, cs], STb[:, h, :], start=True, stop=False)
                        self.mm(pY[0:64, h * 64:(h + 1) * 64], CM['Rb'][:, j, h, :], Usb[:, h, :], start=False, stop=False)
                        self.mm(pY[0:64, h * 64:(h + 1) * 64], CM['Rk'][:, j, h, :], V_[:, j, h, :], start=False, stop=True)
                    STn = self.rot('rw_ST', [64, 4, 64], F32, 3)
                    self.tt('dve', STn[:], pS[0:64, 0:256].rearrange("p (h d) -> p h d", d=64), ST[:], ALU.add)
                    self.tt('dve', STn[:], STn[:], bcast(ep[:, :, j * 64 + 63:j * 64 + 64], 2, 64), ALU.mult)
                    ST = STn
                    STb = self.rot('rw_STb', [64, 4, 64], BF16, 3)
                    self.cp('act', STb[:], ST[:])
                    self.cp('act', Ysb[:, j, :, :], pY[0:64, 0:256].rearrange("p (h d) -> p h d", d=64))
                yv = Ysb[:].rearrange("p j h d -> p (j h) d")
                s1 = self.rot('rw_s1', [64, NB, 1], F32, 1)
                self.em.op('dve', lambda e, s1=s1, yv=yv: e.tensor_reduce(out=s1[:, :, 0], in_=yv, axis=AX.X, op=ALU.add), reads=[kname(Ysb[:])], writes=[kname(s1[:])])
                self.ts('dve', s1[:], s1[:], 1.0 / 64, None, ALU.mult)
                yc = self.rot('rw_yc', [64, NB, 64], F32, 1)
                self.tt('dve', yc[:], yv, bcast(s1[:], 2, 64), ALU.subtract)
                sq2 = self.rot('rw_sq2', [64, NB, 64], F32, 1)
                self.tt('pool', sq2[:], yc[:], yc[:], ALU.mult)
                s2 = self.rot('rw_s2', [64, NB, 1], F32, 1)
                self.em.op('dve', lambda e, s2=s2, sq2=sq2: e.tensor_reduce(out=s2[:, :, 0], in_=sq2[:], axis=AX.X, op=ALU.add), reads=[kname(sq2[:])], writes=[kname(s2[:])])
                self.act(s2[:], s2[:], AF.Sqrt, scale=1.0 / 64, bias=64e-5)
                self.em.op('dve', lambda e, s2=s2: e.reciprocal(out=s2[:], in_=s2[:]), reads=[kname(s2[:])], writes=[kname(s2[:])])
                self.tt('dve', yc[:], yc[:], bcast(s2[:], 2, 64), ALU.mult)
                y4 = yc[:].rearrange("p (j h) d -> p j (h d)", h=4)
                self.tt('pool', y4, y4, bcast(gng[:].unsqueeze(1), 1, NJ), ALU.mult)
                self.tt('pool', y4, y4, bcast(gnb[:].unsqueeze(1), 1, NJ), ALU.add)
                bv = self.rot('rw_bv', [64, NB, 64], F32, 1)
                self.tt('dve', bv[:], Vf_[:].rearrange("p j h d -> p (j h) d"), bcast(bon[:], 2, 64), ALU.mult)
                self.tt('dve', yc[:], yc[:], bv[:], ALU.add)
                self.tt('pool', yc[:], yc[:], TM['G'][:].rearrange("p j h d -> p (j h) d"), ALU.mult)
                self.st(self.s_y[tok0:tok0 + GW, 0:256].rearrange("(j p) n -> p j n", p=64), y4, rk=[kname(yc[:])], wk=[('s_y_a', g)])

            CTX = {}
            cur = self.capture(prepare, 0)
            self.emit_interleaved(cur, [])
            for g in range(NGR):
                nxt = self.capture(prepare, g + 1) if g + 1 < NGR else []
                tl = self.capture(tail, g)
                self.emit_interleaved(nxt, tl, self.rw_bfrac)

    def phase_rwkv_mlstm(self, l):
        with self.scope():
            A = self.capture(self.phase_rwkv, l, False, dict(a=[0, 1, 2], c=[3, 4]))
            B = self.capture(self.phase_mlstm, l, False, dict(misc=[5], tk=[6], acc=[6], qk=[7], P=[7]), 256)
            self.emit_interleaved(A, B, self.rm_bfrac)

    def phase_moba_merge(self, l):
        I = self.I
        with self.scope():
            wgt = self.em.sb([128, 8, 3072], BF16, "wgt_pf")
            wbr = self.em.sb([128, 8, D], BF16, "wbr_pf")
            wout = self.em.sb([128, 8, D], BF16, "wout_pf")
            g_pre = self.colvec(I["ln_mix_pre"][l], 8, "g_pre_pf")

            def loader():
                for cb in range(6):
                    self.load_w(I["w_in"][l][:, GT_OFF + cb * 512:GT_OFF + (cb + 1) * 512], 8, 512, gcol=g_pre,
                                dst=wgt[:, :, cb * 512:(cb + 1) * 512], kchunk=2, nbuf=2, engs=('dve',), tag='pf')
                for (nm, k0, kn) in (("w_br_rwkv", 0, 2), ("w_br_moba", 2, 4), ("w_br_mlstm", 6, 2)):
                    for cb in range(2):
                        self.load_w(I[nm][l][:, cb * 512:(cb + 1) * 512], kn, 512, dst=wbr[:, k0:k0 + kn, cb * 512:(cb + 1) * 512],
                                    kchunk=2, nbuf=2, engs=('dve',), tag='pf')
                for cb in range(2):
                    self.load_w(I["w_out"][l][:, cb * 512:(cb + 1) * 512], 8, 512, dst=wout[:, :, cb * 512:(cb + 1) * 512],
                                kchunk=2, nbuf=2, engs=('dve',), tag='pf')

            self.phase_moba(l, extra=loader)
            self.phase_merge(l, wgt_pf=wgt, wbr_pf=wbr, wout_pf=wout)


from concourse.bass_utils import run_bass_kernel_spmd


def build_program():
    kb = KB()
    for l in range(2):
        kb.phase_in(l)
        kb.phase_rwkv_mlstm(l)
        kb.phase_moba_merge(l)
        kb.phase_ffn(l)
        if l == 1:
            kb.phase_ple(l, kb.out, "out")
        else:
            kb.phase_ple(l, kb.xres, "xres")
    kb.em.build()
    return kb


def kernel(**inputs):
    kb = build_program()
    in_maps = []
    for b in range(8):
        m = {}
        for name, shape in IN_SPECS:
            a = np.asarray(inputs[name], dtype=np.float32)
            if name == "x":
                a = a[b]
            elif name == "p":
                a = a[:, b]
            m[name] = np.ascontiguousarray(a)
        in_maps.append(m)
    res = run_bass_kernel_spmd(kb.nc, in_maps, core_ids=list(range(8)))
    return np.stack([np.asarray(r["out"], dtype=np.float32) for r in res.results], axis=0)
```

```python
import contextlib
import numpy as np
import concourse.bass as bass
import concourse.mybir as mybir

F32 = mybir.dt.float32
BF16 = mybir.dt.bfloat16
AF = mybir.ActivationFunctionType
ALU = mybir.AluOpType
AX = mybir.AxisListType

SAME_ENGINE_RAW = True


class Em:
    def __init__(self, nc, ndma=8):
        self.nc = nc
        self.engs = ['pe', 'act', 'dve', 'pool', 'sp']
        self.prog = {e: [] for e in self.engs}
        self.cnt = {e: 0 for e in self.engs}
        self.seen = {e: {} for e in self.engs}
        self.res = {}
        self.ndma = ndma
        self.slotval = {}
        self.dnext = {e: 0 for e in self.engs}
        self.stack = contextlib.ExitStack()
        self.nalloc = 0
        self.psum_banks = []
        self.psum_next = 0
        self.epoch = 0

    def sb(self, shape, dtype=F32, name=None):
        self.nalloc += 1
        name = f"{name or 'sb'}_{self.nalloc}"
        return self.stack.enter_context(self.nc.sbuf_tensor(name, list(shape), dtype))

    def init_psum(self):
        for i in range(8):
            t = self.stack.enter_context(self.nc.psum_tensor(f"psb{i}", [128, 512], F32))
            self.psum_banks.append(t)

    def psum(self):
        i = self.psum_next
        self.psum_next = (i + 1) % 8
        return self.psum_banks[i], ('ps', i)

    def _wait(self, eng, tok, kind):
        if tok is None:
            return
        sk, val = tok
        if sk[0] == 'e' and sk[1] == eng:
            if eng in ('pe', 'sp'):
                return
            if not SAME_ENGINE_RAW:
                return
        if self.seen[eng].get(sk, 0) >= val:
            return
        self.seen[eng][sk] = val
        self.prog[eng].append(('wait', sk, val))

    def _deps(self, eng, reads, writes):
        for r in reads:
            e = self.res.get(r)
            if e is not None:
                self._wait(eng, e[0], 'raw')
        for w in writes:
            e = self.res.get(w)
            if e is not None:
                self._wait(eng, e[0], 'waw')
                for sk, val in e[1].items():
                    self._wait(eng, (sk, val), 'war')

    def _mark(self, tok, reads, writes):
        sk, val = tok
        for r in reads:
            e = self.res.setdefault(r, [None, {}])
            e[1][sk] = max(e[1].get(sk, 0), val)
        for w in writes:
            self.res[w] = [tok, {}]

    def op(self, eng, fn, reads=(), writes=(), inc=True):
        self._deps(eng, reads, writes)
        tok = (('e', eng, self.epoch), self.cnt[eng] + 1)
        if inc:
            self.cnt[eng] += 1
        self.prog[eng].append(('op', fn, inc, ('e', eng, self.epoch)))
        self._mark(tok, reads, writes)

    def dma(self, issuer, out, in_, reads=(), writes=(), **kw):
        self._deps(issuer, reads, writes)
        slot = self.dnext[issuer]
        self.dnext[issuer] = (slot + 1) % self.ndma
        sk = ('dma', issuer, slot)
        cur = self.slotval.get(sk, 0)
        if cur:
            self._wait(issuer, (sk, cur), 'raw')
        self.slotval[sk] = cur + 16
        tok = (sk, cur + 16)
        self.prog[issuer].append(('dma', out, in_, sk, kw))
        self._mark(tok, reads, writes)

    def barrier(self):
        toks = [(sk, v) for sk, v in self.slotval.items()]
        toks += [(('e', e, self.epoch), self.cnt[e]) for e in self.engs if self.cnt[e]]
        for e in self.engs:
            for tok in toks:
                if not (tok[0][0] == 'e' and tok[0][1] == e):
                    self._wait(e, tok, 'raw')
        self.res = {}
        self.epoch += 1
        self.cnt = {e: 0 for e in self.engs}

    def build(self):
        nc = self.nc
        for sk, v in self.slotval.items():
            self._wait('sp', (sk, v), 'raw')
        for e in self.engs:
            if e != 'sp' and self.cnt[e]:
                self._wait('sp', (('e', e, self.epoch), self.cnt[e]), 'raw')
        semkeys = set()
        for e in self.engs:
            for it in self.prog[e]:
                if it[0] == 'wait':
                    semkeys.add(it[1])
                elif it[0] == 'dma':
                    semkeys.add(it[3])
                elif it[0] == 'op' and it[2]:
                    semkeys.add(it[3])
        sems = {}
        for i, sk in enumerate(sorted(semkeys, key=str)):
            nm = "s_" + ("_".join(str(x) for x in sk) if isinstance(sk, tuple) else sk)
            sems[sk] = self.stack.enter_context(nc.semaphore(nm))
        prog = self.prog

        def run(eng_name):
            def body(e):
                for it in prog[eng_name]:
                    if it[0] == 'wait':
                        e.wait_ge(sems[it[1]], it[2])
                    elif it[0] == 'op':
                        ins = it[1](e)
                        if it[2]:
                            ins.then_inc(sems[it[3]], 1)
                    else:
                        e.dma_start(out=it[1], in_=it[2], **it[4]).then_inc(sems[it[3]], 16)
            return body

        with nc.Block() as block:
            if prog['sp']:
                block.sync(run('sp'))
            if prog['pe']:
                block.tensor(run('pe'))
            if prog['act']:
                block.scalar(run('act'))
            if prog['dve']:
                block.vector(run('dve'))
            if prog['pool']:
                block.gpsimd(run('pool'))
        self.stack.close()
        n = {e: len(prog[e]) for e in self.engs}
        return n


S = 4096
D = 1024
NTILE = 32
NG = 8
TG = 512
PAD = 4
DFF = 2816
RW_OFF, MQ_OFF, MK_OFF, MV_OFF = 0, 1024, 1536, 2048
LQ_OFF, LK_OFF, LV_OFF, LO_OFF, LI_OFF, LF_OFF, GT_OFF = 2560, 2816, 3072, 3328, 3584, 3588, 3592
INW = 6664
FM_RW, FM_MQK, FM_LQK, FM_LIF = 0, 1024, 2048, 2560
NFM = 2568

IN_SPECS = [
    ("x", [S, D]), ("p", [2, S, 256]),
    ("ln_mix_pre", [2, D]), ("ln_mix_post", [2, D]), ("ln_ffn_pre", [2, D]), ("ln_ffn_post", [2, D]), ("ln_ple", [2, D]),
    ("w_in", [2, D, INW]), ("rwkv_mu", [2, 1024]), ("rwkv_w0", [2, 256]), ("rwkv_w2", [2, 64, 256]), ("rwkv_a0", [2, 256]),
    ("rwkv_a2", [2, 64, 256]), ("rwkv_g2", [2, 128, 256]), ("rwkv_k_k", [2, 256]), ("rwkv_k_a", [2, 256]), ("rwkv_r_k", [2, 256]),
    ("rwkv_gn_g", [2, 256]), ("rwkv_gn_b", [2, 256]), ("rwkv_v0", [1, 256]), ("rwkv_v1", [1, 256, 32]), ("rwkv_v2", [1, 32, 256]),
    ("mlstm_conv_w", [2, 4, 512]), ("mlstm_conv_b", [2, 512]), ("mlstm_i_b", [2, 4]), ("mlstm_f_b", [2, 4]), ("mlstm_hn_g", [2, 256]),
    ("w_br_rwkv", [2, 256, D]), ("w_br_moba", [2, 512, D]), ("w_br_mlstm", [2, 256, D]), ("w_out", [2, D, D]),
    ("ffn_up", [2, D, 2 * DFF]), ("ffn_conv_w", [2, 3, 2 * DFF]), ("ffn_conv_b", [2, 2 * DFF]), ("ffn_down", [2, DFF, D]),
    ("ple_proj", [2, 256, D]), ("ple_gate", [2, D, D]),
]


def bcast(ap, dim, n):
    l = [list(a) for a in ap.ap]
    l[dim] = [0, n]
    return bass.AP(ap.tensor, ap.offset, l)


def kname(ap):
    return ap.tensor.name


class KB:
    def __init__(self, dbg=None, ext_in=(), ext_out=()):
        self.nc = nc = bass.Bass("TRN2", target_bir_lowering=False)
        self.em = Em(nc)
        self.em.init_psum()
        self.I = {}
        for name, shape in IN_SPECS:
            self.I[name] = nc.dram_tensor(name, shape, F32, kind="ExternalInput").ap()
        self.out = nc.dram_tensor("out", [S, D], F32, kind="ExternalOutput").ap()
        self.dbg = dbg or {}
        self.dbg_out = {}
        mk = lambda n, sh, dt=F32: nc.dram_tensor(n, sh, dt, kind=("ExternalInput" if n in ext_in else "ExternalOutput" if n in ext_out else "Internal")).ap()
        self.xres = mk("xres", [S, D])
        self.s_fm = mk("s_fm", [NFM, PAD + S])
        self.s_tm = mk("s_tm", [S, 1024])
        self.s_y = mk("s_y", [S, 1024])
        self.s_hT = mk("s_hT", [128, 8, S], BF16)
        self.vfirst = mk("vfirst", [256, S])
        self.rot_cache = {}
        self.pp = {}
        self.consts()

    def keys(self, aps, override):
        if override is not None:
            return list(override)
        ks = []
        for a in aps:
            if a is None or isinstance(a, (int, float)):
                continue
            k = kname(a)
            if k not in ks:
                ks.append(k)
        return ks

    def rot(self, name, shape, dtype, n):
        key = (name, self.scope_id)
        if key not in self.rot_cache:
            self.rot_cache[key] = [[self.em.sb(shape, dtype, name=f"{name}_{self.scope_id}_{i}") for i in range(n)], 0]
        ent = self.rot_cache[key]
        t = ent[0][ent[1]]
        ent[1] = (ent[1] + 1) % n
        return t

    scope_id = 0
    scope_ctr = 0

    @contextlib.contextmanager
    def scope(self):
        em = self.em
        old = em.stack
        em.stack = contextlib.ExitStack()
        old_id = self.scope_id
        KB.scope_ctr += 1
        self.scope_id = KB.scope_ctr
        try:
            yield
        finally:
            em.barrier()
            em.stack.close()
            em.stack = old
            self.scope_id = old_id

    def ps(self, pool='g'):
        banks = self.pp.setdefault(pool, {'g': [0, 1, 2, 3, 4, 5, 6, 7]}.get(pool))
        st = self.pp.setdefault(pool + '_i', [0])
        b = banks[st[0] % len(banks)]
        st[0] += 1
        return self.em.psum_banks[b]

    def set_pools(self, **pools):
        for k, v in pools.items():
            self.pp[k] = v
            self.pp[k + '_i'] = [0]

    def tt(self, eng, out, in0, in1, op, rk=None, wk=None):
        self.em.op(eng, lambda e: e.tensor_tensor(out=out, in0=in0, in1=in1, op=op), reads=self.keys([in0, in1], rk), writes=self.keys([out], wk))

    def ts(self, eng, out, in0, s1, s2, op0, op1=None, rk=None, wk=None, accum=None):
        def f(e):
            kw = {}
            if accum is not None:
                kw['accum_out'] = accum
            if op1 is None:
                return e.tensor_scalar(out=out, in0=in0, scalar1=s1, scalar2=s2, op0=op0, **kw)
            return e.tensor_scalar(out=out, in0=in0, scalar1=s1, scalar2=s2, op0=op0, op1=op1, **kw)
        self.em.op(eng, f, reads=self.keys([in0, s1, s2], rk), writes=self.keys([out, accum], wk))

    def stt(self, eng, out, in0, sc, in1, op0, op1, rk=None, wk=None):
        self.em.op(eng, lambda e: e.scalar_tensor_tensor(out=out, in0=in0, scalar=sc, in1=in1, op0=op0, op1=op1),
                   reads=self.keys([in0, sc, in1], rk), writes=self.keys([out], wk))

    def act(self, out, in_, func, bias=None, scale=None, accum=None, rk=None, wk=None, eng='act'):
        def f(e):
            kw = {}
            if bias is not None:
                kw['bias'] = bias
            if scale is not None:
                kw['scale'] = scale
            if accum is not None:
                kw['accum_out'] = accum
            return e.activation(out=out, in_=in_, func=func, **kw)
        self.em.op(eng, f, reads=self.keys([in_, bias, scale], rk), writes=self.keys([out, accum], wk))

    def cp(self, eng, out, in_, rk=None, wk=None):
        if eng == 'act':
            self.em.op(eng, lambda e: e.copy(out=out, in_=in_), reads=self.keys([in_], rk), writes=self.keys([out], wk))
        else:
            self.em.op(eng, lambda e: e.tensor_copy(out=out, in_=in_), reads=self.keys([in_], rk), writes=self.keys([out], wk))

    fp32r = False
    rw_bfrac = 1.0
    rm_bfrac = 1.0

    def mm(self, out, lhsT, rhs, start=True, stop=True, rk=None, wk=None):
        reads = self.keys([lhsT, rhs], rk)
        if self.fp32r and lhsT.dtype == F32 and rhs.dtype == F32:
            lhsT = lhsT.bitcast(mybir.dt.float32r)
            rhs = rhs.bitcast(mybir.dt.float32r)
        self.em.op('pe', lambda e: e.matmul(out, lhsT=lhsT, rhs=rhs, start=start, stop=stop),
                   reads=reads, writes=self.keys([out], wk), inc=stop)

    def tr(self, out, in_, ident, rk=None, wk=None):
        self.em.op('pe', lambda e: e.transpose(out=out, in_=in_, identity=ident), reads=self.keys([in_, ident], rk), writes=self.keys([out], wk))

    def memset(self, eng, ap, val, wk=None):
        self.em.op(eng, lambda e: e.memset(ap, val), writes=self.keys([ap], wk))

    def asel(self, out, pattern, cmp, fill, base, cm):
        self.em.op('pool', lambda e: e.affine_select(out=out, in_=out, pattern=pattern, compare_op=cmp, fill=fill, base=base, channel_multiplier=cm),
                   reads=self.keys([out], None), writes=self.keys([out], None))

    def ld(self, out, in_, rk=None, wk=None, q='sp', **kw):
        self.em.dma(q, out, in_, reads=self.keys([in_], rk), writes=self.keys([out], wk), **kw)

    def st(self, out, in_, rk=None, wk=None, q='pool', **kw):
        self.em.dma(q, out, in_, reads=self.keys([in_], rk), writes=self.keys([out], wk), **kw)

    def capture(self, fn, *args):
        real = self.em

        class _Rec:
            def __init__(s):
                s.items = []

            def op(s, *a, **kw):
                s.items.append(('op', a, kw))

            def dma(s, *a, **kw):
                s.items.append(('dma', a, kw))

            def __getattr__(s, name):
                return getattr(real, name)

        rec = _Rec()
        self.em = rec
        try:
            fn(*args)
        finally:
            self.em = real
        return rec.items

    @staticmethod
    def merge_streams(A, B, bfrac=1.0):
        na, nb = len(A), len(B)
        ia = ib = 0
        out = []
        while ia < na or ib < nb:
            if ib >= nb or (ia < na and ia * nb <= ib * na * bfrac):
                out.append(A[ia])
                ia += 1
            else:
                out.append(B[ib])
                ib += 1
        return out

    def emit_interleaved(self, A, B, bfrac=1.0):
        for it in self.merge_streams(A, B, bfrac):
            getattr(self.em, it[0])(*it[1], **it[2])

    def consts(self):
        em = self.em
        self.identf = em.sb([128, 128], F32, "identf")
        self.identb = em.sb([128, 128], BF16, "identb")
        self.memset('pool', self.identf[:], 0.0)
        self.asel(self.identf[:], [[-1, 128]], ALU.not_equal, 1.0, 0, 1)
        self.cp('pool', self.identb[:], self.identf[:])
        self.zeros = em.sb([128, 512], F32, "zeros")
        self.memset('pool', self.zeros[:], 0.0)
        for r0 in range(0, NFM, 128):
            n = min(128, NFM - r0)
            self.st(self.s_fm[r0:r0 + n, 0:PAD], self.zeros[0:n, 0:PAD], wk=[('s_fm_pad', r0)])

    def colvec(self, src_vec, C, name):
        t = self.em.sb([128, C], F32, name)
        self.ld(t[:], src_vec.rearrange("(c p) -> p c", p=128), allow_slow_non_contiguous=True)
        return t

    def rowbc(self, src_vec, n, P, name):
        t = self.em.sb([P, n], F32, name)
        self.ld(t[:], src_vec.partition_broadcast(P))
        return t

    def norm_group(self, src, srckey, g, hT_dst, hT_key, eps=1e-6):
        for tt in range(4):
            t = g * 4 + tt
            xt = self.rot('ng_x', [128, D], F32, 2)
            self.ld(xt[:], src[t * 128:(t + 1) * 128, :], rk=[(srckey, t)])
            sq = self.rot('ng_sq', [128, D], BF16, 2)
            ss = self.rot('ng_ss', [128, 1], F32, 4)
            self.act(sq[:], xt[:], AF.Square, accum=ss[:])
            rs = self.rot('ng_rs', [128, 1], F32, 4)
            self.act(rs[:], ss[:], AF.Sqrt, scale=1.0 / D, bias=eps)
            self.em.op('dve', lambda e, rs=rs: e.reciprocal(out=rs[:], in_=rs[:]), reads=[kname(rs[:])], writes=[kname(rs[:])])
            hb = self.rot('ng_hb', [128, D], BF16, 2)
            self.ts('dve', hb[:], xt[:], rs[:, 0:1], None, ALU.mult)
            pt = self.ps('tr')
            ptb = pt.bitcast(BF16)
            for c in range(8):
                self.tr(ptb[:, c * 128:(c + 1) * 128], hb[:, c * 128:(c + 1) * 128], self.identb[:])
            self.cp('act' if tt % 2 == 0 else 'dve', hT_dst[:, :, tt * 128:(tt + 1) * 128], ptb[:, 0:1024].rearrange("p (c t) -> p c t", c=8),
                    wk=[hT_key])

    def load_w(self, src, K, ncols, gcol=None, dst=None, dst_key=None, eng='pool', kchunk=None, nbuf=2, engs=('act', 'dve', 'act', 'dve', 'pool'), tag=''):
        if dst is None:
            dst = self.rot(f'wb_{K}_{ncols}{tag}', [128, K, ncols], BF16, nbuf)
            dst = dst[:]
        kc = kchunk or max(1, min(K, 4096 // ncols))
        k0 = 0
        while k0 < K:
            kn = min(kc, K - k0)
            wf = self.rot(f'wf_{kc * ncols}{tag}', [128, kc * ncols], F32, nbuf)
            wfv = wf[:, 0:kn * ncols].rearrange("p (c n) -> p c n", n=ncols)
            self.ld(wfv, src[k0 * 128:(k0 + kn) * 128, :].rearrange("(c p) n -> p c n", p=128))
            for c in range(kn):
                wk = None
                self.cast_rr = getattr(self, 'cast_rr', 0) + 1
                e_ = engs[self.cast_rr % len(engs)]
                if gcol is not None and e_ == 'pool':
                    e_ = 'act' if self.cast_rr % 2 == 0 else 'dve'
                if gcol is not None:
                    if e_ == 'act':
                        self.act(dst[:, k0 + c, :], wfv[:, c, :], AF.Copy, scale=gcol[:, k0 + c:k0 + c + 1], wk=wk)
                    else:
                        self.ts(e_, dst[:, k0 + c, :], wfv[:, c, :], gcol[:, k0 + c:k0 + c + 1], None, ALU.mult, wk=wk)
                else:
                    self.cp(e_, dst[:, k0 + c, :], wfv[:, c, :], wk=wk)
            k0 += kn
        return dst

    def phase_in(self, l):
        I = self.I
        src, srckey = (I["x"], "x") if l == 0 else (self.xres, "xres")
        with self.scope():
            self.set_pools(tr=[0, 1], mm=[2, 3, 4, 5, 6, 7])
            g_pre = self.colvec(I["ln_mix_pre"][l], 8, "g_pre")
            hT = self.em.sb([128, 8, S], BF16, "hT_all")
            def norms(gs):
                for g in gs:
                    self.norm_group(src, srckey, g, hT[:, :, g * TG:(g + 1) * TG], ('hT', g))
                    self.st(self.s_hT[:, :, g * TG:(g + 1) * TG], hT[:, :, g * TG:(g + 1) * TG], rk=[('hT', g)], wk=[('s_hT', g)])

            w_in = I["w_in"][l]

            def fm_iter(wb, r0, cc, g):
                pt = self.ps('mm')
                for k in range(8):
                    self.mm(pt[:, 0:TG], wb[:, k, cc * 128:(cc + 1) * 128], hT[:, k, g * TG:(g + 1) * TG], start=(k == 0), stop=(k == 7),
                            rk=[kname(wb), ('hT', g)])
                ev = self.rot('tm_ev', [128, 512], F32, 4)
                self.cp('act' if g % 2 == 0 else 'dve', ev[:], pt[:, 0:TG])
                self.st(self.s_fm[r0 + cc * 128:r0 + (cc + 1) * 128, PAD + g * TG:PAD + (g + 1) * TG], ev[:], wk=[('s_fm', r0 + cc * 128, g)])

            def first_block_g(wb, g):
                for cc in range(4):
                    fm_iter(wb, 0, cc, g)

            norms([0])
            wb0 = self.load_w(w_in[:, 0:512], 8, 512, gcol=g_pre)
            for g in range(NG):
                nxt = self.capture(norms, [g + 1]) if g + 1 < NG else []
                self.emit_interleaved(nxt, self.capture(first_block_g, wb0, g))
            fm_blocks = [(512, 512, 512), (1024, 512, 1024), (1536, 512, 1536), (LQ_OFF, 512, FM_LQK)]
            for (c0, ncols, r0) in fm_blocks:
                wb = self.load_w(w_in[:, c0:c0 + ncols], 8, ncols, gcol=g_pre)
                for cc in range(ncols // 128):
                    for g in range(NG):
                        pt = self.ps('mm')
                        for k in range(8):
                            self.mm(pt[:, 0:TG], wb[:, k, cc * 128:(cc + 1) * 128], hT[:, k, g * TG:(g + 1) * TG], start=(k == 0), stop=(k == 7),
                                    rk=[kname(wb), ('hT', g)])
                        ev = self.rot('tm_ev', [128, 512], F32, 4)
                        self.cp('act' if g % 2 == 0 else 'dve', ev[:], pt[:, 0:TG])
                        self.st(self.s_fm[r0 + cc * 128:r0 + (cc + 1) * 128, PAD + g * TG:PAD + (g + 1) * TG], ev[:], wk=[('s_fm', r0 + cc * 128, g)])
            wb = self.load_w(w_in[:, LI_OFF:LI_OFF + 8], 8, 8, gcol=g_pre)
            for g in range(NG):
                pt = self.ps('mm')
                for k in range(8):
                    self.mm(pt[0:8, 0:TG], wb[:, k, 0:8], hT[:, k, g * TG:(g + 1) * TG], start=(k == 0), stop=(k == 7), rk=[kname(wb), ('hT', g)])
                ev = self.rot('tm_ev', [128, 512], F32, 4)
                self.cp('act' if g % 2 == 0 else 'dve', ev[0:8, :], pt[0:8, 0:TG])
                self.st(self.s_fm[FM_LIF:FM_LIF + 8, PAD + g * TG:PAD + (g + 1) * TG], ev[0:8, :], wk=[('s_fm', FM_LIF, g)])
            for (c0, tc0) in [(MV_OFF, 0), (LV_OFF, 512)]:
                wb = self.load_w(w_in[:, c0:c0 + 512], 8, 512, gcol=g_pre)
                for t in range(NTILE):
                    pt = self.ps('mm')
                    for k in range(8):
                        self.mm(pt[:, 0:512], hT[:, k, t * 128:(t + 1) * 128], wb[:, k, :], start=(k == 0), stop=(k == 7), rk=[kname(wb), ('hT', t // 4)])
                    ev = self.rot('tm_ev', [128, 512], F32, 4)
                    self.cp('act' if t % 2 == 0 else 'dve', ev[:], pt[:, 0:512])
                    self.st(self.s_tm[t * 128:(t + 1) * 128, tc0:tc0 + 512], ev[:], wk=[('s_tm', tc0, t)])

    def resid_epilogue(self, pts, t, gbc, xsrc, xsrckey, dst, dstkey, eps=1e-6):
        ssa = self.rot('ep_ss', [128, 2], F32, 4)
        junk = self.rot('ep_junk', [128, 512], BF16, 2)
        for hh in range(2):
            self.act(junk[:], pts[hh][:, 0:512], AF.Square, accum=ssa[:, hh:hh + 1])
        rs = self.rot('ep_rs', [128, 1], F32, 4)
        self.tt('dve', rs[:], ssa[:, 0:1], ssa[:, 1:2], ALU.add)
        self.act(rs[:], rs[:], AF.Sqrt, scale=1.0 / D, bias=eps)
        self.em.op('dve', lambda e, rs=rs: e.reciprocal(out=rs[:], in_=rs[:]), reads=[kname(rs[:])], writes=[kname(rs[:])])
        xt = self.rot('ep_x', [128, D], F32, 2)
        self.ld(xt[:], xsrc[t * 128:(t + 1) * 128, :], rk=[(xsrckey, t)])
        ot = self.rot('ep_o', [128, D], F32, 2)
        for hh in range(2):
            self.stt('dve', ot[:, hh * 512:(hh + 1) * 512], pts[hh][:, 0:512], rs[:, 0:1], gbc[:, hh * 512:(hh + 1) * 512], ALU.mult, ALU.mult)
        self.tt('pool', ot[:], ot[:], xt[:], ALU.add)
        self.st(dst[t * 128:(t + 1) * 128, :], ot[:], wk=[(dstkey, t)])

    def phase_merge(self, l, wgt_pf=None, wbr_pf=None, wout_pf=None):
        I = self.I
        xsrc, xkey = (I["x"], "x") if l == 0 else (self.xres, "xres")
        with self.scope():
            self.set_pools(tr=[0], mm=[1, 2, 3, 4, 5], o=[6, 7])
            g_pre = self.colvec(I["ln_mix_pre"][l], 8, "g_pre_m")
            gpost = self.rowbc(I["ln_mix_post"][l], D, 128, "gpost_bc")
            wbr = wbr_pf if wbr_pf is not None else self.em.sb([128, 8, D], BF16, "wbr")
            wgt = wgt_pf if wgt_pf is not None else self.em.sb([128, 8, 3072], BF16, "wgt")
            wout = wout_pf if wout_pf is not None else self.em.sb([128, 8, D], BF16, "wout")
            with self.scope():
                if wbr_pf is None:
                    self.load_w(I["w_br_rwkv"][l], 2, D, dst=wbr[:, 0:2, :], dst_key='wbr', nbuf=4)
                    self.load_w(I["w_br_moba"][l], 4, D, dst=wbr[:, 2:6, :], dst_key='wbr', nbuf=4)
                    self.load_w(I["w_br_mlstm"][l], 2, D, dst=wbr[:, 6:8, :], dst_key='wbr', nbuf=4)
                for cb in (range(6) if wgt_pf is None else ()):
                    self.load_w(I["w_in"][l][:, GT_OFF + cb * 512:GT_OFF + (cb + 1) * 512], 8, 512, gcol=g_pre, dst=wgt[:, :, cb * 512:(cb + 1) * 512], dst_key='wgt', nbuf=4)
                for cb in (range(2) if wout_pf is None else ()):
                    self.load_w(I["w_out"][l][:, cb * 512:(cb + 1) * 512], 8, 512, dst=wout[:, :, cb * 512:(cb + 1) * 512], dst_key='wout', nbuf=4)
            kgrp = [(0, 2), (2, 6), (6, 8)]
            def stA(g):
                yT = self.rot('mg_yT', [128, 8, TG], BF16, 1)
                for tt in range(4):
                    t = g * 4 + tt
                    yt = self.rot('mg_y', [128, D], F32, 2)
                    self.ld(yt[:], self.s_y[t * 128:(t + 1) * 128, :], rk=['s_y'])
                    yb = self.rot('mg_yb', [128, D], BF16, 2)
                    self.cp('pool', yb[:], yt[:])
                    pt = self.ps('tr')
                    ptb = pt.bitcast(BF16)
                    for c in range(8):
                        self.tr(ptb[:, c * 128:(c + 1) * 128], yb[:, c * 128:(c + 1) * 128], self.identb[:])
                    self.cp('act', yT[:, :, tt * 128:(tt + 1) * 128], ptb[:, 0:1024].rearrange("p (c t) -> p c t", c=8))
                hTg = self.rot('mg_hT', [128, 8, TG], BF16, 1)
                self.ld(hTg[:], self.s_hT[:, :, g * TG:(g + 1) * TG], rk=[('s_hT', g)])
                mT = self.rot('mg_mT', [128, 8, TG], BF16, 2)
                for fc in range(8):
                    acc = self.rot('mg_acc', [128, TG], F32, 2)
                    for b in range(3):
                        pg = self.ps('mm')
                        for k in range(8):
                            self.mm(pg[:, 0:TG], wgt[:, k, b * 1024 + fc * 128:b * 1024 + (fc + 1) * 128], hTg[:, k, :], start=(k == 0), stop=(k == 7))
                        sg = self.rot('mg_sg', [128, TG], F32, 3)
                        self.act(sg[:], pg[:, 0:TG], AF.Sigmoid)
                        pb = self.ps('mm')
                        k0, k1 = kgrp[b]
                        for k in range(k0, k1):
                            self.mm(pb[:, 0:TG], wbr[:, k, fc * 128:(fc + 1) * 128], yT[:, k, :], start=(k == k0), stop=(k == k1 - 1))
                        if b == 0:
                            self.tt('dve', acc[:], sg[:], pb[:, 0:TG], ALU.mult)
                        else:
                            tmp = self.rot('mg_tmp', [128, TG], F32, 2)
                            self.tt('dve', tmp[:], sg[:], pb[:, 0:TG], ALU.mult)
                            if b == 1:
                                self.tt('pool', acc[:], acc[:], tmp[:], ALU.add)
                            else:
                                self.tt('pool', mT[:, fc, :], acc[:], tmp[:], ALU.add)
                MT[g] = mT

            def stB(g):
                mT = MT.pop(g)
                for tt in range(4):
                    t = g * 4 + tt
                    pts = [self.ps('o'), self.ps('o')]
                    for hh in range(2):
                        for k in range(8):
                            self.mm(pts[hh][:, 0:512], mT[:, k, tt * 128:(tt + 1) * 128], wout[:, k, hh * 512:(hh + 1) * 512], start=(k == 0), stop=(k == 7))
                    self.resid_epilogue(pts, t, gpost, xsrc, xkey, self.xres, "xres")


            MT = {}
            self.emit_interleaved(self.capture(stA, 0), [])
            for g in range(NG):
                nxt = self.capture(stA, g + 1) if g + 1 < NG else []
                self.emit_interleaved(nxt, self.capture(stB, g))
    def phase_ffn(self, l):
        I = self.I
        if not hasattr(self, 's_aT'):
            self.s_aT = self.nc.dram_tensor("s_aT", [22, 128, S], BF16, kind="Internal").ap()
        with self.scope():
            self.set_pools(tr=[0, 1], mm0=[2, 3, 4], mm1=[5, 6, 7])
            g_pre = self.colvec(I["ln_ffn_pre"][l], 8, "g_ffn")
            cw = self.em.sb([128, 3, 44], F32, "ffn_cw")
            for j3 in range(3):
                self.ld(cw[:, j3, :], I["ffn_conv_w"][l][j3].rearrange("(c p) -> p c", p=128), allow_slow_non_contiguous=True)
            cb = self.colvec(I["ffn_conv_b"][l], 44, "ffn_cb")
            hT = self.em.sb([128, 8, S], BF16, "ffn_hT_all")
            def fnorms(gs):
                for g in gs:
                    self.norm_group(self.xres, "xres", g, hT[:, :, g * TG:(g + 1) * TG], ('fhT', g))

            fnorms([0])
            uprev = {0: {}, 1: {}}
            pend = {0: [], 1: []}

            def back(s):
                aT_, g_, ucs_, j_ = pend[s].pop(0)
                ge = self.rot('ff_ge%d' % s, [128, TG], F32, 2)
                self.act(ge[:], ucs_[0][:], AF.Gelu_apprx_tanh)
                self.tt('pool', aT_[:, g_ * TG:(g_ + 1) * TG], ge[:], ucs_[1][:], ALU.mult)
                if g_ == NG - 1:
                    self.st(self.s_aT[j_], aT_[:], wk=[('s_aT', j_)], q='sp')

            def up_w(s, j):
                wbs = []
                for half in range(2):
                    col0 = half * DFF + j * 128
                    wbs.append(self.load_w(I["ffn_up"][l][:, col0:col0 + 128], 8, 128, gcol=g_pre, nbuf=3, engs=('act', 'act', 'pool'), tag='s%d' % s))
                aT = self.rot('ff_aT%d' % s, [128, S], BF16, 1)
                return wbs, aT

            def up_jg(s, j, g, wbs, aT):
                ucs = []
                for half in range(2):
                    jj = half * 22 + j
                    wb = wbs[half]
                    pt = self.ps('mm%d' % s)
                    for k in range(8):
                        self.mm(pt[:, 0:TG], wb[:, k, :], hT[:, k, g * TG:(g + 1) * TG], start=(k == 0), stop=(k == 7), rk=[kname(wb), ('fhT', g)])
                    u = self.rot('ff_u%d_%d' % (half, s), [128, TG + 2], F32, 3)
                    self.cp('act', u[:, 2:TG + 2], pt[:, 0:TG])
                    if g == 0:
                        self.memset('pool', u[:, 0:2], 0.0)
                    else:
                        self.cp('pool', u[:, 0:2], uprev[s][half][:, TG:TG + 2])
                    uprev[s][half] = u
                    uc = self.rot('ff_uc%d_%d' % (half, s), [128, TG], F32, 3)
                    self.act(uc[:], pt[:, 0:TG], AF.Identity, scale=cw[:, 2, jj:jj + 1], bias=cb[:, jj:jj + 1])
                    self.stt('dve', uc[:], u[:, 1:TG + 1], cw[:, 1, jj:jj + 1], uc[:], ALU.mult, ALU.add)
                    self.stt('dve', uc[:], u[:, 0:TG], cw[:, 0, jj:jj + 1], uc[:], ALU.mult, ALU.add)
                    ucs.append(uc)
                pend[s].append((aT, g, ucs, j))
                if len(pend[s]) > 1:
                    back(s)

            def run_stream(s, js):
                for j in js:
                    wbs_, aT_j = up_w(s, j)
                    for g in range(NG):
                        up_jg(s, j, g, wbs_, aT_j)
                while pend[s]:
                    back(s)

            wbs_, aT_j = up_w(0, 0)
            for g in range(NG):
                nxt = self.capture(fnorms, [g + 1]) if g + 1 < NG else []
                self.emit_interleaved(nxt, self.capture(up_jg, 0, 0, g, wbs_, aT_j))
            self.emit_interleaved(self.capture(run_stream, 0, range(1, 12)), self.capture(run_stream, 1, range(12, 22)))
        with self.scope():
            self.set_pools(o=[0, 1, 2, 3, 4, 5, 6, 7])
            gpost = self.rowbc(I["ln_ffn_post"][l], D, 128, "gffn_post_bc")
            wdn = self.em.sb([128, 22, D], BF16, "wdn")
            for cbk in range(4):
                self.load_w(I["ffn_down"][l][:, cbk * 256:(cbk + 1) * 256], 22, 256, dst=wdn[:, :, cbk * 256:(cbk + 1) * 256], kchunk=8)
            for g in range(NG):
                aTg = self.rot('ff_aTg', [128, 22, TG], BF16, 2)
                self.ld(aTg[:], self.s_aT[:, :, g * TG:(g + 1) * TG].rearrange("j p t -> p j t"), rk=['s_aT'])
                for tt in range(4):
                    t = g * 4 + tt
                    pts = [self.ps('o'), self.ps('o')]
                    for hh in range(2):
                        for j in range(22):
                            self.mm(pts[hh][:, 0:512], aTg[:, j, tt * 128:(tt + 1) * 128], wdn[:, j, hh * 512:(hh + 1) * 512], start=(j == 0), stop=(j == 21))
                    self.resid_epilogue(pts, t, gpost, self.xres, "xres", self.xres, "xres")

    def phase_ple(self, l, dst, dstkey):
        I = self.I
        with self.scope():
            self.set_pools(tr=[0, 1], mm=[2, 3, 4, 5, 6, 7])
            g_pre = self.colvec(I["ln_ple"][l], 8, "g_ple")
            wg = self.em.sb([128, 8, D], BF16, "wpg")
            for cbk in range(2):
                self.load_w(I["ple_gate"][l][:, cbk * 512:(cbk + 1) * 512], 8, 512, gcol=g_pre, dst=wg[:, :, cbk * 512:(cbk + 1) * 512], dst_key='wpg')
            wp = self.em.sb([128, 2, D], BF16, "wpp")
            self.load_w(I["ple_proj"][l], 2, D, dst=wp[:], dst_key='wpp')
            HT = {}

            def stA(g):
                hTg = self.rot('pl_hT', [128, 8, TG], BF16, 2)
                self.norm_group(self.xres, "xres", g, hTg[:], kname(hTg[:]))
                HT[g] = hTg

            def stB(g):
                hTg = HT.pop(g)
                for tt in range(4):
                    t = g * 4 + tt
                    pt_ = self.rot('pl_p', [128, 256], F32, 2)
                    self.ld(pt_[:], I["p"][l][t * 128:(t + 1) * 128, :])
                    pb = self.rot('pl_pb', [128, 256], BF16, 2)
                    self.cp('pool', pb[:], pt_[:])
                    ptr = self.ps('tr')
                    ptrb = ptr.bitcast(BF16)
                    for c in range(2):
                        self.tr(ptrb[:, c * 128:(c + 1) * 128], pb[:, c * 128:(c + 1) * 128], self.identb[:])
                    pT = self.rot('pl_pT', [128, 2, 128], BF16, 2)
                    self.cp('act', pT[:], ptrb[:, 0:256].rearrange("p (c t) -> p c t", c=2))
                    xt = self.rot('pl_x', [128, D], F32, 2)
                    self.ld(xt[:], self.xres[t * 128:(t + 1) * 128, :], rk=[("xres", t)])
                    ot = self.rot('pl_o', [128, D], F32, 2)
                    for hh in range(2):
                        pg = self.ps('mm')
                        for k in range(8):
                            self.mm(pg[:, 0:512], hTg[:, k, tt * 128:(tt + 1) * 128], wg[:, k, hh * 512:(hh + 1) * 512], start=(k == 0), stop=(k == 7))
                        sg = self.rot('pl_sg', [128, 512], F32, 2)
                        self.act(sg[:], pg[:, 0:512], AF.Sigmoid)
                        pp = self.ps('mm')
                        for k in range(2):
                            self.mm(pp[:, 0:512], pT[:, k, :], wp[:, k, hh * 512:(hh + 1) * 512], start=(k == 0), stop=(k == 1))
                        self.tt('dve', sg[:], sg[:], pp[:, 0:512], ALU.mult)
                        self.tt('pool', ot[:, hh * 512:(hh + 1) * 512], sg[:], xt[:, hh * 512:(hh + 1) * 512], ALU.add)
                    self.st(dst[t * 128:(t + 1) * 128, :], ot[:], wk=[(dstkey, t)])

            self.emit_interleaved(self.capture(stA, 0), [])
            for g in range(NG):
                nxt = self.capture(stA, g + 1) if g + 1 < NG else []
                self.emit_interleaved(nxt, self.capture(stB, g))

    def phase_moba(self, l, extra=None):
        with self.scope():
            self.set_pools(sc=[0, 1, 2, 3], acc=[4, 5], ot=[6], bs=[7], tr=[7])
            em = self.em
            KQ = []
            for i_ in range(2):
                Ka = em.sb([80, S], BF16, "mb_KaugT%d" % i_)
                Qa = em.sb([80, S], BF16, "mb_QaugT%d" % i_)
                Va = em.sb([128, 32, 65], BF16, "mb_Vaug%d" % i_)
                self.memset('pool', Va[:, :, 64:65], 1.0)
                KQ.append((Ka, Qa, Va))
            with self.scope():
                ohb = em.sb([16, S], BF16, "mb_ohb")
                oh = em.sb([16, S], F32, "mb_oh")
                self.memset('pool', oh[:], 1.0)
                self.asel(oh[:], [[1, S]], ALU.is_ge, 0.0, 0, -256)
                self.asel(oh[:], [[-1, S]], ALU.is_ge, 0.0, 255, 256)
                self.cp('pool', ohb[:], oh[:])
                for i_ in range(2):
                    self.st(KQ[i_][0][64:80, :], ohb[:], q='sp')
            EX = self.capture(extra) if extra is not None else []
            tri = em.sb([128, 128], BF16, "mb_tri")
            trif = em.sb([128, 128], F32, "mb_trif")
            self.memset('pool', trif[:], 1.0)
            self.asel(trif[:], [[1, 128]], ALU.is_ge, 0.0, 0, -1)
            self.cp('pool', tri[:], trif[:])
            pastm = em.sb([128, 32, 16], F32, "mb_pastm")
            ownm = em.sb([128, 32, 16], F32, "mb_ownm")
            negm = em.sb([128, 32, 16], F32, "mb_negm")
            self.memset('pool', pastm[:], 0.0)
            self.memset('pool', ownm[:], 0.0)
            for tt in range(32):
                ob = tt // 2
                if ob > 0:
                    self.memset('pool', pastm[:, tt, 0:ob], 1.0)
                self.memset('pool', ownm[:, tt, ob:ob + 1], 1.0)
            self.ts('pool', negm[:], pastm[:], -1.0, 1e30, ALU.add, ALU.mult)
            ident65 = self.identf[0:65, 0:65]
            HS = {}

            def setupA(h):
                KaugT, QaugT, Vaug = KQ[h % 2]
                qf = self.rot('mb_qf', [64, S], F32, 1)
                kf = self.rot('mb_kf', [64, S], F32, 1)
                r0 = FM_MQK + h * 64
                self.ld(qf[:], self.s_fm[r0:r0 + 64, PAD:PAD + S])
                self.ld(kf[:], self.s_fm[r0 + 512:r0 + 576, PAD:PAD + S])
                vf = self.rot('mb_vf', [128, 32, 64], F32, 1)
                self.ld(vf[:], self.s_tm[:, h * 64:(h + 1) * 64].rearrange("(n p) d -> p n d", p=128))
                self.cp('dve', Vaug[:, :, 0:64], vf[:])
                self.cp('dve', KaugT[0:64, :], kf[:])
                self.cp('dve', QaugT[0:64, :], qf[:])
                km = self.rot('mb_km', [64, 16], F32, 2)
                self.em.op('dve', lambda e, km=km, kf=kf: e.tensor_reduce(out=km[:], in_=kf[:].rearrange("p (j s) -> p j s", s=256), axis=AX.X, op=ALU.add),
                           reads=[kname(kf[:])], writes=[kname(km[:])])
                bs = self.ps('bs')
                for tt in range(32):
                    self.mm(bs[:, tt * 16:(tt + 1) * 16], qf[:, tt * 128:(tt + 1) * 128], km[:, :])
                bsm = self.rot('mb_bsm', [128, 32, 16], F32, 1)
                self.tt('dve', bsm[:], bs[:, 0:512].rearrange("p (t j) -> p t j", j=16), pastm[:], ALU.mult)
                self.tt('dve', bsm[:], bsm[:], negm[:], ALU.add)
                m8 = self.rot('mb_m8', [128, 32, 8], F32, 1)
                for tt in range(32):
                    self.em.op('dve', lambda e, tt=tt, m8=m8, bsm=bsm: e.max(out=m8[:, tt, :], in_=bsm[:, tt, :]),
                               reads=[kname(bsm[:])], writes=[kname(m8[:])])
                sel = self.rot('mb_sel', [128, 32, 16], F32, 2)
                self.tt('dve', sel[:], bsm[:], bcast(m8[:, :, 2:3], 2, 16), ALU.is_ge)
                self.tt('dve', sel[:], sel[:], pastm[:], ALU.mult)
                self.tt('dve', sel[:], sel[:], ownm[:], ALU.add)
                self.ts('dve', sel[:], sel[:], -1.0, 30000.0, ALU.add, ALU.mult)
                HS[h] = sel

            def setupB(h):
                KaugT, QaugT, Vaug = KQ[h % 2]
                sel = HS[h]
                mbT = self.rot('mb_mbT', [16, S], BF16, 1)
                for t4 in range(8):
                    pt = self.ps('tr')
                    for q in range(4):
                        tt = t4 * 4 + q
                        self.tr(pt[0:16, q * 128:(q + 1) * 128], sel[:, tt, :], self.identf[:])
                    self.cp('dve', mbT[:, t4 * 512:(t4 + 1) * 512], pt[0:16, 0:512])
                self.st(QaugT[64:80, :], mbT[:], q='sp')

            def main(h):
                KaugT, QaugT, Vaug = KQ[h % 2]
                iters = [(tg, st_) for tg in range(8) for st_ in range(4 * (tg + 1))]
                LA = 3
                pTs = {}
                ots = {}

                def front(i):
                    tg, st_ = iters[i]
                    sl_ = st_ - 4 * tg
                    c0 = 256 if sl_ >= 2 else 0
                    sc = self.ps('sc')
                    self.mm(sc[:, c0:512], KaugT[0:80, st_ * 128:(st_ + 1) * 128], QaugT[0:80, tg * 512 + c0:(tg + 1) * 512])
                    pT = self.rot('mb_pT', [128, 512], BF16, 6)
                    self.act(pT[:, c0:512], sc[:, c0:512], AF.Exp, scale=0.125)
                    if sl_ >= 0:
                        if sl_ * 128 > c0:
                            self.memset('dve', pT[:, c0:sl_ * 128], 0.0)
                        self.tt('dve', pT[:, sl_ * 128:(sl_ + 1) * 128], pT[:, sl_ * 128:(sl_ + 1) * 128], tri[:], ALU.mult)
                    pTs[i] = (pT, c0)

                def back(i):
                    tg, st_ = iters[i]
                    nst = 4 * (tg + 1)
                    if st_ == 0:
                        ots[tg] = self.ps('acc')
                    ot = ots[tg]
                    pT, c0 = pTs.pop(i)
                    self.mm(ot[0:65, c0:512], Vaug[:, st_, :], pT[:, c0:512], start=(st_ == 0), stop=(st_ == nst - 1))
                    if st_ == nst - 1:
                        osb = self.rot('mb_osb', [65, 512], F32, 2)
                        self.cp('dve', osb[:], ot[0:65, 0:512])
                        po = self.ps('ot')
                        for qi in range(4):
                            self.tr(po[:, qi * 65:(qi + 1) * 65], osb[0:65, qi * 128:(qi + 1) * 128], ident65)
                        rd = self.rot('mb_rd', [128, 4, 1], F32, 2)
                        pov = po[:, 0:260].rearrange("p (q d) -> p q d", d=65)
                        self.em.op('dve', lambda e, rd=rd, pov=pov: e.reciprocal(out=rd[:], in_=pov[:, :, 64:65]), reads=[kname(po[:])], writes=[kname(rd[:])])
                        yo = self.rot('mb_yo', [128, 4, 64], F32, 2)
                        self.tt('dve', yo[:], pov[:, :, 0:64], bcast(rd[:], 2, 64), ALU.mult)
                        self.st(self.s_y[tg * 512:(tg + 1) * 512, 256 + h * 64:256 + (h + 1) * 64].rearrange("(q p) d -> p q d", p=128), yo[:], wk=[('s_y_b', h, tg)])

                n = len(iters)
                for i in range(min(LA, n)):
                    front(i)
                for i in range(n):
                    if i + LA < n:
                        front(i + LA)
                    back(i)

            setupA(0)
            setupB(0)
            for h in range(8):
                M_ = self.capture(main, h)
                if h + 1 < 8:
                    SA = self.capture(setupA, h + 1)
                    SB = self.capture(setupB, h + 1)
                    c1 = len(M_) // 20
                    c2 = (len(M_) * 3) // 4
                    seq = M_[:c1] + self.merge_streams(M_[c1:c2], SA) + SB + M_[c2:]
                else:
                    seq = M_
                ex_h = EX[(len(EX) * h) // 8:(len(EX) * (h + 1)) // 8]
                self.emit_interleaved(seq, ex_h)

    def phase_mlstm(self, l, scoped=True, pools=None, GW=512):
        I = self.I
        with (self.scope() if scoped else contextlib.nullcontext()):
            self.set_pools(**(pools or dict(tk=[0, 1], qk=[2, 3], acc=[4, 5], P=[6], misc=[7])))
            em = self.em
            NJ = GW // 64
            NGm = S // GW
            cw = em.sb([64, 4, 8], F32, "ml_cw")
            for j in range(4):
                self.ld(cw[:, j, :], I["mlstm_conv_w"][l][j].rearrange("(c p) -> p c", p=64), allow_slow_non_contiguous=True)
            cb = em.sb([64, 8], F32, "ml_cb")
            self.ld(cb[:], I["mlstm_conv_b"][l].rearrange("(c p) -> p c", p=64), allow_slow_non_contiguous=True)
            ib = em.sb([4, 1], F32, "ml_ib")
            fb = em.sb([4, 1], F32, "ml_fb")
            self.ld(ib[:], I["mlstm_i_b"][l].rearrange("(h o) -> h o", o=1))
            self.ld(fb[:], I["mlstm_f_b"][l].rearrange("(h o) -> h o", o=1))
            nfb = em.sb([4, 1], F32, "ml_nfb")
            self.ts('dve', nfb[:], fb[:], -1.0, None, ALU.mult)
            hng = self.rowbc(I["mlstm_hn_g"][l], 256, 64, "ml_hng")
            selT = em.sb([4, 4, 64], F32, "ml_selT")
            self.memset('pool', selT[:], 0.0)
            self.asel(selT[:], [[-1, 4], [0, 64]], ALU.not_equal, 1.0, 0, 1)
            mask4 = em.sb([64, 4, 64], F32, "ml_mask4")
            self.memset('pool', mask4[:], 1.0)
            self.asel(mask4[:], [[0, 4], [1, 64]], ALU.is_ge, 0.0, 0, -1)
            ones4 = em.sb([4, GW], F32, "ml_ones4")
            zeros4 = em.sb([4, GW], F32, "ml_zeros4")
            self.memset('pool', ones4[:], 1.0)
            self.memset('pool', zeros4[:], 0.0)
            Chat = em.sb([64, 4, 65], F32, "ml_Chat0")
            self.memset('dve', Chat[:], 0.0)
            Fprev = None
            cprev = None
            def prepare(g):
                nonlocal Fprev, cprev
                c0 = PAD + g * GW
                li = self.rot('ml_li', [4, GW], F32, 1)
                lf = self.rot('ml_lf', [4, GW], F32, 1)
                self.ld(li[:], self.s_fm[FM_LIF:FM_LIF + 4, c0:c0 + GW])
                self.ld(lf[:], self.s_fm[FM_LIF + 4:FM_LIF + 8, c0:c0 + GW])
                logi = self.rot('ml_logi', [4, GW], F32, 1)
                self.ts('dve', logi[:], li[:], ib[:, 0:1], None, ALU.add)
                e1 = self.rot('ml_e1', [4, GW], F32, 1)
                self.act(e1[:], lf[:], AF.Exp, bias=nfb[:, 0:1], scale=-1.0)
                self.act(e1[:], e1[:], AF.Ln, bias=1.0)
                logf = self.rot('ml_logf', [4, GW], F32, 1)
                self.ts('dve', logf[:], e1[:], -1.0, None, ALU.mult)
                F = self.rot('ml_F', [4, GW], F32, 2)
                self.em.op('dve', lambda e, F=F, logf=logf, init=(0.0 if Fprev is None else Fprev[:, GW - 1:GW]): e.tensor_tensor_scan(
                    out=F[:], data0=ones4[:], data1=logf[:], initial=init, op0=ALU.mult, op1=ALU.add),
                    reads=[kname(ones4[:]), kname(logf[:])] + ([] if Fprev is None else [kname(Fprev[:])]), writes=[kname(F[:])])
                G = self.rot('ml_G', [4, GW], F32, 1)
                self.tt('dve', G[:], logi[:], F[:], ALU.subtract)
                c = self.rot('ml_c', [4, GW], F32, 2)
                self.em.op('dve', lambda e, c=c, G=G, init=(0.0 if cprev is None else cprev[:, GW - 1:GW]): e.tensor_tensor_scan(
                    out=c[:], data0=zeros4[:], data1=G[:], initial=init, op0=ALU.max, op1=ALU.max),
                    reads=[kname(zeros4[:]), kname(G[:])] + ([] if cprev is None else [kname(cprev[:])]), writes=[kname(c[:])])
                cend = c[:].rearrange("p (j s) -> p j s", s=64)[:, :, 63:64]
                wrow = self.rot('ml_wrow', [4, GW], F32, 1)
                self.tt('dve', wrow[:].rearrange("p (j s) -> p j s", s=64), G[:].rearrange("p (j s) -> p j s", s=64), bcast(cend, 2, 64), ALU.subtract)
                self.act(wrow[:], wrow[:], AF.Exp)
                zrow = self.rot('ml_zrow', [4, GW], F32, 1)
                self.tt('dve', zrow[:].rearrange("p (j s) -> p j s", s=64), F[:].rearrange("p (j s) -> p j s", s=64), bcast(cend, 2, 64), ALU.add)
                self.act(zrow[:], zrow[:], AF.Exp, scale=-1.0)
                cpv = self.rot('ml_cpv', [4, NJ], F32, 1)
                if cprev is None:
                    self.memset('dve', cpv[:, 0:1], 0.0)
                else:
                    self.cp('dve', cpv[:, 0:1], cprev[:, GW - 1:GW])
                cend2 = c[:].rearrange("p (j s) -> p j s", s=64)[:, :, 63]
                self.cp('dve', cpv[:, 1:NJ], cend2[:, 0:NJ - 1], rk=[kname(c[:])])
                crow = self.rot('ml_crow', [4, NJ], F32, 1)
                self.tt('dve', crow[:], cpv[:], cend2, ALU.subtract)
                self.act(crow[:], crow[:], AF.Exp)
                pm = self.ps('misc')
                for j in range(NJ):
                    self.tr(pm[0:64, j * 4:(j + 1) * 4], wrow[0:4, j * 64:(j + 1) * 64], self.identf[0:4, 0:4])
                    self.tr(pm[0:64, 64 + j * 4:64 + (j + 1) * 4], zrow[0:4, j * 64:(j + 1) * 64], self.identf[0:4, 0:4])
                for h in range(4):
                    self.mm(pm[0:64, 128 + h * NJ:128 + (h + 1) * NJ], selT[0:4, h, :], crow[0:4, :])
                TP = self.rot('ml_TP', [64, 128 + 4 * NJ], F32, 2)
                self.cp('act', TP[:], pm[0:64, 0:128 + 4 * NJ])
                TPw = TP[:, 0:4 * NJ].rearrange("p (j h) -> p j h", h=4)
                TPz = TP[:, 64:64 + 4 * NJ].rearrange("p (j h) -> p j h", h=4)
                carry = TP[:, 128:128 + 4 * NJ].rearrange("p (h j) -> p h j", j=NJ)
                Fprev, cprev = F, c
                qk = self.rot('ml_qkraw', [64, 8, GW + 3], F32, 1)
                self.ld(qk[:], self.s_fm[FM_LQK:FM_LQK + 512, c0 - 3:c0 + GW].rearrange("(c p) n -> p c n", p=64))
                qkc = self.rot('ml_qkc', [64, 8, GW], F32, 1)
                for ch in range(8):
                    eng = 'dve'
                    self.act(qkc[:, ch, :], qk[:, ch, 3:GW + 3], AF.Identity, scale=cw[:, 3, ch:ch + 1], bias=cb[:, ch:ch + 1])
                    for j3 in range(3):
                        self.stt(eng, qkc[:, ch, :], qk[:, ch, j3:j3 + GW], cw[:, j3, ch:ch + 1], qkc[:, ch, :], ALU.mult, ALU.add)
                qkb = self.rot('ml_qkb', [64, 8, GW], BF16, 2)
                self.act(qkb[:], qkc[:], AF.Silu)
                self.act(qkb[:, 4:8, :], qkb[:, 4:8, :], AF.Copy, scale=0.125)
                vraw = self.rot('ml_vraw', [64, NJ, 256], F32, 1)
                self.ld(vraw[:], self.s_tm[g * GW:(g + 1) * GW, 512:768].rearrange("(j p) n -> p j n", p=64))
                vo = self.rot('ml_oraw', [64, NJ, 256], F32, 2)
                self.ld(vo[:], self.s_tm[g * GW:(g + 1) * GW, 768:1024].rearrange("(j p) n -> p j n", p=64))
                Vaug = self.rot('ml_Vaug', [64, NJ, 4, 65], BF16, 2)
                self.memset('pool', Vaug[:, :, :, 64:65], 1.0)
                self.cp('pool', Vaug[:, :, :, 0:64], vraw[:].rearrange("p j (h d) -> p j h d", d=64))
                CTX[g] = dict(TP=TP, TPw=TPw, carry=carry, qkb=qkb, Vaug=Vaug, vo=vo)

            def tail(g):
                nonlocal Chat
                c_ = CTX.pop(g)
                TP, TPw, carry, qkb, Vaug, vo = (c_[n_] for n_ in ('TP', 'TPw', 'carry', 'qkb', 'Vaug', 'vo'))
                accs = self.rot('ml_accs', [64, NJ, 4, 65], F32, 1)
                for j in range(NJ):
                    t0 = j * 64
                    pk = self.ps('tk')
                    pkb = pk.bitcast(BF16)
                    for h in range(4):
                        self.tr(pkb[0:64, h * 64:(h + 1) * 64], qkb[0:64, 4 + h, t0:t0 + 64], self.identb[0:64, 0:64])
                    Khat = self.rot('ml_Khat', [64, 4, 64], BF16, 2)
                    wb_ = bcast(TPw[:, j, :].unsqueeze(2), 2, 64)
                    for h in range(4):
                        self.act(Khat[:, h, :], pkb[0:64, h * 64:(h + 1) * 64], AF.Copy, scale=TPw[:, j, h:h + 1], rk=[kname(pk[:]), kname(TP[:])])
                    pq = self.ps('qk')
                    for h in range(4):
                        self.mm(pq[0:64, h * 64:(h + 1) * 64], qkb[0:64, 4 + h, t0:t0 + 64], qkb[0:64, h, t0:t0 + 64])
                    qkw = self.rot('ml_qkw', [64, 4, 64], BF16, 2)
                    self.tt('dve', qkw[:], pq[0:64, 0:256].rearrange("p (h d) -> p h d", d=64), wb_, ALU.mult, rk=[kname(pq[:]), kname(TP[:])])
                    self.tt('pool', qkw[:], qkw[:], mask4[:], ALU.mult)
                    Cs = self.rot('ml_Cs', [64, 4, 65], F32, 2)
                    self.tt('dve', Cs[:], Chat[:], bcast(carry[:, :, j:j + 1], 2, 65), ALU.mult, rk=[kname(Chat[:]), kname(TP[:])])
                    Csb = self.rot('ml_Csb', [64, 4, 65], BF16, 2)
                    self.cp('act', Csb[:], Cs[:])
                    pa = self.ps('acc')
                    for h in range(4):
                        self.mm(pa[0:64, h * 65:(h + 1) * 65], qkb[0:64, h, t0:t0 + 64], Csb[0:64, h, :], start=True, stop=False)
                        self.mm(pa[0:64, h * 65:(h + 1) * 65], qkw[0:64, h, :], Vaug[0:64, j, h, :], start=False, stop=True)
                    pP = self.ps('P')
                    for h in range(4):
                        self.mm(pP[0:64, h * 65:(h + 1) * 65], Khat[0:64, h, :], Vaug[0:64, j, h, :])
                    Chat = self.rot('ml_Chat', [64, 4, 65], F32, 3)
                    self.tt('dve', Chat[:], Cs[:], pP[0:64, 0:260].rearrange("p (h d) -> p h d", d=65), ALU.add)
                    self.cp('act', accs[:, j, :, :], pa[0:64, 0:260].rearrange("p (h d) -> p h d", d=65))
                NB = NJ * 4
                av = accs[:].rearrange("p j h d -> p (j h) d")
                dn = self.rot('ml_dn', [64, NB, 1], F32, 1)
                self.stt('dve', dn[:], av[:, :, 64:65], -1.0, av[:, :, 64:65], ALU.mult, ALU.max)
                self.tt('dve', dn[:], dn[:], TP[:, 64:64 + NB].unsqueeze(2), ALU.max)
                self.em.op('dve', lambda e, dn=dn: e.reciprocal(out=dn[:], in_=dn[:]), reads=[kname(dn[:])], writes=[kname(dn[:])])
                hh_ = self.rot('ml_hh', [64, NB, 64], F32, 1)
                self.tt('dve', hh_[:], av[:, :, 0:64], bcast(dn[:], 2, 64), ALU.mult)
                s1 = self.rot('ml_s1', [64, NB, 1], F32, 1)
                self.em.op('dve', lambda e, s1=s1, hh_=hh_: e.tensor_reduce(out=s1[:, :, 0], in_=hh_[:], axis=AX.X, op=ALU.add), reads=[kname(hh_[:])], writes=[kname(s1[:])])
                self.ts('dve', s1[:], s1[:], 1.0 / 64, None, ALU.mult)
                self.tt('pool', hh_[:], hh_[:], bcast(s1[:], 2, 64), ALU.subtract)
                sq = self.rot('ml_sq', [64, NB, 64], F32, 1)
                self.tt('pool', sq[:], hh_[:], hh_[:], ALU.mult)
                s2 = self.rot('ml_s2', [64, NB, 1], F32, 1)
                self.em.op('dve', lambda e, s2=s2, sq=sq: e.tensor_reduce(out=s2[:, :, 0], in_=sq[:], axis=AX.X, op=ALU.add), reads=[kname(sq[:])], writes=[kname(s2[:])])
                self.act(s2[:], s2[:], AF.Sqrt, scale=1.0 / 64, bias=1e-6)
                self.em.op('dve', lambda e, s2=s2: e.reciprocal(out=s2[:], in_=s2[:]), reads=[kname(s2[:])], writes=[kname(s2[:])])
                self.tt('dve', hh_[:], hh_[:], bcast(s2[:], 2, 64), ALU.mult)
                sgo_v = sq[:].rearrange("p (j h) d -> p j (h d)", h=4)
                self.act(sgo_v, vo[:], AF.Sigmoid)
                hv = hh_[:].rearrange("p (j h) d -> p j (h d)", h=4)
                self.tt('pool', sgo_v, sgo_v, bcast(hng[:].unsqueeze(1), 1, NJ), ALU.mult)
                self.tt('dve', sgo_v, sgo_v, hv, ALU.mult)
                self.st(self.s_y[g * GW:(g + 1) * GW, 768:1024].rearrange("(j p) n -> p j n", p=64), sgo_v, wk=[('s_y_c', g)])


            CTX = {}
            self.emit_interleaved(self.capture(prepare, 0), [])
            for g in range(NGm):
                nxt = self.capture(prepare, g + 1) if g + 1 < NGm else []
                tl = self.capture(tail, g)
                self.emit_interleaved(nxt, tl)
    def phase_rwkv(self, l, scoped=True, pools=None):
        I = self.I
        with (self.scope() if scoped else contextlib.nullcontext()):
            self.set_pools(**(pools or dict(a=[0, 1, 2, 3], c=[4, 5, 6, 7])))
            em = self.em
            GW = 128
            NJ = GW // 64
            NGR = S // GW
            NB = NJ * 4
            hp = lambda v: v.rearrange("(h p) -> p h", p=64)
            mu = I["rwkv_mu"][l]
            mu3 = em.sb([64, 3, 4], F32, "rw_mu3")
            for X in range(3):
                self.ld(mu3[:, X, :], hp(mu[X * 256:(X + 1) * 256]), allow_slow_non_contiguous=True)
            mu_w = em.sb([64, 1], F32, "rw_muw")
            mu_a = em.sb([64, 1], F32, "rw_mua")
            mu_g = em.sb([128, 1], F32, "rw_mug")
            self.ld(mu_w[:], mu[768:832].rearrange("(p o) -> p o", o=1))
            self.ld(mu_a[:], mu[832:896].rearrange("(p o) -> p o", o=1))
            self.ld(mu_g[:], mu[896:1024].rearrange("(p o) -> p o", o=1))
            def hvec(name):
                t = em.sb([64, 4], F32, "rw_" + name)
                self.ld(t[:], hp(I["rwkv_" + name][l]), allow_slow_non_contiguous=True)
                return t
            w0, a0, k_k, k_a, r_k = hvec("w0"), hvec("a0"), hvec("k_k"), hvec("k_a"), hvec("r_k")
            omka = em.sb([64, 4], F32, "rw_omka")
            self.ts('dve', omka[:], k_a[:], -1.0, 1.0, ALU.mult, ALU.add)
            w2 = em.sb([64, 256], F32, "rw_w2")
            a2 = em.sb([64, 256], F32, "rw_a2")
            g2 = em.sb([128, 256], F32, "rw_g2")
            self.ld(w2[:], I["rwkv_w2"][l])
            self.ld(a2[:], I["rwkv_a2"][l])
            self.ld(g2[:], I["rwkv_g2"][l])
            if l > 0:
                v0 = em.sb([64, 4], F32, "rw_v0")
                self.ld(v0[:], hp(I["rwkv_v0"][l - 1]), allow_slow_non_contiguous=True)
                v1 = em.sb([64, 4, 32], F32, "rw_v1")
                self.ld(v1[:], I["rwkv_v1"][l - 1].rearrange("(h p) r -> p h r", p=64))
                v2 = em.sb([32, 256], F32, "rw_v2")
                self.ld(v2[:], I["rwkv_v2"][l - 1])
            gng = self.rowbc(I["rwkv_gn_g"][l], 256, 64, "rw_gng")
            gnb = self.rowbc(I["rwkv_gn_b"][l], 256, 64, "rw_gnb")
            sl4 = em.sb([64, 4, 64], F32, "rw_sl4")
            su4 = em.sb([64, 4, 64], F32, "rw_su4")
            sui4 = em.sb([64, 4, 64], F32, "rw_sui4")
            for t_, pat, base, cm in ((sl4, [[0, 4], [-1, 64]], -1, 1), (su4, [[0, 4], [1, 64]], -1, -1), (sui4, [[0, 4], [1, 64]], 0, -1)):
                self.memset('pool', t_[:], 1.0)
                self.asel(t_[:], pat, ALU.is_ge, 0.0, base, cm)
            segm = em.sb([64, 4 * GW], F32, "rw_segm")
            self.memset('pool', segm[:], 1.0)
            self.memset('pool', segm[:].rearrange("p (n s) -> p n s", s=64)[:, :, 0:1], 0.0)
            ones64 = em.sb([64, 64], F32, "rw_ones64")
            self.memset('pool', ones64[:], 1.0)
            id64 = self.identf[0:64, 0:64]
            id64b = self.identb[0:64, 0:64]
            I8 = em.sb([64, NB, 64], F32, "rw_I8")
            for b_ in range(NB):
                self.cp('pool', I8[:, b_, :], id64)
            ST = em.sb([64, 4, 64], F32, "rw_ST0")
            self.memset('dve', ST[:], 0.0)
            STb = em.sb([64, 4, 64], BF16, "rw_STb0")
            self.memset('dve', STb[:], 0.0)
            bc3 = lambda t_: bcast(t_[:].unsqueeze(2), 2, GW)
            NEG = -0.6065306597126334
            def prepare(g):
                c0 = PAD + g * GW
                tok0 = g * GW
                raw = self.rot('rw_raw', [64, 3, 4, GW + 1], F32, 1)
                for X in range(3):
                    self.ld(raw[:, X, :, :], self.s_fm[X * 256:(X + 1) * 256, c0 - 1:c0 + GW].rearrange("(h p) n -> p h n", p=64))
                rwa = self.rot('rw_rwa', [64, 2, GW + 1], F32, 1)
                self.ld(rwa[:], self.s_fm[768:896, c0 - 1:c0 + GW].rearrange("(x p) n -> p x n", p=64))
                rg = self.rot('rw_rg', [128, GW + 1], F32, 1)
                self.ld(rg[:], self.s_fm[896:1024, c0 - 1:c0 + GW])
                L3 = self.rot('rw_L3', [64, 3, 4, GW], F32, 1)
                for X in range(3):
                    d = self.rot('rw_d', [64, 4, GW], F32, 1)
                    self.tt('dve', d[:], raw[:, X, :, 0:GW], raw[:, X, :, 1:GW + 1], ALU.subtract)
                    self.tt('pool', d[:], d[:], bc3(mu3[:, X, :]), ALU.mult)
                    self.tt('dve', L3[:, X, :, :], d[:], raw[:, X, :, 1:GW + 1], ALU.add)
                r_, k_, v_ = L3[:, 0, :, :], L3[:, 1, :, :], L3[:, 2, :, :]
                xwa = self.rot('rw_xwa', [64, 2, GW], F32, 1)
                for X, m_ in ((0, mu_w), (1, mu_a)):
                    d = self.rot('rw_d1', [64, GW], F32, 2)
                    self.tt('dve', d[:], rwa[:, X, 0:GW], rwa[:, X, 1:GW + 1], ALU.subtract)
                    self.stt('dve', xwa[:, X, :], d[:], m_[:, 0:1], rwa[:, X, 1:GW + 1], ALU.mult, ALU.add)
                xg = self.rot('rw_xg', [128, GW], F32, 1)
                dg = self.rot('rw_dg', [128, GW], F32, 1)
                self.tt('dve', dg[:], rg[:, 0:GW], rg[:, 1:GW + 1], ALU.subtract)
                self.stt('dve', xg[:], dg[:], mu_g[:, 0:1], rg[:, 1:GW + 1], ALU.mult, ALU.add)
                self.act(xwa[:, 0, :], xwa[:, 0, :], AF.Tanh)
                self.act(xg[:], xg[:], AF.Sigmoid)
                lw = self.rot('rw_lw', [64, 4, GW], F32, 1)
                a_ = self.rot('rw_a', [64, 4, GW], F32, 1)
                gT = self.rot('rw_gT', [64, 4, GW], F32, 1)
                for h in range(4):
                    p1 = self.ps('a')
                    self.mm(p1[0:64, 0:GW], w2[:, h * 64:(h + 1) * 64], xwa[:, 0, :])
                    self.act(lw[:, h, :], p1[0:64, 0:GW], AF.Sigmoid, bias=w0[:, h:h + 1])
                    p2 = self.ps('a')
                    self.mm(p2[0:64, 0:GW], a2[:, h * 64:(h + 1) * 64], xwa[:, 1, :])
                    self.act(a_[:, h, :], p2[0:64, 0:GW], AF.Sigmoid, bias=a0[:, h:h + 1])
                    p3 = self.ps('a')
                    self.mm(p3[0:64, 0:GW], g2[:, h * 64:(h + 1) * 64], xg[:, :])
                    self.cp('act', gT[:, h, :], p3[0:64, 0:GW])
                if l > 0:
                    p4 = self.ps('a')
                    for h in range(4):
                        self.mm(p4[0:32, 0:GW], v1[:, h, :], L3[:, 2, h, :], start=(h == 0), stop=(h == 3))
                    t1 = self.rot('rw_t1', [32, GW], F32, 1)
                    self.cp('act', t1[:], p4[0:32, 0:GW])
                    sgv = self.rot('rw_sgv', [64, 4, GW], F32, 1)
                    for h in range(4):
                        p5 = self.ps('a')
                        self.mm(p5[0:64, 0:GW], v2[:, h * 64:(h + 1) * 64], t1[:, :])
                        self.act(sgv[:, h, :], p5[0:64, 0:GW], AF.Sigmoid, bias=v0[:, h:h + 1])
                    vf = self.rot('rw_vf', [64, 4, GW], F32, 1)
                    self.ld(vf[:], self.vfirst[:, tok0:tok0 + GW].rearrange("(h p) n -> p h n", p=64))
                    self.tt('dve', vf[:], vf[:], v_, ALU.subtract)
                    self.tt('pool', vf[:], vf[:], sgv[:], ALU.mult)
                    self.tt('dve', v_, v_, vf[:], ALU.add)
                else:
                    self.st(self.vfirst[:, tok0:tok0 + GW].rearrange("(h p) n -> p h n", p=64), v_, wk=[('vfirst', g)])
                kk = self.rot('rw_kk', [64, 4, GW], F32, 1)
                self.tt('dve', kk[:], k_, bc3(k_k), ALU.mult)
                sq = self.rot('rw_sq', [64, 4, GW], F32, 1)
                self.tt('pool', sq[:], kk[:], kk[:], ALU.mult)
                nr = self.rot('rw_nr', [64, 4, GW], F32, 1)
                for h in range(4):
                    p6 = self.ps('a')
                    self.mm(p6[0:64, 0:GW], ones64[:, :], sq[:, h, :])
                    self.act(nr[:, h, :], p6[0:64, 0:GW], AF.Ln, bias=1e-30)
                self.act(nr[:], nr[:], AF.Exp, scale=-0.5)
                self.tt('dve', kk[:], kk[:], nr[:], ALU.mult)
                k2 = self.rot('rw_k2', [64, 4, GW], F32, 1)
                self.tt('pool', k2[:], a_[:], bc3(k_a), ALU.mult)
                self.tt('pool', k2[:], k2[:], bc3(omka), ALU.add)
                self.tt('dve', k2[:], k2[:], k_, ALU.mult)
                b_ = self.rot('rw_b', [64, 4, GW], F32, 1)
                self.tt('pool', b_[:], kk[:], a_[:], ALU.mult)
                cw = self.rot('rw_cw', [64, 4, GW], F32, 1)
                self.em.op('dve', lambda e, cw=cw, lw=lw: e.tensor_tensor_scan(out=cw[:].rearrange("p h n -> p (h n)"), data0=segm[:], data1=lw[:].rearrange("p h n -> p (h n)"),
                                                                               initial=0.0, op0=ALU.mult, op1=ALU.add),
                           reads=[kname(segm[:]), kname(lw[:])], writes=[kname(cw[:])])
                ep = self.rot('rw_ep', [64, 4, GW], F32, 2)
                en = self.rot('rw_en', [64, 4, GW], F32, 1)
                epv = self.rot('rw_epv', [64, 4, GW], F32, 1)
                self.act(ep[:], cw[:], AF.Exp, scale=NEG)
                self.act(en[:], cw[:], AF.Exp, scale=-NEG)
                self.tt('dve', epv[:], cw[:], lw[:], ALU.subtract)
                self.act(epv[:], epv[:], AF.Exp, scale=NEG)
                AR = self.rot('rw_AR', [64, 2, 4, GW], BF16, 2)
                at = AR[:, 0, :, :]
                rt = AR[:, 1, :, :]
                kt = self.rot('rw_kt', [64, 4, GW], BF16, 1)
                bt = self.rot('rw_bt', [64, 4, GW], BF16, 1)
                self.tt('dve', rt[:], r_, ep[:], ALU.mult)
                self.tt('pool', kt[:], k2[:], en[:], ALU.mult)
                self.tt('dve', bt[:], b_[:], en[:], ALU.mult)
                self.stt('dve', at[:], kk[:], -1.0, epv[:], ALU.mult, ALU.mult)
                rk = self.rot('rw_rk', [64, 4, GW], F32, 1)
                self.tt('pool', rk[:], r_, k2[:], ALU.mult)
                self.tt('pool', rk[:], rk[:], bc3(r_k), ALU.mult)
                pb = self.ps('a')
                for j in range(NJ):
                    for h in range(4):
                        self.mm(pb[0:64, j * 4 + h:j * 4 + h + 1], rk[:, h, j * 64:(j + 1) * 64], ones64[:, 0:1])
                bon = self.rot('rw_bon', [64, NB, 1], F32, 2)
                self.cp('act', bon[:, :, 0], pb[0:64, 0:NB])
                TM = {}
                for nm, src_ in (('B', bt[:]), ('K', kt[:]), ('V', v_), ('G', gT[:])):
                    isb = nm in ('B', 'K')
                    dst_ = self.rot('rw_tm' + nm, [64, NJ, 4, 64], BF16 if isb else F32, 2)
                    for j in range(NJ):
                        pt = self.ps('a')
                        ptv = pt.bitcast(BF16) if isb else pt
                        for h in range(4):
                            self.tr(ptv[0:64, h * 64:(h + 1) * 64], src_[:, h, j * 64:(j + 1) * 64], id64b if isb else id64)
                        self.cp('act' if j % 2 == 0 else 'dve', dst_[:, j, :, :], ptv[0:64, 0:256].rearrange("p (h d) -> p h d", d=64))
                    TM[nm] = dst_
                Vb = self.rot('rw_Vb', [64, NJ, 4, 64], BF16, 2)
                self.cp('act', Vb[:], TM['V'][:])
                CM = {}
                for nm in ('A', 'Bm', 'Ak', 'Rb', 'Rk'):
                    CM[nm] = self.rot('rw_cm' + nm, [64, NJ, 4, 64], BF16, 2)
                for j in range(NJ):
                    cs = slice(j * 64, (j + 1) * 64)
                    pt = self.ps('a')
                    for h in range(4):
                        self.mm(pt[0:64, h * 64:(h + 1) * 64], at[:, h, cs], bt[:, h, cs])
                    self.tt('dve', CM['A'][:, j, :, :], pt[0:64, 0:256].rearrange("p (h d) -> p h d", d=64), sl4[:], ALU.mult)
                    for lh, n0, n1 in ((bt, 'Bm', 'Rb'), (kt, 'Ak', 'Rk')):
                        pt = self.ps('a')
                        for h in range(4):
                            self.mm(pt[0:64, h * 128:(h + 1) * 128], lh[:, h, cs], AR[:, :, h, cs])
                        pv = pt[0:64, 0:512].rearrange("p (h x d) -> p h x d", x=2, d=64)
                        self.tt('dve', CM[n0][:, j, :, :], pv[:, :, 0, :], su4[:], ALU.mult)
                        self.tt('pool' if False else 'dve', CM[n1][:, j, :, :], pv[:, :, 1, :], sui4[:], ALU.mult)
                P = CM['Bm'][:].rearrange("p j h d -> p (j h) d")
                Q = CM['A'][:].rearrange("p j h d -> p (j h) d")
                M = self.rot('rw_M', [64, NB, 64], F32, 2)
                self.tt('pool', M[:], I8[:], P, ALU.add)
                M = M[:]
                Mb = self.rot('rw_Mb', [64, NB, 64], BF16, 2)
                self.cp('act', Mb[:], M)
                Mb = Mb[:]
                for lev in range(5):
                    last = (lev == 4)
                    pQ = self.ps('a')
                    for b in range(NB):
                        self.mm(pQ[0:64, b * 64:(b + 1) * 64], P[:, b, :], Q[:, b, :])
                    if not last:
                        pP = self.ps('a')
                        for b in range(NB):
                            self.mm(pP[0:64, b * 64:(b + 1) * 64], Q[:, b, :], P[:, b, :])
                    Qn = self.rot('rw_Qn', [64, NB, 64], BF16, 2)
                    self.cp('act', Qn[:], pQ[0:64, 0:NB * 64].rearrange("p (b d) -> p b d", d=64))
                    if not last:
                        Pn = self.rot('rw_Pn', [64, NB, 64], BF16, 2)
                        self.cp('act', Pn[:], pP[0:64, 0:NB * 64].rearrange("p (b d) -> p b d", d=64))
                        P = Pn[:]
                    Q = Qn[:]
                    pM = self.ps('a')
                    for b in range(NB):
                        self.mm(pM[0:64, b * 64:(b + 1) * 64], Q[:, b, :], Mb[:, b, :])
                    Mn = self.rot('rw_M', [64, NB, 64], F32, 2)
                    self.tt('dve', Mn[:], M, pM[0:64, 0:NB * 64].rearrange("p (b d) -> p b d", d=64), ALU.add)
                    M = Mn[:]
                    Mbn = self.rot('rw_Mb', [64, NB, 64], BF16, 2)
                    self.cp('act', Mbn[:], M)
                    Mb = Mbn[:]
                TTt = self.rot('rw_TT', [64, NB, 64], BF16, 2)
                self.cp('act', TTt[:], M)
                TT = TTt[:]
                CTX[g] = dict(CM=CM, TM=TM, Vb=Vb, at=at, rt=rt, TT=TT, ep=ep, bon=bon, tok0=tok0)

            def tail(g):
                nonlocal ST, STb
                c_ = CTX.pop(g)
                CM, TM, Vb, at, rt, TT, ep, bon, tok0 = (c_[n_] for n_ in ('CM', 'TM', 'Vb', 'at', 'rt', 'TT', 'ep', 'bon', 'tok0'))
                Ysb = self.rot('rw_Ysb', [64, NJ, 4, 64], F32, 1)
                V_, B_, K_ = Vb, TM['B'], TM['K']
                Vf_ = TM['V']
                for j in range(NJ):
                    cs = slice(j * 64, (j + 1) * 64)
                    pX = self.ps('c')
                    for h in range(4):
                        self.mm(pX[0:64, h * 64:(h + 1) * 64], CM['Ak'][:, j, h, :], V_[:, j, h, :], start=True, stop=False)
                        self.mm(pX[0:64, h * 64:(h + 1) * 64], at[:, h, cs], STb[:, h, :], start=False, stop=True)
                    Xsb = self.rot('rw_Xsb', [64, 4, 64], BF16, 2)
                    self.cp('act', Xsb[:], pX[0:64, 0:256].rearrange("p (h d) -> p h d", d=64))
                    pU = self.ps('c')
                    for h in range(4):
                        self.mm(pU[0:64, h * 64:(h + 1) * 64], TT[:, j * 4 + h, :], Xsb[:, h, :])
                    Usb = self.rot('rw_Usb', [64, 4, 64], BF16, 2)
                    self.cp('act', Usb[:], pU[0:64, 0:256].rearrange("p (h d) -> p h d", d=64))
                    pS = self.ps('c')
                    for h in range(4):
                        self.mm(pS[0:64, h * 64:(h + 1) * 64], B_[:, j, h, :], Usb[:, h, :], start=True, stop=False)
                        self.mm(pS[0:64, h * 64:(h + 1) * 64], K_[:, j, h, :], V_[:, j, h, :], start=False, stop=True)
                    pY = self.ps('c')
                    for h in range(4):
                        self.mm(pY[0:64, h * 64:(h + 1) * 64], rt[:, h, cs], STb[:, h, :], start=True, stop=False)
                        self.mm(pY[0:64, h * 64:(h + 1) * 64], CM['Rb'][:, j, h, :], Usb[:, h, :], start=False, stop=False)
                        self.mm(pY[0:64, h * 64:(h + 1) * 64], CM['Rk'][:, j, h, :], V_[:, j, h, :], start=False, stop=True)
                    STn = self.rot('rw_ST', [64, 4, 64], F32, 3)
                    self.tt('dve', STn[:], pS[0:64, 0:256].rearrange("p (h d) -> p h d", d=64), ST[:], ALU.add)
                    self.tt('dve', STn[:], STn[:], bcast(ep[:, :, j * 64 + 63:j * 64 + 64], 2, 64), ALU.mult)
                    ST = STn
                    STb = self.rot('rw_STb', [64, 4, 64], BF16, 3)
                    self.cp('act', STb[:], ST[:])
                    self.cp('act', Ysb[:, j, :, :], pY[0:64, 0:256].rearrange("p (h d) -> p h d", d=64))
                yv = Ysb[:].rearrange("p j h d -> p (j h) d")
                s1 = self.rot('rw_s1', [64, NB, 1], F32, 1)
                self.em.op('dve', lambda e, s1=s1, yv=yv: e.tensor_reduce(out=s1[:, :, 0], in_=yv, axis=AX.X, op=ALU.add), reads=[kname(Ysb[:])], writes=[kname(s1[:])])
                self.ts('dve', s1[:], s1[:], 1.0 / 64, None, ALU.mult)
                yc = self.rot('rw_yc', [64, NB, 64], F32, 1)
                self.tt('dve', yc[:], yv, bcast(s1[:], 2, 64), ALU.subtract)
                sq2 = self.rot('rw_sq2', [64, NB, 64], F32, 1)
                self.tt('pool', sq2[:], yc[:], yc[:], ALU.mult)
                s2 = self.rot('rw_s2', [64, NB, 1], F32, 1)
                self.em.op('dve', lambda e, s2=s2, sq2=sq2: e.tensor_reduce(out=s2[:, :, 0], in_=sq2[:], axis=AX.X, op=ALU.add), reads=[kname(sq2[:])], writes=[kname(s2[:])])
                self.act(s2[:], s2[:], AF.Sqrt, scale=1.0 / 64, bias=64e-5)
                self.em.op('dve', lambda e, s2=s2: e.reciprocal(out=s2[:], in_=s2[:]), reads=[kname(s2[:])], writes=[kname(s2[:])])
                self.tt('dve', yc[:], yc[:], bcast(s2[:], 2, 64), ALU.mult)
                y4 = yc[:].rearrange("p (j h) d -> p j (h d)", h=4)
                self.tt('pool', y4, y4, bcast(gng[:].unsqueeze(1), 1, NJ), ALU.mult)
                self.tt('pool', y4, y4, bcast(gnb[:].unsqueeze(1), 1, NJ), ALU.add)
                bv = self.rot('rw_bv', [64, NB, 64], F32, 1)
                self.tt('dve', bv[:], Vf_[:].rearrange("p j h d -> p (j h) d"), bcast(bon[:], 2, 64), ALU.mult)
                self.tt('dve', yc[:], yc[:], bv[:], ALU.add)
                self.tt('pool', yc[:], yc[:], TM['G'][:].rearrange("p j h d -> p (j h) d"), ALU.mult)
                self.st(self.s_y[tok0:tok0 + GW, 0:256].rearrange("(j p) n -> p j n", p=64), y4, rk=[kname(yc[:])], wk=[('s_y_a', g)])

            CTX = {}
            cur = self.capture(prepare, 0)
            self.emit_interleaved(cur, [])
            for g in range(NGR):
                nxt = self.capture(prepare, g + 1) if g + 1 < NGR else []
                tl = self.capture(tail, g)
                self.emit_interleaved(nxt, tl, self.rw_bfrac)

    def phase_rwkv_mlstm(self, l):
        with self.scope():
            A = self.capture(self.phase_rwkv, l, False, dict(a=[0, 1, 2], c=[3, 4]))
            B = self.capture(self.phase_mlstm, l, False, dict(misc=[5], tk=[6], acc=[6], qk=[7], P=[7]), 256)
            self.emit_interleaved(A, B, self.rm_bfrac)

    def phase_moba_merge(self, l):
        I = self.I
        with self.scope():
            wgt = self.em.sb([128, 8, 3072], BF16, "wgt_pf")
            wbr = self.em.sb([128, 8, D], BF16, "wbr_pf")
            wout = self.em.sb([128, 8, D], BF16, "wout_pf")
            g_pre = self.colvec(I["ln_mix_pre"][l], 8, "g_pre_pf")

            def loader():
                for cb in range(6):
                    self.load_w(I["w_in"][l][:, GT_OFF + cb * 512:GT_OFF + (cb + 1) * 512], 8, 512, gcol=g_pre,
                                dst=wgt[:, :, cb * 512:(cb + 1) * 512], kchunk=2, nbuf=2, engs=('dve',), tag='pf')
                for (nm, k0, kn) in (("w_br_rwkv", 0, 2), ("w_br_moba", 2, 4), ("w_br_mlstm", 6, 2)):
                    for cb in range(2):
                        self.load_w(I[nm][l][:, cb * 512:(cb + 1) * 512], kn, 512, dst=wbr[:, k0:k0 + kn, cb * 512:(cb + 1) * 512],
                                    kchunk=2, nbuf=2, engs=('dve',), tag='pf')
                for cb in range(2):
                    self.load_w(I["w_out"][l][:, cb * 512:(cb + 1) * 512], 8, 512, dst=wout[:, :, cb * 512:(cb + 1) * 512],
                                kchunk=2, nbuf=2, engs=('dve',), tag='pf')

            self.phase_moba(l, extra=loader)
            self.phase_merge(l, wgt_pf=wgt, wbr_pf=wbr, wout_pf=wout)


from concourse.bass_utils import run_bass_kernel_spmd


def build_program():
    kb = KB()
    for l in range(2):
        kb.phase_in(l)
        kb.phase_rwkv_mlstm(l)
        kb.phase_moba_merge(l)
        kb.phase_ffn(l)
        if l == 1:
            kb.phase_ple(l, kb.out, "out")
        else:
            kb.phase_ple(l, kb.xres, "xres")
    kb.em.build()
    return kb


def kernel(**inputs):
    kb = build_program()
    in_maps = []
    for b in range(8):
        m = {}
        for name, shape in IN_SPECS:
            a = np.asarray(inputs[name], dtype=np.float32)
            if name == "x":
                a = a[b]
            elif name == "p":
                a = a[:, b]
            m[name] = np.ascontiguousarray(a)
        in_maps.append(m)
    res = run_bass_kernel_spmd(kb.nc, in_maps, core_ids=list(range(8)))
    return np.stack([np.asarray(r["out"], dtype=np.float32) for r in res.results], axis=0)
```
